# Optimizing a Trainium2 kernel written in Bass

```python
import math
import jax, jax.numpy as jnp
from jax import lax
import numpy as np

D_MODEL = 1024
BATCH = 8
SEQ = 4096
DEPTH = 2
DEC_BATCH = 32
DEC_SEQ = 1
PAST_LEN = 16384
PAGE_SIZE = 128

N_EVEN = (DEPTH + 1) // 2
N_ODD = DEPTH // 2

N_HEADS = 8
HEAD_DIM = 64
KV_HEADS = 2
GROUP = N_HEADS // KV_HEADS
ATTN_WIDTH = N_HEADS * HEAD_DIM
CMP_LEN = 32
CMP_STRIDE = 16
N_SUB = CMP_LEN // CMP_STRIDE
SEL_BLOCK = 64
SEL_RATIO = SEL_BLOCK // CMP_STRIDE
N_SELECT = 16
WINDOW = 512
Q_BLOCK = 64
N_BRANCH = 3
KV_COLS = 2 * KV_HEADS * HEAD_DIM
SCALE = HEAD_DIM ** -0.5
CONV_CH = D_MODEL // 2
CONV_WIDTH = 31
POOL_WINDOWS = (2, 4, 8, 16)
POOL_GROUP = D_MODEL // len(POOL_WINDOWS)
POOL_BUF = max(POOL_WINDOWS) - 1
D_FF = -(-8 * D_MODEL // (3 * 256)) * 256

IN_A = ATTN_WIDTH + N_BRANCH * KV_COLS + N_HEADS * N_BRANCH + 2 * CONV_CH
MIX_A = ATTN_WIDTH + CONV_CH
EPS = 1e-6
BIG = 1e9

kernel_name = 'nsa_conformer_pool_hybrid_step'


def rmsnorm(x, g):
    x32 = x.astype(jnp.float32)
    y = x32 * lax.rsqrt(jnp.mean(x32 * x32, axis=-1, keepdims=True) + EPS)
    return (y * g.astype(jnp.float32)).astype(x.dtype)


def masked_softmax(s, mask):
    s = jnp.where(mask, s, -jnp.inf)
    m = jnp.max(s, axis=-1, keepdims=True)
    m = jnp.where(jnp.isfinite(m), m, 0.0)
    e = jnp.where(mask, jnp.exp(s - m), 0.0)
    return e / jnp.maximum(jnp.sum(e, axis=-1, keepdims=True), jnp.finfo(jnp.float32).tiny)


def swiglu(h, w_gate, w_up, w_down):
    return (jax.nn.silu(h @ w_gate) * (h @ w_up)) @ w_down


def split_even_proj(h, w_in):
    b, t = h.shape[0], h.shape[1]
    z = h @ w_in
    q = z[..., :ATTN_WIDTH].reshape(b, t, KV_HEADS, GROUP, HEAD_DIM)
    off = ATTN_WIDTH
    kv = []
    for _ in range(N_BRANCH):
        kv.append(z[..., off:off + KV_COLS].reshape(b, t, 2, KV_HEADS, HEAD_DIM))
        off += KV_COLS
    gates = z[..., off:off + N_HEADS * N_BRANCH].reshape(b, t, KV_HEADS, GROUP, N_BRANCH)
    off += N_HEADS * N_BRANCH
    u = z[..., off:]
    return q, kv[0], kv[1], kv[2], gates, u


def compress_kv(k, pos, w1, w2):
    b, L = k.shape[0], k.shape[1]
    n_ch = L // CMP_STRIDE
    nb = n_ch - N_SUB + 1
    ch = k[:, :n_ch * CMP_STRIDE].reshape(b, n_ch, CMP_STRIDE, KV_HEADS, HEAD_DIM)
    w1s = w1.reshape(N_SUB, CMP_STRIDE, HEAD_DIM, HEAD_DIM)
    hid = jnp.einsum('ld,lde->e', pos, w1)
    for c in range(N_SUB):
        hid = hid + jnp.einsum('bnlgd,lde->bnge', ch[:, c:c + nb], w1s[c])
    return jnp.einsum('bnge,ef->bngf', jax.nn.gelu(hid), w2)


def compressed_attention(q, t_pos, kcmp, vcmp):
    nb = kcmp.shape[1]
    s = jnp.einsum('bqghd,bngd->bghqn', q, kcmp).astype(jnp.float32) * SCALE
    end = jnp.arange(nb) * CMP_STRIDE + (CMP_LEN - 1)
    p = masked_softmax(s, end[None, :] <= t_pos[:, None])
    o = jnp.einsum('bghqn,bngd->bqghd', p.astype(vcmp.dtype), vcmp)
    return o, p


def select_blocks(p_cmp, t_pos, ns):
    imp = jnp.sum(p_cmp, axis=2)
    nb = imp.shape[-1]
    P = jnp.pad(imp, ((0, 0), (0, 0), (0, 0), (N_SUB - 1, SEL_RATIO * ns - nb)))
    score = 0.0
    for m in range(SEL_RATIO):
        for n in range(N_SUB):
            st = N_SUB - 1 + m - n
            score = score + P[..., st:st + SEL_RATIO * ns:SEL_RATIO]
    blk = jnp.arange(ns)[None, :]
    cur = (t_pos // SEL_BLOCK)[:, None]
    valid = blk * SEL_BLOCK <= t_pos[:, None]
    forced = valid & ((blk == 0) | (blk == cur) | (blk == cur - 1))
    score = jnp.where(forced, BIG, jnp.where(valid, score, -BIG))
    _, idx = lax.top_k(score, min(N_SELECT, ns))
    return idx


def selected_attention(q, t_pos, idx, sel_kv):
    key_pos = idx[..., None] * SEL_BLOCK + jnp.arange(SEL_BLOCK)
    b, g, nq, k, sb = key_pos.shape
    mask = (key_pos <= t_pos[None, None, :, None, None]).reshape(b, g, 1, nq, k * sb)
    s = jnp.einsum('bqghd,bgqkld->bghqkl', q, sel_kv[..., 0, :]).astype(jnp.float32) * SCALE
    p = masked_softmax(s.reshape(b, g, GROUP, nq, k * sb), mask)
    p = p.reshape(b, g, GROUP, nq, k, sb).astype(sel_kv.dtype)
    return jnp.einsum('bghqkl,bgqkld->bqghd', p, sel_kv[..., 1, :])


def window_attention(q, t_pos, slab, key_pos):
    s = jnp.einsum('bqghd,bkgd->bghqk', q, slab[:, :, 0]).astype(jnp.float32) * SCALE
    diff = t_pos[:, None] - key_pos[None, :]
    mask = (diff >= 0) & (diff < WINDOW) & (key_pos[None, :] >= 0)
    p = masked_softmax(s, mask)
    return jnp.einsum('bghqk,bkgd->bqghd', p.astype(slab.dtype), slab[:, :, 1])


def gate_mix(gates, o_c, o_s, o_w):
    g = jax.nn.sigmoid(gates.astype(jnp.float32)).astype(o_c.dtype)
    return g[..., 0:1] * o_c + g[..., 1:2] * o_s + g[..., 2:3] * o_w


def nsa_prompt(q, kv_c, kv_s, kv_w, gates, ck):
    b, t = q.shape[0], q.shape[1]
    kcmp = compress_kv(kv_c[:, :, 0], ck[0], ck[1], ck[2])
    vcmp = compress_kv(kv_c[:, :, 1], ck[3], ck[4], ck[5])
    ns = -(-t // SEL_BLOCK)
    kvs = jnp.pad(kv_s, ((0, 0), (0, ns * SEL_BLOCK - t), (0, 0), (0, 0), (0, 0)))
    kvs = kvs.reshape(b, ns, SEL_BLOCK, 2, KV_HEADS, HEAD_DIM).transpose(0, 4, 1, 2, 3, 5)
    kvw = jnp.pad(kv_w, ((0, 0), (WINDOW, 0), (0, 0), (0, 0), (0, 0)))
    bi = jnp.arange(b)[:, None, None, None]
    gi = jnp.arange(KV_HEADS)[None, :, None, None]

    def block(i):
        q0 = i * Q_BLOCK
        t_pos = q0 + jnp.arange(Q_BLOCK)
        qb = lax.dynamic_slice_in_dim(q, q0, Q_BLOCK, axis=1)
        gb = lax.dynamic_slice_in_dim(gates, q0, Q_BLOCK, axis=1)
        o_c, p_c = compressed_attention(qb, t_pos, kcmp, vcmp)
        idx = select_blocks(p_c, t_pos, ns)
        o_s = selected_attention(qb, t_pos, idx, kvs[bi, gi, idx])
        slab = lax.dynamic_slice_in_dim(kvw, q0, WINDOW + Q_BLOCK, axis=1)
        o_w = window_attention(qb, t_pos, slab, q0 - WINDOW + jnp.arange(WINDOW + Q_BLOCK))
        return gate_mix(gb, o_c, o_s, o_w)

    out = lax.map(block, jnp.arange(t // Q_BLOCK))
    return jnp.moveaxis(out, 0, 1).reshape(b, t, ATTN_WIDTH)


def nsa_sample(q, kv_c, kv_s, kv_w, gates, ck, li, pool_c, pool_s, win_buf, page_table):
    b, s_new = q.shape[0], q.shape[1]
    n_pages = page_table.shape[1]
    past = n_pages * PAGE_SIZE
    past_c = pool_c[li, page_table].reshape(b, past, 2, KV_HEADS, HEAD_DIM)
    full_c = jnp.concatenate([past_c.astype(kv_c.dtype), kv_c], axis=1)
    kcmp = compress_kv(full_c[:, :, 0], ck[0], ck[1], ck[2])
    vcmp = compress_kv(full_c[:, :, 1], ck[3], ck[4], ck[5])
    t_pos = past + jnp.arange(s_new)
    o_c, p_c = compressed_attention(q, t_pos, kcmp, vcmp)
    ns = -(-(past + s_new) // SEL_BLOCK)
    idx = select_blocks(p_c, t_pos, ns)
    key_pos = idx[..., None] * SEL_BLOCK + jnp.arange(SEL_BLOCK)
    bi = jnp.arange(b)[:, None, None, None, None]
    gi = jnp.arange(KV_HEADS)[None, :, None, None, None]
    phys = page_table[bi, jnp.clip(key_pos // PAGE_SIZE, 0, n_pages - 1)]
    from_pool = pool_s[li, phys, key_pos % PAGE_SIZE, :, gi]
    from_new = kv_s[bi, jnp.clip(key_pos - past, 0, s_new - 1), :, gi]
    sel = jnp.where((key_pos < past)[..., None, None], from_pool.astype(kv_s.dtype), from_new)
    o_s = selected_attention(q, t_pos, idx, sel)
    w_buf = win_buf.shape[1]
    slab = jnp.concatenate([win_buf.astype(kv_w.dtype), kv_w], axis=1)
    o_w = window_attention(q, t_pos, slab, past - w_buf + jnp.arange(w_buf + s_new))
    out = gate_mix(gates, o_c, o_s, o_w).reshape(b, s_new, ATTN_WIDTH)
    return out, slab[:, -w_buf:]


def conv_module(u, prev, conv_w, conv_b, ln_g, ln_b):
    a = u[..., :CONV_CH] * jax.nn.sigmoid(u[..., CONV_CH:])
    ext = jnp.concatenate([prev.astype(a.dtype), a], axis=1)
    y = lax.conv_general_dilated(ext, conv_w[:, None, :].astype(a.dtype), window_strides=(1,), padding='VALID',
                                 dimension_numbers=('NWC', 'WIO', 'NWC'), feature_group_count=CONV_CH) + conv_b
    y32 = y.astype(jnp.float32)
    mu = jnp.mean(y32, axis=-1, keepdims=True)
    var = jnp.mean(jnp.square(y32 - mu), axis=-1, keepdims=True)
    yn = (y32 - mu) * lax.rsqrt(var + EPS) * ln_g + ln_b
    return jax.nn.silu(yn).astype(a.dtype), ext[:, -(CONV_WIDTH - 1):]


def pool_module(h, prev, first_pos, pool_w, pool_scale):
    n = h.shape[1]
    ext = jnp.concatenate([prev.astype(h.dtype), h], axis=1)
    cs = jnp.pad(jnp.cumsum(ext.astype(jnp.float32), axis=1), ((0, 0), (1, 0), (0, 0)))
    pos = first_pos + jnp.arange(n)
    hf = h.astype(jnp.float32)
    outs = []
    for gidx, w in enumerate(POOL_WINDOWS):
        c0, c1 = gidx * POOL_GROUP, (gidx + 1) * POOL_GROUP
        win_sum = cs[:, POOL_BUF + 1:, c0:c1] - cs[:, POOL_BUF + 1 - w:POOL_BUF + 1 - w + n, c0:c1]
        cnt = jnp.minimum(w, pos + 1).astype(jnp.float32)
        outs.append(win_sum / cnt[None, :, None] - hf[..., c0:c1])
    z = jnp.stack(outs, axis=2).astype(h.dtype)
    y = jnp.einsum('bngc,gce->bnge', z, pool_w).reshape(h.shape) * pool_scale
    return y, ext[:, -POOL_BUF:]


def setup_inputs(seed: int = 0) -> dict:
    key = jax.random.key(seed)
    ks = jax.random.split(key, 32)
    n_pages = PAST_LEN // PAGE_SIZE
    n_used = DEC_BATCH * n_pages
    n_pool = n_used + n_used // 4
    w_buf = min(WINDOW, PAST_LEN)
    f32 = jnp.float32

    def nrm(k, shape, scale=1.0):
        return jax.random.normal(k, shape, f32) * scale

    page_table = jax.random.permutation(ks[7], n_pool)[:n_used].reshape(DEC_BATCH, n_pages).astype(jnp.int32)
    return {
        'x_prompt': nrm(ks[0], (BATCH, SEQ, D_MODEL)),
        'x_sample': nrm(ks[1], (DEC_BATCH, DEC_SEQ, D_MODEL)),
        'cache_cmp_kv': nrm(ks[2], (N_EVEN, n_pool, PAGE_SIZE, 2, KV_HEADS, HEAD_DIM)),
        'cache_slc_kv': nrm(ks[3], (N_EVEN, n_pool, PAGE_SIZE, 2, KV_HEADS, HEAD_DIM)),
        'state_win_kv': nrm(ks[4], (N_EVEN, DEC_BATCH, w_buf, 2, KV_HEADS, HEAD_DIM)),
        'state_conv': nrm(ks[5], (N_EVEN, DEC_BATCH, CONV_WIDTH - 1, CONV_CH), 0.5),
        'state_pool': nrm(ks[6], (N_ODD, DEC_BATCH, POOL_BUF, D_MODEL)),
        'page_table': page_table,
        'norm_mix': 1.0 + nrm(ks[8], (DEPTH, D_MODEL), 0.05),
        'norm_ffn': 1.0 + nrm(ks[9], (DEPTH, D_MODEL), 0.05),
        'norm_final': 1.0 + nrm(ks[10], (D_MODEL,), 0.05),
        'w_in_a': nrm(ks[11], (N_EVEN, D_MODEL, IN_A), D_MODEL ** -0.5),
        'w_out_a': nrm(ks[12], (N_EVEN, MIX_A, D_MODEL), MIX_A ** -0.5),
        'cmp_pos_k': nrm(ks[13], (N_EVEN, CMP_LEN, HEAD_DIM), 0.5),
        'cmp_w1_k': nrm(ks[14], (N_EVEN, CMP_LEN, HEAD_DIM, HEAD_DIM), (CMP_LEN * HEAD_DIM) ** -0.5),
        'cmp_w2_k': nrm(ks[15], (N_EVEN, HEAD_DIM, HEAD_DIM), HEAD_DIM ** -0.5),
        'cmp_pos_v': nrm(ks[16], (N_EVEN, CMP_LEN, HEAD_DIM), 0.5),
        'cmp_w1_v': nrm(ks[17], (N_EVEN, CMP_LEN, HEAD_DIM, HEAD_DIM), (CMP_LEN * HEAD_DIM) ** -0.5),
        'cmp_w2_v': nrm(ks[18], (N_EVEN, HEAD_DIM, HEAD_DIM), HEAD_DIM ** -0.5),
        'conv_w': nrm(ks[19], (N_EVEN, CONV_WIDTH, CONV_CH), CONV_WIDTH ** -0.5),
        'conv_b': nrm(ks[20], (N_EVEN, CONV_CH), 0.02),
        'conv_ln_g': 1.0 + nrm(ks[21], (N_EVEN, CONV_CH), 0.05),
        'conv_ln_b': nrm(ks[22], (N_EVEN, CONV_CH), 0.02),
        'pool_w': nrm(ks[23], (N_ODD, len(POOL_WINDOWS), POOL_GROUP, POOL_GROUP), POOL_GROUP ** -0.5),
        'pool_scale': 0.5 + nrm(ks[24], (N_ODD, D_MODEL), 0.1),
        'w_ffn_gate': nrm(ks[25], (DEPTH, D_MODEL, D_FF), D_MODEL ** -0.5),
        'w_ffn_up': nrm(ks[26], (DEPTH, D_MODEL, D_FF), D_MODEL ** -0.5),
        'w_ffn_down': nrm(ks[27], (DEPTH, D_FF, D_MODEL), D_FF ** -0.5),
    }


def reference(x_prompt, x_sample, cache_cmp_kv, cache_slc_kv, state_win_kv, state_conv, state_pool, page_table,
              norm_mix, norm_ffn, norm_final, w_in_a, w_out_a, cmp_pos_k, cmp_w1_k, cmp_w2_k, cmp_pos_v, cmp_w1_v,
              cmp_w2_v, conv_w, conv_b, conv_ln_g, conv_ln_b, pool_w, pool_scale, w_ffn_gate, w_ffn_up, w_ffn_down):
    xp, xs = x_prompt, x_sample
    bp, tp = xp.shape[0], xp.shape[1]
    past = page_table.shape[1] * PAGE_SIZE
    cmp_p, cmp_s, slc_p, slc_s, win_p, win_s, conv_p, conv_s, pool_p, pool_s = [], [], [], [], [], [], [], [], [], []
    for l in range(DEPTH):
        hp = rmsnorm(xp, norm_mix[l])
        hs = rmsnorm(xs, norm_mix[l])
        if l % 2 == 0:
            i = l // 2
            ck = (cmp_pos_k[i], cmp_w1_k[i], cmp_w2_k[i], cmp_pos_v[i], cmp_w1_v[i], cmp_w2_v[i])
            cw = (conv_w[i], conv_b[i], conv_ln_g[i], conv_ln_b[i])
            q, kvc, kvs, kvw, g, u = split_even_proj(hp, w_in_a[i])
            a_out = nsa_prompt(q, kvc, kvs, kvw, g, ck)
            c_out, c_st = conv_module(u, jnp.zeros((bp, CONV_WIDTH - 1, CONV_CH), hp.dtype), *cw)
            mp = jnp.concatenate([a_out, c_out], axis=-1) @ w_out_a[i]
            cmp_p.append(kvc)
            slc_p.append(kvs)
            win_p.append(kvw[:, -min(WINDOW, tp):])
            conv_p.append(c_st)
            q, kvc, kvs, kvw, g, u = split_even_proj(hs, w_in_a[i])
            a_out, w_st = nsa_sample(q, kvc, kvs, kvw, g, ck, i, cache_cmp_kv, cache_slc_kv, state_win_kv[i], page_table)
            c_out, c_st = conv_module(u, state_conv[i], *cw)
            ms = jnp.concatenate([a_out, c_out], axis=-1) @ w_out_a[i]
            cmp_s.append(kvc)
            slc_s.append(kvs)
            win_s.append(w_st)
            conv_s.append(c_st)
        else:
            j = l // 2
            mp, p_st = pool_module(hp, jnp.zeros((bp, POOL_BUF, D_MODEL), hp.dtype), 0, pool_w[j], pool_scale[j])
            ms, s_st = pool_module(hs, state_pool[j], past, pool_w[j], pool_scale[j])
            pool_p.append(p_st)
            pool_s.append(s_st)
        xp = xp + mp
        xs = xs + ms
        xp = xp + swiglu(rmsnorm(xp, norm_ffn[l]), w_ffn_gate[l], w_ffn_up[l], w_ffn_down[l])
        xs = xs + swiglu(rmsnorm(xs, norm_ffn[l]), w_ffn_gate[l], w_ffn_up[l], w_ffn_down[l])
    y_prompt = rmsnorm(xp, norm_final)
    y_sample = rmsnorm(xs, norm_final)
    return (y_prompt, y_sample, jnp.stack(cmp_p), jnp.stack(cmp_s), jnp.stack(slc_p), jnp.stack(slc_s),
            jnp.stack(win_p), jnp.stack(win_s), jnp.stack(conv_p), jnp.stack(conv_s),
            jnp.stack(pool_p), jnp.stack(pool_s))
```

```python
import numpy as np
import concourse.bass as bass
import concourse.mybir as mybir
from concourse.bass_utils import run_bass_kernel_spmd
from contextlib import ExitStack

F32 = mybir.dt.float32
BF16 = mybir.dt.bfloat16
I32 = mybir.dt.int32
U32 = mybir.dt.uint32
AF = mybir.ActivationFunctionType
ALU = mybir.AluOpType
AX = mybir.AxisListType

SCALE = 0.125
NEG = -30000.0
EPS = 1e-6
T = 4096
D = 1024
NBLK = 8
TB = 512
DFF = 2816
NF = 22
INA = 2328
NS = 4


class Buf:
    __slots__ = ("name", "w", "r", "base")

    def __init__(self, name):
        self.name = name
        self.w = {}
        self.r = {}
        self.base = {}


class Eng:
    def __init__(self, name):
        self.name = name
        self.cnt = 0
        self.known = {}


class MK:
    NLANES = 20

    def __init__(self, nc, es):
        self.nc = nc
        self.es = es
        self.eng = {n: Eng(n) for n in ("pe", "act", "dve", "pool", "sp")}
        self.h = {"pe": nc.tensor, "act": nc.scalar, "dve": nc.vector, "pool": nc.gpsimd, "sp": nc.sync}
        self.sems = {}
        self.nbuf = 0
        self.out_events = []
        self.lanes = {q: {"next": 0, "cnt": [0] * self.NLANES} for q in ("sp", "pool")}
        self.nops = 0
        self.uid = 0
        self.stopped = False

    def sb(self, name, shape, dt, es=None):
        self.uid += 1
        return (es or self.es).enter_context(self.nc.sbuf_tensor(f"{name}_{self.uid}", list(shape), dt))

    def ps(self, name, shape, dt, es=None):
        self.uid += 1
        return (es or self.es).enter_context(self.nc.psum_tensor(f"{name}_{self.uid}", list(shape), dt))

    def buf(self, name=None):
        self.nbuf += 1
        return Buf(name or f"b{self.nbuf}")

    def _sem(self, key):
        if key not in self.sems:
            self.sems[key] = self.es.enter_context(self.nc.semaphore(key))
        return self.sems[key]

    def _collect(self, e, reads, writes, add=False):
        waits = {}

        def need(ev):
            k, v = ev
            if e.name == "pe" and k == "e_pe":
                return
            if e.known.get(k, 0) >= v:
                return
            if waits.get(k, 0) < v:
                waits[k] = v

        for b in reads:
            for ev in b.w.items():
                need(ev)
        for b in writes:
            for ev in (b.base if add else b.w).items():
                need(ev)
            for ev in b.r.items():
                need(ev)
        return waits

    def _emit_waits(self, e, waits):
        h = self.h[e.name]
        for k, v in waits.items():
            e.known[k] = v
            h.wait_ge(self._sem(k), v)

    def _record(self, ev, reads, writes, add):
        k, v = ev
        for b in reads:
            if b.r.get(k, 0) < v:
                b.r[k] = v
        for b in writes:
            if add:
                if b.w.get(k, 0) < v:
                    b.w[k] = v
            else:
                b.w = {k: v}
                b.base = {k: v}
                b.r = {}

    def op(self, engname, fn, reads=(), writes=(), sig=True, add=False):
        if self.stopped:
            return None
        e = self.eng[engname]
        waits = self._collect(e, reads, writes, add)
        self._emit_waits(e, waits)
        key = "e_" + engname
        ins = fn(self.h[engname])
        if sig:
            e.cnt += 1
            ev = (key, e.cnt)
            ins.then_inc(self._sem(key), 1)
        else:
            ev = (key, e.cnt + 1)
        self._record(ev, reads, writes, add)
        self.nops += 1
        return ev

    def dma(self, engname, fn, reads=(), writes=(), is_out=False, add=False):
        if self.stopped:
            return None
        e = self.eng[engname]
        ln = self.lanes[engname]
        li = ln["next"]
        ln["next"] = (li + 1) % self.NLANES
        key = f"l_{engname}{li}"
        waits = self._collect(e, reads, writes, add)
        if ln["cnt"][li] > 0 and e.known.get(key, 0) < ln["cnt"][li]:
            waits[key] = max(waits.get(key, 0), ln["cnt"][li])
        self._emit_waits(e, waits)
        ins = fn(self.h[engname])
        ln["cnt"][li] += 16
        ev = (key, ln["cnt"][li])
        ins.then_inc(self._sem(key), 16)
        self._record(ev, reads, writes, add)
        if is_out:
            self.out_events.append(ev)
        self.nops += 1
        return ev

    def barrier(self):
        if self.stopped:
            return
        targets = {}
        for n, e in self.eng.items():
            if e.cnt > 0:
                targets["e_" + n] = e.cnt
        for q, ln in self.lanes.items():
            for i, c in enumerate(ln["cnt"]):
                if c > 0:
                    targets[f"l_{q}{i}"] = c
        for n, e in self.eng.items():
            waits = {}
            for k, v in targets.items():
                if k == "e_pe" and n == "pe":
                    continue
                if e.known.get(k, 0) < v:
                    waits[k] = v
            self._emit_waits(e, waits)

    def finish(self):
        e = self.eng["sp"]
        final = {}
        for q, ln in self.lanes.items():
            for i, c in enumerate(ln["cnt"]):
                k = f"l_{q}{i}"
                if c > 0 and e.known.get(k, 0) < c:
                    final[k] = c
        for n, en in self.eng.items():
            if n != "sp" and en.cnt > 0:
                final["e_" + n] = en.cnt
        self._emit_waits(e, final)


class Rot:
    def __init__(self, items):
        self.items = items
        self.i = 0

    def next(self):
        it = self.items[self.i]
        self.i = (self.i + 1) % len(self.items)
        return it


class _Stop(Exception):
    pass


def build_nc(do_samples=True, stop=None):
    nc = bass.Bass("TRN2", target_bir_lowering=False)
    es = ExitStack()
    with es:
        mk = MK(nc, es)

        hits = {}

        def chk(n):
            if stop is None:
                return
            code, nth = stop if isinstance(stop, tuple) else (stop, 1)
            if n == code:
                hits[n] = hits.get(n, 0) + 1
                if hits[n] == nth:
                    mk.stopped = True
        _build_body(nc, mk, do_samples, chk)
        mk.finish()
        print("nops", mk.nops, {n: e.cnt for n, e in mk.eng.items()})
    return nc


def _build_body(nc, mk, do_samples, chk):
    if True:

        def din(name, shape, dt=F32):
            return nc.dram_tensor(name, list(shape), dt, kind="ExternalInput").ap()

        def dout(name, shape, dt=F32):
            return nc.dram_tensor(name, list(shape), dt, kind="ExternalOutput").ap()

        def dscr(name, shape, dt=F32):
            return nc.dram_tensor(name, list(shape), dt, kind="Internal").ap()

        xp = din("xp", [T, D])
        w_in = din("w_in", [D, INA])
        w_out = din("w_out", [D, D])
        w1k = din("w1k", [32, 64, 64]); w2k = din("w2k", [64, 64]); posk = din("posk", [32, 64])
        w1v = din("w1v", [32, 64, 64]); w2v = din("w2v", [64, 64]); posv = din("posv", [32, 64])
        conv_w = din("conv_w", [31, 512]); conv_b = din("conv_b", [512])
        ln_g = din("ln_g", [512]); ln_b = din("ln_b", [512])
        pool_w = din("pool_w", [4, 256, 256]); pool_scale = din("pool_scale", [1, D])
        norms = din("norms", [5, D])
        wg_d = din("wg", [2, D, DFF]); wu_d = din("wu", [2, D, DFF]); wd_d = din("wd", [2, DFF, D])
        c_ident = din("c_ident", [128, 128])
        c_eall = din("c_eall", [64, 32, 128])
        c_band = din("c_band", [20, 128, 128])

        y_p = dout("y_p", [T, D])
        o_cmp_p = dout("o_cmp_p", [T, 256]); o_slc_p = dout("o_slc_p", [T, 256]); o_win_p = dout("o_win_p", [512, 256])
        o_conv_p = dout("o_conv_p", [30, 512]); o_pool_p = dout("o_pool_p", [15, D])
        xs1 = dscr("xs1", [T, D]); xs3 = dscr("xs3", [T, D])
        bxs1 = [mk.buf(f"xs1_{i}") for i in range(NBLK)]
        bxs3 = [mk.buf(f"xs3_{i}") for i in range(NBLK)]

        if do_samples:
            xs_d = din("xs", [NS, D])
            pt_d = din("pt", [NS, 128], I32)
            cmp_rows = din("cache_cmp", [5120 * 8, 16 * 256])
            slc_hp = din("cache_slc", [5120 * 2, 64 * 256])
            swin_d = din("state_win", [NS, 512, 256])
            sconv_d = din("state_conv", [NS, 30, 512])
            spool_d = din("state_pool", [NS, 15, D])
            c_samp = din("c_samp", [128, 1024])
            y_s = dout("y_s", [NS, D])
            o_cmp_s = dout("o_cmp_s", [NS, 256]); o_slc_s = dout("o_slc_s", [NS, 256]); o_win_s = dout("o_win_s", [NS, 512, 256])
            o_conv_s = dout("o_conv_s", [NS, 30, 512]); o_pool_s = dout("o_pool_s", [NS, 15, D])
            hp_scr = dscr("hp_scr", [8, 16], I32); b_hp_scr = mk.buf("hp_scr")
            xs_scr = dscr("xs_scr", [NS, D]); b_xs_scr = mk.buf("xs_scr")
            C_OH = 0
            C_SEL2 = 512
            C_GS = 640
            C_GSEL = 648
            C_SELGH = 656
            C_IND30 = 688
            C_IOTA = 692
            C_CADD = 820
            C_DM = 828

        ident = mk.sb("ident", [128, 128], BF16); b_ident = mk.buf("ident")
        identf = mk.sb("identf", [128, 128], F32); b_identf = mk.buf("identf")
        onesf = mk.sb("onesf", [128, 128], F32); b_onesf = mk.buf("onesf")
        mk.dma("sp", lambda h: h.dma_start(out=identf[:], in_=c_ident), writes=[b_identf])
        mk.dma("pool", lambda h: h.dma_start(out=ident[:], in_=c_ident), writes=[b_ident])
        mk.op("dve", lambda h: h.memset(onesf[:], 1.0), writes=[b_onesf])

        def load_gain(gt, b_gt, slot, i):
            mk.dma("sp", lambda h: h.dma_start(out=gt[:, slot, :], in_=norms[i:i + 1, :].partition_broadcast(128)), writes=[b_gt], add=True)

        P = {"pd": [], "banks": [], "i": 0, "ng": 0}

        def setup_psum(scope, ng):
            P["pd"] = [mk.ps(f"pd{i}", [128, 1024], F32, scope) for i in range(ng // 2)]
            P["banks"] = []
            for i in range(ng // 2):
                P["banks"].append((P["pd"][i][:, 0:512], mk.buf(f"bank{2 * i}")))
                P["banks"].append((P["pd"][i][:, 512:1024], mk.buf(f"bank{2 * i + 1}")))
            P["i"] = 0
            P["ng"] = ng
            return Rot([(mk.ps(f"ptb{i}", [128, 1024], BF16, scope), mk.buf(f"ptb{i}")) for i in range(1)])

        def getbank():
            b = P["banks"][P["i"]]
            P["i"] = (P["i"] + 1) % P["ng"]
            return b

        def getdbl():
            if P["i"] % 2:
                P["i"] = (P["i"] + 1) % P["ng"]
            j = P["i"] // 2
            P["i"] = (P["i"] + 2) % P["ng"]
            return P["pd"][j], [P["banks"][2 * j][1], P["banks"][2 * j + 1][1]]

        PTR = {}

        evac_tog = {"i": 0}

        def evac(out, in_, reads, writes, eng=None, add=False):
            if eng is None:
                eng = "act" if evac_tog["i"] % 2 == 0 else "dve"
                evac_tog["i"] += 1
            if eng == "act":
                return mk.op("act", lambda h: h.copy(out=out, in_=in_), reads=reads, writes=writes, add=add)
            return mk.op("dve", lambda h: h.tensor_copy(out=out, in_=in_), reads=reads, writes=writes, add=add)

        def rmsnorm_tile(x_ap, npart, g_ap, reads, st, out_bf=None, b_bf=None, out_f32=None, b_f32=None):
            st_t, st_b = st
            jt, jb = (out_f32, b_f32) if out_f32 is not None else (out_bf, b_bf)
            mk.op("act", lambda h: h.activation(out=jt, in_=x_ap, func=AF.Square, accum_out=st_t[0:npart, 0:1]), reads=reads, writes=[jb, st_b])
            mk.op("act", lambda h: h.activation(out=st_t[0:npart, 1:2], in_=st_t[0:npart, 0:1], func=AF.Sqrt, scale=1.0 / D, bias=EPS),
                  reads=[st_b], writes=[st_b])
            mk.op("dve", lambda h: h.reciprocal(out=st_t[0:npart, 2:3], in_=st_t[0:npart, 1:2]), reads=[st_b], writes=[st_b])
            mk.op("dve", lambda h: h.scalar_tensor_tensor(out=jt, in0=x_ap, scalar=st_t[0:npart, 2:3], in1=g_ap, op0=ALU.mult, op1=ALU.mult),
                  reads=list(reads) + [st_b], writes=[jb])
            if out_f32 is not None and out_bf is not None:
                mk.op("act", lambda h: h.copy(out=out_bf, in_=out_f32), reads=[b_f32], writes=[b_bf])

        def transpose_to(hb_ap, hb_buf, dst_ap, dst_buf, nchunk, npart=128):
            pt, bpt = PTR["r"].next()
            for kc in range(nchunk):
                mk.op("pe", lambda h, kc=kc: h.transpose(out=pt[:, kc * 128:kc * 128 + npart], in_=hb_ap[:, kc * 128:(kc + 1) * 128],
                                                         identity=ident[0:npart, 0:npart]),
                      reads=[hb_buf, b_ident], writes=[bpt], sig=(kc == nchunk - 1), add=(kc > 0))
            src = pt[:, 0:nchunk * 128].rearrange("p (c t) -> p c t", t=128)[:, :, 0:npart]
            evac(dst_ap, src, [bpt], [dst_buf], add=True)

        esA = ExitStack()
        with esA:
            PTR["r"] = setup_psum(esA, 2)
            po_r = Rot([(mk.ps(f"pacc{i}", [128, 512], F32, esA), mk.buf(f"pacc{i}")) for i in range(2)])
            pq_r = Rot([(mk.ps(f"pq{i}", [128, 512], F32, esA), mk.buf(f"pq{i}")) for i in range(1)])
            S_r = Rot([(mk.ps(f"psS{i}", [128, 512], F32, esA)[:, :], mk.buf(f"psS{i}")) for i in range(2)])
            gbA = mk.sb("gbA", [128, 1, D], F32, esA); b_gbA = mk.buf("gbA")
            load_gain(gbA, b_gbA, 0, 0)
            Win = mk.sb("Win", [128, 8, INA], BF16, esA); b_Win = mk.buf("Win")
            Wout = mk.sb("Wout", [128, 8, D], BF16, esA); b_Wout = mk.buf("Wout")
            for kc in range(8):
                mk.dma("pool", lambda h, kc=kc: h.dma_start(out=Win[:, kc, :], in_=w_in[kc * 128:(kc + 1) * 128, :]), writes=[b_Win], add=True)
            for kc in range(8):
                mk.dma("pool", lambda h, kc=kc: h.dma_start(out=Wout[:, kc, :], in_=w_out[kc * 128:(kc + 1) * 128, :]), writes=[b_Wout], add=True)
            W1 = {}; W2 = {}; post = {}
            b_W1 = mk.buf("W1"); b_W2 = mk.buf("W2"); b_post = mk.buf("post")
            posT = mk.sb("posT", [128, 2, 32], BF16, esA); b_posT = mk.buf("posT")
            postt = mk.sb("postt", [128, 2], F32, esA)
            for j, (w1d, w2d, posd) in enumerate(((w1k, w2k, posk), (w1v, w2v, posv))):
                W1[j] = mk.sb(f"W1_{j}", [128, 32, 128], BF16, esA)
                W2[j] = mk.sb(f"W2_{j}", [128, 128], BF16, esA)
                mk.op("pool", lambda h, j=j: h.memset(W1[j][:], 0.0), writes=[b_W1], add=(j > 0))
                mk.op("pool", lambda h, j=j: h.memset(W2[j][:], 0.0), writes=[b_W2], add=(j > 0))
            with nc.allow_non_contiguous_dma(reason="small weight relayout"):
                for j, (w1d, w2d, posd) in enumerate(((w1k, w2k, posk), (w1v, w2v, posv))):
                    for g in range(2):
                        mk.dma("pool", lambda h, j=j, g=g, w1d=w1d: h.dma_start(out=W1[j][g * 64:(g + 1) * 64, :, g * 64:(g + 1) * 64],
                                                                                  in_=w1d.rearrange("l d e -> d l e")), writes=[b_W1], add=(j + g > 0))
                        mk.dma("pool", lambda h, j=j, g=g, w2d=w2d: h.dma_start(out=W2[j][g * 64:(g + 1) * 64, g * 64:(g + 1) * 64], in_=w2d),
                               writes=[b_W2], add=(j + g > 0))
                        mk.dma("pool", lambda h, j=j, g=g, posd=posd: h.dma_start(out=posT[g * 64:(g + 1) * 64, j, :], in_=posd.rearrange("l d -> d l")),
                               writes=[b_posT], add=(j + g > 0))
                cw = mk.sb("cw", [128, 4, 31], F32, esA); b_cw = mk.buf("cw")
                cvec = mk.sb("cvec", [128, 3, 4], F32, esA); b_cvec = mk.buf("cvec")
                for c in range(4):
                    mk.dma("sp", lambda h, c=c: h.dma_start(out=cw[:, c, :], in_=conv_w[:, c * 128:(c + 1) * 128].rearrange("k p -> p k")), writes=[b_cw], add=(c > 0))
                for i, v in enumerate((conv_b, ln_g, ln_b)):
                    mk.dma("sp", lambda h, i=i, v=v: h.dma_start(out=cvec[:, i, :], in_=v.rearrange("(c p) -> p c", p=128)), writes=[b_cvec], add=(i > 0))
            for j in range(2):
                pb, bpb = getbank()
                for l in range(32):
                    mk.op("pe", lambda h, j=j, l=l: h.matmul(pb[:, 0:1], lhsT=W1[j][:, l, :], rhs=posT[:, j, l:l + 1], start=(l == 0), stop=(l == 31)),
                          reads=[b_W1, b_posT], writes=[bpb], sig=(l == 31), add=(l > 0))
                evac(postt[:, j:j + 1], pb[:, 0:1], [bpb], [b_post], add=(j > 0))

            chk(1)
            if do_samples:
                esS = ExitStack()
                with esS:
                    csamp = mk.sb("csamp", [128, 1024], F32, esS); b_csamp = mk.buf("csamp")
                    mk.dma("sp", lambda h: h.dma_start(out=csamp[:], in_=c_samp), writes=[b_csamp])
                    sst_ = mk.sb("s_st", [128, 4], F32, esS); b_sst_ = mk.buf("s_st")
                    xs_sb = mk.sb("xs_sb", [NS, D], F32, esS); b_xs_sb = mk.buf("xs_sb")
                    hbs = mk.sb("hbs", [NS, D], BF16, esS); b_hbs = mk.buf("hbs")
                    hTs = mk.sb("hTs", [128, 8, NS], BF16, esS); b_hTs = mk.buf("hTs")
                    zs = mk.sb("zs", [NS, INA], F32, esS); b_zs = mk.buf("zs")
                    mk.dma("sp", lambda h: h.dma_start(out=xs_sb[:], in_=xs_d), writes=[b_xs_sb])
                    rmsnorm_tile(xs_sb[:], NS, gbA[0:NS, 0, :], [b_xs_sb, b_gbA], (sst_, b_sst_), out_bf=hbs[:], b_bf=b_hbs)
                    transpose_to(hbs[:], b_hbs, hTs[:, :, :], b_hTs, 8, NS)
                    for c0 in range(0, INA, 512):
                        n = min(512, INA - c0)
                        pb, bpb = getbank()
                        for kc in range(8):
                            mk.op("pe", lambda h, kc=kc: h.matmul(pb[0:NS, 0:n], lhsT=hTs[:, kc, :], rhs=Win[:, kc, c0:c0 + n], start=(kc == 0), stop=(kc == 7)),
                                  reads=[b_hTs, b_Win], writes=[bpb], sig=(kc == 7), add=(kc > 0))
                        evac(zs[:, c0:c0 + n], pb[0:NS, 0:n], [bpb], [b_zs], add=True)
                    qTs = mk.sb("qTs", [128, 4, NS], BF16, esS); b_qTs = mk.buf("qTs")
                    for hh in range(4):
                        pb, bpb = getbank()
                        for kc in range(8):
                            mk.op("pe", lambda h, kc=kc: h.matmul(pb[:, 0:NS], lhsT=Win[:, kc, hh * 128:(hh + 1) * 128], rhs=hTs[:, kc, :], start=(kc == 0), stop=(kc == 7)),
                                  reads=[b_hTs, b_Win], writes=[bpb], sig=(kc == 7), add=(kc > 0))
                        evac(qTs[:, hh, :], pb[:, 0:NS], [bpb], [b_qTs], add=True)
                    Lq = mk.sb("Lq", [128, NS, 32], BF16, esS); b_Lq = mk.buf("Lq")
                    mk.op("dve", lambda h: h.memset(Lq[:], 0.0), writes=[b_Lq])
                    for s_ in range(NS):
                        for g in range(2):
                            c0 = s_ * 8 + g * 4
                            mk.op("dve", lambda h, s_=s_, g=g, c0=c0: h.tensor_copy(out=Lq[g * 64:(g + 1) * 64, s_, c0:c0 + 4], in_=qTs[g * 64:(g + 1) * 64, :, s_]),
                                  reads=[b_qTs], writes=[b_Lq], add=True)
                    mk.dma("sp", lambda h: h.dma_start(out=o_cmp_s, in_=zs[:, 512:768]), reads=[b_zs], is_out=True)
                    mk.dma("sp", lambda h: h.dma_start(out=o_slc_s, in_=zs[:, 768:1024]), reads=[b_zs], is_out=True)
                    for s_ in range(NS):
                        mk.dma("sp", lambda h, s_=s_: h.dma_start(out=o_win_s[s_, 0:511, :], in_=swin_d[s_, 1:512, :]), is_out=True)
                        mk.dma("sp", lambda h, s_=s_: h.dma_start(out=o_win_s[s_, 511:512, :], in_=zs[s_:s_ + 1, 1024:1280]), reads=[b_zs], is_out=True)
                        mk.dma("sp", lambda h, s_=s_: h.dma_start(out=o_conv_s[s_, 0:29, :], in_=sconv_d[s_, 1:30, :]), is_out=True)
                        mk.dma("sp", lambda h, s_=s_: h.dma_start(out=o_pool_s[s_, 0:14, :], in_=spool_d[s_, 1:15, :]), is_out=True)

                    ptT = mk.sb("ptT", [128, NS], I32, esS); b_ptT = mk.buf("ptT")
                    with nc.allow_non_contiguous_dma(reason="page table transpose"):
                        mk.dma("sp", lambda h: h.dma_start(out=ptT[:], in_=pt_d.rearrange("s p -> p s")), writes=[b_ptT])
                    ptTf = mk.sb("ptTf", [128, NS], F32, esS); b_ptTf = mk.buf("ptTf")
                    mk.op("dve", lambda h: h.tensor_copy(out=ptTf[:], in_=ptT[:]), reads=[b_ptT], writes=[b_ptTf])
                    mk.op("dve", lambda h: h.tensor_scalar(out=ptTf[:], in0=ptTf[:], scalar1=8.0, scalar2=None, op0=ALU.mult), reads=[b_ptTf], writes=[b_ptTf])
                    idxcf = mk.sb("idxcf", [128, NS, 8], F32, esS); b_idxcf = mk.buf("idxcf")
                    mk.op("dve", lambda h: h.tensor_tensor(out=idxcf[:], in0=ptTf[:].unsqueeze(2).to_broadcast([128, NS, 8]),
                                                           in1=csamp[:, C_CADD:C_CADD + 8].unsqueeze(1).to_broadcast([128, NS, 8]), op=ALU.add),
                          reads=[b_ptTf, b_csamp], writes=[b_idxcf])
                    idxc = mk.sb("idxc", [128, NS * 8], I32, esS); b_idxc = mk.buf("idxc")
                    mk.op("dve", lambda h: h.tensor_copy(out=idxc[:], in_=idxcf[:].rearrange("p a b -> p (a b)")), reads=[b_idxcf], writes=[b_idxc])

                    KCs = mk.sb("KCs", [128, NS, 8, 128], BF16, esS); b_KCs = [mk.buf(f"KCs{i}") for i in range(NS)]
                    VCs = mk.sb("VCs", [128, NS, 8, 128], BF16, esS); b_VCs = [mk.buf(f"VCs{i}") for i in range(NS)]
                    esC = ExitStack()
                    with esC:
                        X_r = Rot([(mk.sb(f"Xg{i}", [128, 16, 256], F32, esC), mk.buf(f"Xg{i}")) for i in range(2)])
                        XTP = (mk.sb("XTP", [128, 2, 16, 128], BF16, esC), mk.buf("XTP"))
                        XT_r = Rot([(mk.sb(f"XT{i}", [128, 2, 16, 128], BF16, esC), mk.buf(f"XT{i}")) for i in range(2)])
                        gh_r = Rot([(mk.sb(f"gh{i}", [128, 128], BF16, esC), mk.buf(f"gh{i}")) for i in range(2)])

                        def compress_chunk(s_, c, XTa, XTb, shift):
                            for kv in range(2):
                                pb, bpb = getbank()
                                for l in range(32):
                                    if l < 16:
                                        rhs = XTa[0][:, kv, l, :]; rb = XTa[1]; ncol = 128
                                    elif not shift:
                                        rhs = XTb[0][:, kv, l - 16, :]; rb = XTb[1]; ncol = 128
                                    else:
                                        rhs = XTb[0][:, kv, l - 16, 1:128]; rb = XTb[1]; ncol = 127
                                    mk.op("pe", lambda h, l=l, rhs=rhs, ncol=ncol: h.matmul(pb[:, 0:ncol], lhsT=W1[kv][:, l, :], rhs=rhs, start=(l == 0), stop=(l == 31)),
                                          reads=[b_W1, rb], writes=[bpb], sig=(l == 31), add=(l > 0))
                                gh, b_gh = gh_r.next()
                                mk.op("act", lambda h, gh=gh, pb=pb, kv=kv: h.activation(out=gh[:], in_=pb[:, 0:128], func=AF.Gelu, bias=postt[:, kv:kv + 1]),
                                      reads=[bpb, b_post], writes=[b_gh])
                                pb2, bpb2 = getbank()
                                if kv == 0:
                                    mk.op("pe", lambda h, gh=gh, pb2=pb2: h.matmul(pb2[:, 0:128], lhsT=W2[0][:], rhs=gh[:], start=True, stop=True), reads=[b_W2, b_gh], writes=[bpb2])
                                    evac(KCs[:, s_, c, :], pb2[:, 0:128], [bpb2], [b_KCs[s_]], add=True)
                                else:
                                    mk.op("pe", lambda h, gh=gh, pb2=pb2: h.matmul(pb2[:, 0:128], lhsT=gh[:], rhs=W2[1][:], start=True, stop=True), reads=[b_W2, b_gh], writes=[bpb2])
                                    evac(VCs[:, s_, c, :], pb2[:, 0:128], [bpb2], [b_VCs[s_]], add=True)

                        for s_ in range(NS):
                            prev = None
                            for c in range(8):
                                X, b_X = X_r.next()
                                col = s_ * 8 + c
                                mk.dma("pool", lambda h, X=X, col=col: h.indirect_dma_start(out=X[:].rearrange("p a b -> p (a b)"), out_offset=None, in_=cmp_rows,
                                                                                          in_offset=bass.IndirectOffsetOnAxis(ap=idxc[:, col:col + 1], axis=0)),
                                       reads=[b_idxc], writes=[b_X])
                                XTc = XTP if c == 0 else XT_r.next()
                                for kv in range(2):
                                    for q4 in range(4):
                                        pb, bpb = getbank()
                                        for r in range(4):
                                            l = q4 * 4 + r
                                            mk.op("pe", lambda h, X=X, l=l, r=r, kv=kv: h.transpose(out=pb[:, r * 128:(r + 1) * 128], in_=X[:, l, kv * 128:(kv + 1) * 128], identity=identf[:]),
                                                  reads=[b_X, b_identf], writes=[bpb], sig=(r == 3), add=(r > 0))
                                        evac(XTc[0][:, kv, q4 * 4:(q4 + 1) * 4, :], pb[:, :].rearrange("p (a b) -> p a b", b=128), [bpb], [XTc[1]], add=True)
                                if prev is not None:
                                    compress_chunk(s_, c - 1, prev, XTc, False)
                                prev = XTc
                            compress_chunk(s_, 7, prev, XTP, True)
                    mk.barrier()
                    pS, bS = getdbl()
                    for half in range(2):
                        for s_ in range(NS):
                            mk.op("pe", lambda h, s_=s_, half=half: h.matmul(pS[0:32, half * 512:(half + 1) * 512], lhsT=Lq[:, s_, :],
                                                                            rhs=KCs[:, s_, half * 4:(half + 1) * 4, :].rearrange("p a b -> p (a b)"), start=(s_ == 0), stop=(s_ == NS - 1)),
                                  reads=[b_Lq, b_KCs[s_]], writes=[bS[half]], sig=(s_ == NS - 1), add=(s_ > 0))
                    pall = mk.sb("pall", [32, 1024], F32, esS); b_pall = mk.buf("pall")
                    sm = mk.sb("sm", [32, 4], F32, esS); b_sm = mk.buf("sm")
                    mk.op("act", lambda h: h.activation(out=pall[:], in_=pS[0:32, :], func=AF.Exp, scale=SCALE), reads=bS, writes=[b_pall])
                    mk.op("dve", lambda h: h.memset(pall[:, 1023:1024], 0.0), reads=[b_pall], writes=[b_pall])
                    mk.op("dve", lambda h: h.tensor_reduce(out=sm[:, 0:1], in_=pall[:], axis=AX.X, op=ALU.add), reads=[b_pall], writes=[b_sm])
                    mk.op("dve", lambda h: h.reciprocal(out=sm[:, 1:2], in_=sm[:, 0:1]), reads=[b_sm], writes=[b_sm])
                    mk.op("dve", lambda h: h.tensor_scalar(out=pall[:], in0=pall[:], scalar1=sm[:, 1:2], scalar2=None, op0=ALU.mult), reads=[b_pall, b_sm], writes=[b_pall])
                    hpp = mk.sb("hpp", [128, 1], I32, esS); b_hpp = mk.buf("hpp")
                    esI = ExitStack()
                    esI.__enter__()
                    Bbuf = mk.sb("Bbuf", [8, 1032], F32, esI); b_Bbuf = mk.buf("Bbuf")
                    mk.op("dve", lambda h: h.memset(Bbuf[:], 0.0), writes=[b_Bbuf])
                    pI, bI = getdbl()
                    for half in range(2):
                        mk.op("pe", lambda h, half=half: h.matmul(pI[0:8, half * 512:(half + 1) * 512], lhsT=csamp[0:32, C_GSEL:C_GSEL + 8], rhs=pall[:, half * 512:(half + 1) * 512],
                                                                 start=True, stop=True), reads=[b_csamp, b_pall], writes=[bI[half]])
                    mk.op("dve", lambda h: h.tensor_copy(out=Bbuf[:, 1:1025].rearrange("p (g c) -> p c g", c=8), in_=pI[0:8, :].rearrange("p (c g) -> p c g", g=128)),
                          reads=bI, writes=[b_Bbuf])
                    scs = mk.sb("scs", [8, 260], F32, esI); b_scs = mk.buf("scs")
                    scs2 = mk.sb("scs2", [8, 260], F32, esI); b_scs2 = mk.buf("scs2")
                    sAs = mk.sb("sAs", [8, 257], F32, esI); b_sAs = mk.buf("sAs")
                    Bv = Bbuf[:, 0:1028].rearrange("p (j r) -> p j r", r=4)
                    Bv2 = Bbuf[:, 4:1032].rearrange("p (j r) -> p j r", r=4)
                    mk.op("dve", lambda h: h.tensor_reduce(out=sAs[:], in_=Bv[:, :, 1:4], axis=AX.X, op=ALU.add), reads=[b_Bbuf], writes=[b_sAs])
                    mk.op("dve", lambda h: h.scalar_tensor_tensor(out=scs[:, 0:257], in0=sAs[:], scalar=2.0, in1=Bv[:, :, 0], op0=ALU.mult, op1=ALU.add),
                          reads=[b_sAs, b_Bbuf], writes=[b_scs])
                    mk.op("dve", lambda h: h.tensor_tensor(out=scs[:, 0:257], in0=scs[:, 0:257], in1=Bv2[:, :, 0], op=ALU.add), reads=[b_scs, b_Bbuf], writes=[b_scs])
                    mk.op("dve", lambda h: h.memset(scs[:, 0:1], 1e9), reads=[b_scs], writes=[b_scs])
                    mk.op("dve", lambda h: h.memset(scs[:, 255:256], 2e9), reads=[b_scs], writes=[b_scs])
                    mk.op("dve", lambda h: h.memset(scs[:, 256:257], 3e9), reads=[b_scs], writes=[b_scs])
                    mk.op("dve", lambda h: h.memset(scs[:, 257:260], -4e9), reads=[b_scs], writes=[b_scs])
                    m8s = mk.sb("m8s", [8, 16], F32, esI); b_m8s = mk.buf("m8s")
                    i8s = mk.sb("i8s", [8, 16], U32, esI); b_i8s = mk.buf("i8s")
                    mk.op("dve", lambda h: h.max(out=m8s[:, 0:8], in_=scs[:]), reads=[b_scs], writes=[b_m8s])
                    mk.op("dve", lambda h: h.max_index(out=i8s[:, 0:8], in_max=m8s[:, 0:8], in_values=scs[:]), reads=[b_scs, b_m8s], writes=[b_i8s])
                    mk.op("dve", lambda h: h.match_replace(out=scs2[:], in_to_replace=m8s[:, 0:8], in_values=scs[:], imm_value=-4e9), reads=[b_scs, b_m8s], writes=[b_scs2])
                    mk.op("dve", lambda h: h.max(out=m8s[:, 8:16], in_=scs2[:]), reads=[b_scs2], writes=[b_m8s])
                    mk.op("dve", lambda h: h.max_index(out=i8s[:, 8:16], in_max=m8s[:, 8:16], in_values=scs2[:]), reads=[b_scs2, b_m8s], writes=[b_i8s])
                    ixf = mk.sb("ixf", [8, 16], F32, esI); b_ixf = mk.buf("ixf")
                    hlf = mk.sb("hlf", [8, 16], F32, esI); b_hlf = mk.buf("hlf")
                    pgf = mk.sb("pgf", [8, 16], F32, esI); b_pgf = mk.buf("pgf")
                    mk.op("dve", lambda h: h.tensor_copy(out=ixf[:], in_=i8s[:]), reads=[b_i8s], writes=[b_ixf])
                    io2 = mk.sb("io2", [8, 128], F32, esI); b_io2 = mk.buf("io2")
                    oh2 = mk.sb("oh2", [8, 16, 128], F32, esI); b_oh2 = mk.buf("oh2")
                    mk.op("dve", lambda h: h.tensor_scalar(out=io2[:], in0=csamp[0:8, C_IOTA:C_IOTA + 128], scalar1=1.0, scalar2=2.0, op0=ALU.add, op1=ALU.mult), reads=[b_csamp], writes=[b_io2])
                    mk.op("dve", lambda h: h.tensor_tensor(out=oh2[:], in0=ixf[:].unsqueeze(2).to_broadcast([8, 16, 128]), in1=io2[:].unsqueeze(1).to_broadcast([8, 16, 128]), op=ALU.is_ge),
                          reads=[b_ixf, b_io2], writes=[b_oh2])
                    mk.op("dve", lambda h: h.tensor_reduce(out=pgf[:], in_=oh2[:], axis=AX.X, op=ALU.add), reads=[b_oh2], writes=[b_pgf])
                    mk.op("dve", lambda h: h.scalar_tensor_tensor(out=hlf[:], in0=pgf[:], scalar=-2.0, in1=ixf[:], op0=ALU.mult, op1=ALU.add), reads=[b_pgf, b_ixf], writes=[b_hlf])
                    pt8 = mk.sb("pt8", [8, 128], I32, esI); b_pt8 = mk.buf("pt8")
                    for s_ in range(NS):
                        for g in range(2):
                            mk.dma("sp", lambda h, s_=s_, g=g: h.dma_start(out=pt8[s_ * 2 + g:s_ * 2 + g + 1, :], in_=pt_d[s_:s_ + 1, :]), writes=[b_pt8], add=True)
                    pt8f = mk.sb("pt8f", [8, 128], F32, esI); b_pt8f = mk.buf("pt8f")
                    mk.op("dve", lambda h: h.tensor_copy(out=pt8f[:], in_=pt8[:]), reads=[b_pt8], writes=[b_pt8f])
                    oh = mk.sb("oh", [8, 16, 128], F32, esI); b_oh = mk.buf("oh")
                    mk.op("dve", lambda h: h.tensor_tensor(out=oh[:], in0=csamp[0:8, C_IOTA:C_IOTA + 128].unsqueeze(1).to_broadcast([8, 16, 128]),
                                                           in1=pgf[:].unsqueeze(2).to_broadcast([8, 16, 128]), op=ALU.is_equal), reads=[b_csamp, b_pgf], writes=[b_oh])
                    mk.op("dve", lambda h: h.tensor_tensor(out=oh[:], in0=oh[:], in1=pt8f[:].unsqueeze(1).to_broadcast([8, 16, 128]), op=ALU.mult), reads=[b_oh, b_pt8f], writes=[b_oh])
                    phys = mk.sb("phys", [8, 16], F32, esI); b_phys = mk.buf("phys")
                    mk.op("dve", lambda h: h.tensor_reduce(out=phys[:], in_=oh[:], axis=AX.X, op=ALU.add), reads=[b_oh], writes=[b_phys])
                    mk.op("dve", lambda h: h.scalar_tensor_tensor(out=phys[:], in0=phys[:], scalar=2.0, in1=hlf[:], op0=ALU.mult, op1=ALU.add), reads=[b_phys, b_hlf], writes=[b_phys])
                    hpi = mk.sb("hpi", [8, 16], I32, esI); b_hpi = mk.buf("hpi")
                    mk.op("dve", lambda h: h.tensor_copy(out=hpi[:], in_=phys[:]), reads=[b_phys], writes=[b_hpi])
                    mk.dma("sp", lambda h: h.dma_start(out=hp_scr, in_=hpi[:]), reads=[b_hpi], writes=[b_hp_scr])
                    mk.op("dve", lambda h: h.memset(hpp[:], 0), writes=[b_hpp])
                    with nc.allow_non_contiguous_dma(reason="index relayout"):
                        for g in range(2):
                            for s_ in range(NS):
                                p0 = g * 64 + s_ * 15
                                mk.dma("sp", lambda h, g=g, s_=s_, p0=p0: h.dma_start(out=hpp[p0:p0 + 15, :], in_=hp_scr[s_ * 2 + g:s_ * 2 + g + 1, 1:16].rearrange("a b -> b a")),
                                       reads=[b_hp_scr], writes=[b_hpp], add=True)
                    esI.close()
                    mk.barrier()
                    oS = mk.sb("oS", [NS, 2, 4, 65], F32, esS); b_oS = mk.buf("oS")
                    esG = ExitStack()
                    with esG:
                        SG = mk.sb("SG", [128, 64, 256], F32, esG); b_SG = mk.buf("SG")
                        mk.dma("pool", lambda h: h.indirect_dma_start(out=SG[:].rearrange("p a b -> p (a b)"), out_offset=None, in_=slc_hp,
                                                                      in_offset=bass.IndirectOffsetOnAxis(ap=hpp[:, 0:1], axis=0)), reads=[b_hpp], writes=[b_SG])
                        qb = mk.sb("qb", [128, 512], F32, esG); b_qb = mk.buf("qb")
                        pb, bpb = getbank()
                        mk.op("pe", lambda h: h.matmul(pb[:, :], lhsT=csamp[0:NS, C_SEL2:C_SEL2 + 128], rhs=zs[:, 0:512], start=True, stop=True), reads=[b_csamp, b_zs], writes=[bpb])
                        evac(qb[:], pb[:, :], [bpb], [b_qb])
                        tmpS = mk.sb("tmpS", [128, 32, 64], F32, esG); b_tmpS = mk.buf("tmpS")
                        scS = mk.sb("scS", [128, 4, 64], F32, esG); b_scS = mk.buf("scS")
                        oaug = mk.sb("oaug", [128, 4, 65], F32, esG); b_oaug = mk.buf("oaug")
                        mk.op("dve", lambda h: h.memset(oaug[:], 0.0), writes=[b_oaug])
                        for g in range(2):
                            pr = slice(g * 64, g * 64 + 60)
                            for hh in range(4):
                                qv = qb[pr, hh * 128 + g * 64:hh * 128 + g * 64 + 64]
                                for rh in range(2):
                                    rr = slice(rh * 32, (rh + 1) * 32)
                                    mk.op("dve", lambda h, pr=pr, qv=qv, g=g, rr=rr: h.tensor_tensor(out=tmpS[pr], in0=SG[pr, rr, g * 64:(g + 1) * 64],
                                                                                                  in1=qv.unsqueeze(1).to_broadcast([60, 32, 64]), op=ALU.mult),
                                          reads=[b_SG, b_qb], writes=[b_tmpS])
                                    mk.op("dve", lambda h, pr=pr, hh=hh, rr=rr: h.tensor_reduce(out=scS[pr, hh, rr], in_=tmpS[pr], axis=AX.X, op=ALU.add), reads=[b_tmpS], writes=[b_scS], add=True)
                            mk.op("act", lambda h, pr=pr: h.activation(out=scS[pr], in_=scS[pr], func=AF.Exp, scale=SCALE), reads=[b_scS], writes=[b_scS])
                            mk.op("dve", lambda h, pr=pr: h.tensor_reduce(out=oaug[pr, :, 64], in_=scS[pr], axis=AX.X, op=ALU.add), reads=[b_scS], writes=[b_oaug], add=True)
                            for hh in range(4):
                                for dh in range(2):
                                    dd = slice(dh * 32, (dh + 1) * 32)
                                    vv = SG[pr, :, 128 + g * 64 + dh * 32:128 + g * 64 + (dh + 1) * 32].rearrange("p r d -> p d r")
                                    mk.op("dve", lambda h, pr=pr, vv=vv, hh=hh: h.tensor_tensor(out=tmpS[pr], in0=vv,
                                                                                              in1=scS[pr, hh, :].unsqueeze(1).to_broadcast([60, 32, 64]), op=ALU.mult),
                                          reads=[b_SG, b_scS], writes=[b_tmpS])
                                    mk.op("dve", lambda h, pr=pr, hh=hh, dd=dd: h.tensor_reduce(out=oaug[pr, hh, dd], in_=tmpS[pr], axis=AX.X, op=ALU.add), reads=[b_tmpS], writes=[b_oaug], add=True)
                        pO, bpO = getdbl()
                        for g in range(2):
                            mk.op("pe", lambda h, g=g: h.matmul(pO[0:NS, g * 512:g * 512 + 260], lhsT=csamp[:, C_GS + g * 4:C_GS + g * 4 + 4], rhs=oaug[:].rearrange("p a b -> p (a b)"),
                                                               start=True, stop=True), reads=[b_csamp, b_oaug], writes=[bpO[g]])
                        for g in range(2):
                            evac(oS[:, g, :, :], pO[0:NS, g * 512:g * 512 + 260].rearrange("p (a b) -> p a b", b=65), [bpO[g]], [b_oS], add=True)
                    mk.barrier()
                    qv4 = zs[:, 0:512].rearrange("p (a g d) -> p a g d", a=4, g=2)
                    tq = mk.sb("tq", [NS, 4, 2, 64], F32, esS); b_tq = mk.buf("tq")
                    enew = mk.sb("enew", [NS, 2, 4, 2], F32, esS); b_enew = mk.buf("enew")
                    for bi, c0 in enumerate((768, 1024)):
                        kn = zs[:, c0:c0 + 128].rearrange("p (g d) -> p g d", g=2)
                        mk.op("dve", lambda h, kn=kn: h.tensor_tensor(out=tq[:], in0=qv4, in1=kn.unsqueeze(1).to_broadcast([NS, 4, 2, 64]), op=ALU.mult), reads=[b_zs], writes=[b_tq])
                        mk.op("dve", lambda h, bi=bi: h.tensor_reduce(out=enew[:, bi, :, :], in_=tq[:], axis=AX.X, op=ALU.add), reads=[b_tq], writes=[b_enew], add=True)
                    mk.op("act", lambda h: h.activation(out=enew[:], in_=enew[:], func=AF.Exp, scale=SCALE), reads=[b_enew], writes=[b_enew])
                    Wn = mk.sb("Wn", [128, NS, 4, 256], F32, esS); b_Wn = mk.buf("Wn")
                    for s_ in range(NS):
                        mk.dma("sp", lambda h, s_=s_: h.dma_start(out=Wn[:, s_, :, :], in_=swin_d[s_].rearrange("(c p) d -> p c d", p=128)), writes=[b_Wn], add=True)
                    Vaug = mk.sb("Vaug", [128, NS, 4, 2, 65], F32, esS); b_Vaug = mk.buf("Vaug")
                    mk.op("pool", lambda h: h.memset(Vaug[:, :, :, :, 64:65], 1.0), writes=[b_Vaug])
                    for s_ in range(NS):
                        mk.op("pool", lambda h, s_=s_: h.tensor_copy(out=Vaug[:, s_, :, :, 0:64], in_=Wn[:, s_, :, 128:256].rearrange("p c (g d) -> p c g d", g=2)),
                              reads=[b_Wn], writes=[b_Vaug], add=True)
                    Pz = mk.sb("Pz", [128, NS, 4, 8, NS], F32, esS); b_Pz = mk.buf("Pz")
                    mk.op("dve", lambda h: h.memset(Pz[:], 0.0), writes=[b_Pz])
                    swt = mk.sb("swt", [128, 4, 8], F32, esS); b_swt = mk.buf("swt")
                    tw = mk.sb("tw", [128, 4, 64], F32, esS); b_tw = mk.buf("tw")
                    qbw = mk.sb("qbw", [128, 512], F32, esS); b_qbw = mk.buf("qbw")
                    for s_ in range(NS):
                        pb, bpb = getbank()
                        mk.op("pe", lambda h, s_=s_: h.matmul(pb[:, :], lhsT=csamp[0:NS, C_OH + s_ * 128:C_OH + (s_ + 1) * 128], rhs=zs[:, 0:512], start=True, stop=True),
                              reads=[b_csamp, b_zs], writes=[bpb])
                        evac(qbw[:], pb[:, :], [bpb], [b_qbw])
                        for g in range(2):
                            for hh in range(4):
                                qv = qbw[:, hh * 128 + g * 64:hh * 128 + g * 64 + 64]
                                mk.op("dve", lambda h, s_=s_, g=g, qv=qv: h.tensor_tensor(out=tw[:], in0=Wn[:, s_, :, g * 64:(g + 1) * 64], in1=qv.unsqueeze(1).to_broadcast([128, 4, 64]), op=ALU.mult),
                                      reads=[b_Wn, b_qbw], writes=[b_tw])
                                mk.op("dve", lambda h, g=g, hh=hh: h.tensor_reduce(out=swt[:, :, g * 4 + hh], in_=tw[:], axis=AX.X, op=ALU.add), reads=[b_tw], writes=[b_swt], add=True)
                        mk.op("act", lambda h, s_=s_: h.activation(out=Pz[:, s_, :, :, s_], in_=swt[:], func=AF.Exp, scale=SCALE), reads=[b_swt], writes=[b_Pz], add=True)
                        mk.op("dve", lambda h, s_=s_: h.memset(Pz[0:1, s_, 0, :, s_], 0.0), reads=[b_Pz], writes=[b_Pz], add=True)
                    pW, bpW = getdbl()
                    for gh in range(8):
                        g = gh // 4
                        first = True
                        for s_ in range(NS):
                            for c in range(4):
                                last = (s_ == NS - 1 and c == 3)
                                mk.op("pe", lambda h, gh=gh, g=g, s_=s_, c=c, first=first, last=last: h.matmul(
                                    pW[0:NS, g * 512 + (gh % 4) * 65:g * 512 + (gh % 4) * 65 + 65], lhsT=Pz[:, s_, c, gh, :], rhs=Vaug[:, s_, c, g, :], start=first, stop=last),
                                      reads=[b_Pz, b_Vaug], writes=[bpW[g]], sig=last, add=not (gh % 4 == 0 and first))
                                first = False
                    oW = mk.sb("oW", [NS, 2, 4, 65], F32, esS); b_oW = mk.buf("oW")
                    for g in range(2):
                        evac(oW[:, g, :, :], pW[0:NS, g * 512:g * 512 + 260].rearrange("p (a b) -> p a b", b=65), [bpW[g]], [b_oW], add=True)
                    pTs = mk.sb("pTs", [128, 8, 32], BF16, esS); b_pTs = mk.buf("pTs")
                    for c in range(8):
                        pb, bpb = getbank()
                        mk.op("pe", lambda h, c=c: h.transpose(out=pb[:, 0:32], in_=pall[:, c * 128:(c + 1) * 128], identity=identf[0:32, 0:32]), reads=[b_pall, b_identf], writes=[bpb])
                        evac(pTs[:, c, :], pb[:, 0:32], [bpb], [b_pTs], add=True)
                    pC, bpC = getdbl()
                    for sg_ in range(8):
                        s_, g = sg_ // 2, sg_ % 2
                        for c in range(8):
                            mk.op("pe", lambda h, sg_=sg_, s_=s_, g=g, c=c: h.matmul(pC[0:32, (sg_ // 4) * 512 + (sg_ % 4) * 64:(sg_ // 4) * 512 + (sg_ % 4) * 64 + 64], lhsT=pTs[:, c, :],
                                                                                   rhs=VCs[:, s_, c, g * 64:(g + 1) * 64], start=(c == 0), stop=(c == 7)),
                                  reads=[b_pTs, b_VCs[s_]], writes=[bpC[sg_ // 4]], sig=(c == 7), add=not (sg_ % 4 == 0 and c == 0))
                    ocx = mk.sb("ocx", [32, 8, 64], F32, esS); b_ocx = mk.buf("ocx")
                    for half in range(2):
                        evac(ocx[:, half * 4:(half + 1) * 4, :], pC[0:32, half * 512:half * 512 + 256].rearrange("p (a b) -> p a b", b=64), [bpC[half]], [b_ocx], add=True)
                    mk.op("dve", lambda h: h.tensor_tensor(out=ocx[:], in0=ocx[:], in1=csamp[0:32, C_DM:C_DM + 8].unsqueeze(2).to_broadcast([32, 8, 64]), op=ALU.mult),
                          reads=[b_ocx, b_csamp], writes=[b_ocx])
                    oc32 = mk.sb("oc32", [32, 64], F32, esS); b_oc32 = mk.buf("oc32")
                    mk.op("dve", lambda h: h.tensor_reduce(out=oc32[:], in_=ocx[:].rearrange("p a d -> p d a"), axis=AX.X, op=ALU.add), reads=[b_ocx], writes=[b_oc32])
                    pR, bpR = getbank()
                    for gh in range(8):
                        mk.op("pe", lambda h, gh=gh: h.matmul(pR[0:NS, gh * 64:(gh + 1) * 64], lhsT=csamp[0:32, C_SELGH + gh * 4:C_SELGH + gh * 4 + 4], rhs=oc32[:], start=True, stop=True),
                              reads=[b_csamp, b_oc32], writes=[bpR], sig=(gh == 7), add=(gh > 0))
                    sgs = mk.sb("sgs", [NS, 24], F32, esS); b_sgs = mk.buf("sgs")
                    mk.op("act", lambda h: h.activation(out=sgs[:], in_=zs[:, 1280:1304], func=AF.Sigmoid), reads=[b_zs], writes=[b_sgs])
                    cat_s = mk.sb("cat_s", [NS, D], F32, esS); b_cat_s = mk.buf("cat_s")
                    gv = sgs[:].rearrange("p (gh b) -> p gh b", b=3)
                    av = cat_s[:, 0:512].rearrange("p (gh d) -> p gh d", d=64)
                    mk.op("dve", lambda h: h.tensor_tensor(out=av, in0=pR[0:NS, :].rearrange("p (gh d) -> p gh d", d=64), in1=gv[:, :, 0:1].to_broadcast([NS, 8, 64]), op=ALU.mult),
                          reads=[bpR, b_sgs], writes=[b_cat_s])
                    fin_s = mk.sb("fin_s", [NS, 2, 4, 4], F32, esS); b_fin_s = mk.buf("fin_s")
                    tmo = mk.sb("tmo", [NS, 2, 4, 65], F32, esS); b_tmo = mk.buf("tmo")
                    for bi, (oX, b_oX) in enumerate(((oS, b_oS), (oW, b_oW))):
                        c0 = (768, 1024)[bi] + 128
                        vn = zs[:, c0:c0 + 128].rearrange("p (g d) -> p g d", g=2)
                        en = enew[:, bi, :, :].rearrange("p a g -> p g a")
                        mk.op("dve", lambda h, vn=vn, en=en: h.tensor_tensor(out=tmo[:, :, :, 0:64], in0=vn.unsqueeze(2).to_broadcast([NS, 2, 4, 64]),
                                                                         in1=en.unsqueeze(3).to_broadcast([NS, 2, 4, 64]), op=ALU.mult), reads=[b_zs, b_enew], writes=[b_tmo])
                        mk.op("dve", lambda h, en=en: h.tensor_copy(out=tmo[:, :, :, 64], in_=en), reads=[b_enew], writes=[b_tmo], add=True)
                        mk.op("dve", lambda h, oX=oX: h.tensor_tensor(out=oX[:], in0=oX[:], in1=tmo[:], op=ALU.add), reads=[b_oX, b_tmo], writes=[b_oX])
                        mk.op("dve", lambda h, oX=oX: h.reciprocal(out=fin_s[:, :, :, 0], in_=oX[:, :, :, 64]), reads=[b_oX], writes=[b_fin_s])
                        gsl = gv[:, :, bi + 1].rearrange("p (g a) -> p g a", g=2)
                        mk.op("dve", lambda h, gsl=gsl: h.tensor_tensor(out=fin_s[:, :, :, 1], in0=fin_s[:, :, :, 0], in1=gsl, op=ALU.mult), reads=[b_fin_s, b_sgs], writes=[b_fin_s])
                        mk.op("dve", lambda h, oX=oX: h.tensor_tensor(out=tmo[:, :, :, 0:64], in0=oX[:, :, :, 0:64], in1=fin_s[:, :, :, 1:2].to_broadcast([NS, 2, 4, 64]), op=ALU.mult),
                              reads=[b_oX, b_fin_s], writes=[b_tmo])
                        mk.op("dve", lambda h: h.tensor_tensor(out=av.rearrange("p (g a) d -> p g a d", g=2), in0=av.rearrange("p (g a) d -> p g a d", g=2), in1=tmo[:, :, :, 0:64], op=ALU.add),
                              reads=[b_cat_s, b_tmo], writes=[b_cat_s])
                    a_s = mk.sb("a_s", [NS, 512], F32, esS); b_a_s = mk.buf("a_s")
                    mk.op("act", lambda h: h.activation(out=a_s[:], in_=zs[:, 1816:2328], func=AF.Sigmoid), reads=[b_zs], writes=[b_a_s])
                    mk.op("dve", lambda h: h.tensor_tensor(out=a_s[:], in0=a_s[:], in1=zs[:, 1304:1816], op=ALU.mult), reads=[b_a_s, b_zs], writes=[b_a_s])
                    for s_ in range(NS):
                        mk.dma("sp", lambda h, s_=s_: h.dma_start(out=o_conv_s[s_, 29:30, :], in_=a_s[s_:s_ + 1, :]), reads=[b_a_s], is_out=True)
                    stc = mk.sb("stc", [120, 512], F32, esS); b_stc = mk.buf("stc")
                    w30 = mk.sb("w30", [120, 512], F32, esS); b_w30 = mk.buf("w30")
                    mk.dma("sp", lambda h: h.dma_start(out=stc[:], in_=sconv_d.rearrange("s k c -> (s k) c")), writes=[b_stc])
                    for s_ in range(NS):
                        mk.dma("sp", lambda h, s_=s_: h.dma_start(out=w30[s_ * 30:(s_ + 1) * 30, :], in_=conv_w[0:30, :]), writes=[b_w30], add=True)
                    cb4 = mk.sb("cb4", [NS, 4, 512], F32, esS); b_cb4 = mk.buf("cb4")
                    mk.dma("sp", lambda h: h.dma_start(out=cb4[:, 0, :], in_=conv_w[30:31, :].partition_broadcast(NS)), writes=[b_cb4], add=True)
                    for i_, v in enumerate((conv_b, ln_g, ln_b)):
                        mk.dma("sp", lambda h, i_=i_, v=v: h.dma_start(out=cb4[:, 1 + i_, :], in_=v.rearrange("(o c) -> o c", o=1).partition_broadcast(NS)), writes=[b_cb4], add=True)
                    mk.op("dve", lambda h: h.tensor_tensor(out=stc[:], in0=stc[:], in1=w30[:], op=ALU.mult), reads=[b_stc, b_w30], writes=[b_stc])
                    pY, bpY = getbank()
                    mk.op("pe", lambda h: h.matmul(pY[0:NS, :], lhsT=csamp[0:120, C_IND30:C_IND30 + 4], rhs=stc[:], start=True, stop=True), reads=[b_csamp, b_stc], writes=[bpY])
                    ys = mk.sb("ys", [NS, 512], F32, esS); b_ys = mk.buf("ys")
                    ysq = mk.sb("ysq", [NS, 512], F32, esS); b_ysq = mk.buf("ysq")
                    mk.op("dve", lambda h: h.tensor_tensor(out=ys[:], in0=a_s[:], in1=cb4[:, 0, :], op=ALU.mult), reads=[b_a_s, b_cb4], writes=[b_ys])
                    mk.op("dve", lambda h: h.tensor_tensor(out=ys[:], in0=ys[:], in1=pY[0:NS, :], op=ALU.add), reads=[b_ys, bpY], writes=[b_ys])
                    mk.op("dve", lambda h: h.tensor_tensor(out=ys[:], in0=ys[:], in1=cb4[:, 1, :], op=ALU.add), reads=[b_ys, b_cb4], writes=[b_ys])
                    lst = mk.sb("lst", [NS, 8], F32, esS); b_lst = mk.buf("lst")
                    mk.op("dve", lambda h: h.tensor_reduce(out=lst[:, 0:1], in_=ys[:], axis=AX.X, op=ALU.add), reads=[b_ys], writes=[b_lst])
                    mk.op("dve", lambda h: h.tensor_scalar(out=lst[:, 1:2], in0=lst[:, 0:1], scalar1=1.0 / 512, scalar2=None, op0=ALU.mult), reads=[b_lst], writes=[b_lst])
                    mk.op("dve", lambda h: h.tensor_scalar(out=ys[:], in0=ys[:], scalar1=lst[:, 1:2], scalar2=None, op0=ALU.subtract), reads=[b_ys, b_lst], writes=[b_ys])
                    mk.op("act", lambda h: h.activation(out=ysq[:], in_=ys[:], func=AF.Square, accum_out=lst[:, 2:3]), reads=[b_ys], writes=[b_ysq, b_lst])
                    mk.op("act", lambda h: h.activation(out=lst[:, 3:4], in_=lst[:, 2:3], func=AF.Sqrt, scale=1.0 / 512, bias=EPS), reads=[b_lst], writes=[b_lst])
                    mk.op("dve", lambda h: h.reciprocal(out=lst[:, 4:5], in_=lst[:, 3:4]), reads=[b_lst], writes=[b_lst])
                    mk.op("dve", lambda h: h.scalar_tensor_tensor(out=ys[:], in0=ys[:], scalar=lst[:, 4:5], in1=cb4[:, 2, :], op0=ALU.mult, op1=ALU.mult), reads=[b_ys, b_lst, b_cb4], writes=[b_ys])
                    mk.op("dve", lambda h: h.tensor_tensor(out=ys[:], in0=ys[:], in1=cb4[:, 3, :], op=ALU.add), reads=[b_ys, b_cb4], writes=[b_ys])
                    mk.op("act", lambda h: h.activation(out=cat_s[:, 512:1024], in_=ys[:], func=AF.Silu), reads=[b_ys], writes=[b_cat_s], add=True)
                    catb = mk.sb("catb", [NS, D], BF16, esS); b_catb = mk.buf("catb")
                    mk.op("act", lambda h: h.copy(out=catb[:], in_=cat_s[:]), reads=[b_cat_s], writes=[b_catb])
                    catTs = mk.sb("catTs", [128, 8, NS], BF16, esS); b_catTs = mk.buf("catTs")
                    transpose_to(catb[:], b_catb, catTs[:, :, :], b_catTs, 8, NS)
                    pM, bpM = getdbl()
                    for half in range(2):
                        for kc in range(8):
                            mk.op("pe", lambda h, kc=kc, half=half: h.matmul(pM[0:NS, half * 512:(half + 1) * 512], lhsT=catTs[:, kc, :], rhs=Wout[:, kc, half * 512:(half + 1) * 512],
                                                                           start=(kc == 0), stop=(kc == 7)), reads=[b_catTs, b_Wout], writes=[bpM[half]], sig=(kc == 7), add=(kc > 0))
                    mk.op("dve", lambda h: h.tensor_tensor(out=xs_sb[:], in0=xs_sb[:], in1=pM[0:NS, :], op=ALU.add), reads=[b_xs_sb] + bpM, writes=[b_xs_sb])
                    mk.dma("sp", lambda h: h.dma_start(out=xs_scr, in_=xs_sb[:]), reads=[b_xs_sb], writes=[b_xs_scr])
                mk.barrier()
            Eall = mk.sb("Eall", [128, 32, 128], BF16, esA); b_Eall = mk.buf("Eall")
            mk.dma("pool", lambda h: h.dma_start(out=Eall[0:64], in_=c_eall), writes=[b_Eall])
            mk.dma("pool", lambda h: h.dma_start(out=Eall[64:128], in_=c_eall), writes=[b_Eall], add=True)

            KTs = mk.sb("KTs", [128, T], BF16, esA); b_KTs = [mk.buf(f"KTs{i}") for i in range(NBLK)]
            KTw = mk.sb("KTw", [128, 2, TB], BF16, esA); b_KTw = [mk.buf(f"KTw{i}") for i in range(2)]
            Vs = mk.sb("Vs", [128, 32, 2, 65], BF16, esA); b_Vs = [mk.buf(f"Vs{i}") for i in range(NBLK)]
            Vw = mk.sb("Vw", [128, 8, 2, 65], BF16, esA); b_Vw = [mk.buf(f"Vw{i}") for i in range(2)]
            mk.op("pool", lambda h: h.memset(Vs[:, :, :, 64:65], 1.0), writes=b_Vs)
            mk.op("pool", lambda h: h.memset(Vw[:, :, :, 64:65], 1.0), writes=b_Vw)
            CKV = mk.sb("CKV", [128, 2, 16 + TB], BF16, esA); b_CKV = mk.buf("CKV")
            mk.op("pool", lambda h: h.memset(CKV[:, :, 0:16], 0.0), writes=[b_CKV])
            KCT = mk.sb("KCT", [128, 256], BF16, esA); b_KCT = mk.buf("KCT")
            GV = mk.sb("GV", [128, 256], BF16, esA); b_GV = mk.buf("GV")
            VC = mk.sb("VC", [128, 2, 2, 65], BF16, esA); b_VC = mk.buf("VC")
            mk.op("pool", lambda h: h.memset(KCT[:], 0.0), writes=[b_KCT])
            mk.op("pool", lambda h: h.memset(GV[:], 0.0), writes=[b_GV])
            mk.op("pool", lambda h: h.memset(VC[:], 0.0), writes=[b_VC])
            mk.op("pool", lambda h: h.memset(VC[:, :, :, 64:65], 1.0), writes=[b_VC])
            mk.op("pool", lambda h: h.memset(VC[0:1, 0, :, 64:65], 0.0), writes=[b_VC])
            gk = mk.sb("gk", [128, 32], BF16, esA); b_gk = mk.buf("gk")

            xt_r = Rot([(mk.sb(f"xt{i}", [128, D], F32, esA), mk.buf(f"xt{i}")) for i in range(2)])
            hb_r = Rot([(mk.sb(f"hb{i}", [128, D], BF16, esA), mk.buf(f"hb{i}")) for i in range(2)])
            st_r = Rot([(mk.sb(f"st{i}", [128, 4], F32, esA), mk.buf(f"st{i}")) for i in range(2)])
            hT_r = Rot([(mk.sb(f"hT{i}", [128, 8, TB], BF16, esA), mk.buf(f"hT{i}")) for i in range(1)])
            qT_r = Rot([(mk.sb(f"qT{i}", [128, 2, 4, TB], BF16, esA), mk.buf(f"qT{i}")) for i in range(1)])
            for qt_, bq_ in qT_r.items:
                mk.op("pool", lambda h, qt_=qt_: h.memset(qt_[:], 0.0), writes=[bq_])
            zkv_r = Rot([(mk.sb(f"zkv{i}", [128, 792], F32, esA), mk.buf(f"zkv{i}")) for i in range(2)])
            sg_r = Rot([(mk.sb(f"sg{i}", [128, 4, 24], F32, esA), mk.buf(f"sg{i}")) for i in range(2)])
            aT = mk.sb("aT", [128, 4, 30 + TB], F32, esA); b_aT = [mk.buf(f"aT{c}") for c in range(4)]
            mk.op("pool", lambda h: h.memset(aT[:, :, 0:30], 0.0), writes=b_aT)
            yc = mk.sb("yc", [128, 4, TB], F32, esA); b_yc = [mk.buf(f"yc{c}") for c in range(4)]
            tmp_r = Rot([(mk.sb(f"tmpc{i}", [128, TB], F32, esA), mk.buf(f"tmpc{i}")) for i in range(2)])
            mean_sb = mk.sb("mean_sb", [128, TB], F32, esA); b_mean = mk.buf("mean")
            rstd_sb = mk.sb("rstd_sb", [128, TB], F32, esA); b_rstd = mk.buf("rstd")
            catT_r = Rot([(mk.sb(f"catT{i}", [128, 8, TB], BF16, esA), mk.buf(f"catT{i}")) for i in range(1)])
            PT_r = Rot([(mk.sb(f"PT{i}", [128, 4, 128], BF16, esA), mk.buf(f"PT{i}")) for i in range(4)])
            eS = mk.sb("eS", [128, 4, 256], F32, esA); b_eS = mk.buf("eS")
            imp = mk.sb("imp", [128, 2, 260], F32, esA); b_imp = mk.buf("imp")
            mk.op("pool", lambda h: h.memset(imp[:], 0.0), writes=[b_imp])
            sst = mk.sb("sst", [128, 16], F32, esA); b_sst = mk.buf("sst")
            sA = mk.sb("sA", [128, 64], F32, esA); b_sA = mk.buf("sA")
            sc = mk.sb("sc", [128, 2, 64], F32, esA); b_sc = mk.buf("sc")
            sc2 = mk.sb("sc2", [128, 2, 64], F32, esA); b_sc2 = mk.buf("sc2")
            m8 = mk.sb("m8", [128, 16], F32, esA); b_m8 = mk.buf("m8")
            nm = mk.sb("nm", [128, 2, 64], BF16, esA); b_nm = mk.buf("nm")
            nmT_r = Rot([(mk.sb(f"nmT{i}", [128, 2, 128], BF16, esA), mk.buf(f"nmT{i}")) for i in range(2)])
            for nt_, bn_ in nmT_r.items:
                mk.op("pool", lambda h, nt_=nt_: h.memset(nt_[:], 0.0), writes=[bn_])
            ot_r = Rot([(mk.sb(f"ot{i}", [65, 512], F32, esA), mk.buf(f"ot{i}")) for i in range(2)])
            fin = mk.sb("fin", [128, 8], F32, esA); b_fin = mk.buf("fin")
            pvs_r = Rot([(mk.sb(f"pvs{i}", [128, 260], F32, esA), mk.buf(f"pvs{i}")) for i in range(2)])
            tmpo = mk.sb("tmpo", [128, 4, 64], F32, esA); b_tmpo = mk.buf("tmpo")
            acc_r = Rot([(mk.sb(f"acc{i}", [128, 8, 64], F32, esA), mk.buf(f"acc{i}")) for i in range(2)])
            abf_r = Rot([(mk.sb(f"abf{i}", [128, 512], BF16, esA), mk.buf(f"abf{i}")) for i in range(2)])
            cst = mk.sb("cst", [30, 512], F32, esA); b_cst = mk.buf("cst")

            print("phaseA sbuf remaining", nc.sbuf_bytes_remaining)

            def selection_tile(i, tb, qT, b_qT, nmT, b_nmT):
                for g in range(2):
                    psS, bS = getdbl()
                    for hh in range(4):
                        mk.op("pe", lambda h, hh=hh, g=g: h.matmul(psS[:, hh * 256:(hh + 1) * 256], lhsT=qT[:, g, hh, tb * 128:(tb + 1) * 128],
                                                                rhs=KCT[:, :], start=True, stop=True),
                              reads=[b_qT, b_KCT], writes=[bS[hh // 2]], sig=(hh % 2 == 1), add=(hh % 2 == 1))
                    mk.op("act", lambda h: h.activation(out=eS[:].rearrange("p a b -> p (a b)"), in_=psS[:], func=AF.Exp, scale=SCALE),
                          reads=bS, writes=[b_eS])
                    yield
                    chk(61)
                    mk.op("pool", lambda h: h.affine_select(out=eS[:], in_=eS[:], pattern=[[0, 4], [-16, 256]], compare_op=ALU.is_ge, fill=0.0,
                                                             base=128 * i - 15, channel_multiplier=1), reads=[b_eS], writes=[b_eS])
                    yield
                    mk.op("pool", lambda h: h.memset(eS[:, :, 0:1], 0.0), reads=[b_eS], writes=[b_eS])
                    yield
                    chk(62)
                    mk.op("dve", lambda h: h.tensor_reduce(out=sst[:, 0:4], in_=eS[:], axis=AX.X, op=ALU.add), reads=[b_eS], writes=[b_sst])
                    yield
                    mk.op("dve", lambda h: h.tensor_scalar_max(out=sst[:, 4:8], in0=sst[:, 0:4], scalar1=1e-30), reads=[b_sst], writes=[b_sst])
                    yield
                    mk.op("dve", lambda h: h.reciprocal(out=sst[:, 8:12], in_=sst[:, 4:8]), reads=[b_sst], writes=[b_sst])
                    yield
                    mk.op("dve", lambda h, g=g: h.tensor_scalar(out=imp[:, g, 0:256], in0=eS[:, 0, :], scalar1=sst[:, 8:9], scalar2=None, op0=ALU.mult),
                          reads=[b_eS, b_sst], writes=[b_imp])
                    yield
                    for hh in range(1, 4):
                        mk.op("dve", lambda h, g=g, hh=hh: h.scalar_tensor_tensor(out=imp[:, g, 0:256], in0=eS[:, hh, :], scalar=sst[:, 8 + hh:9 + hh],
                                                                               in1=imp[:, g, 0:256], op0=ALU.mult, op1=ALU.add),
                              reads=[b_eS, b_sst, b_imp], writes=[b_imp])
                        yield
                    v4 = imp[:, g, 0:256].rearrange("p (j r) -> p j r", r=4)
                    v4b = imp[:, g, 4:260].rearrange("p (j r) -> p j r", r=4)
                    mk.op("dve", lambda h, v4=v4: h.tensor_reduce(out=sA[:], in_=v4[:, :, 1:4], axis=AX.X, op=ALU.add), reads=[b_imp], writes=[b_sA])
                    yield
                    mk.op("dve", lambda h, g=g, v4=v4: h.scalar_tensor_tensor(out=sc[:, g, :], in0=sA[:], scalar=2.0, in1=v4[:, :, 0], op0=ALU.mult, op1=ALU.add),
                          reads=[b_sA, b_imp], writes=[b_sc])
                    yield
                    mk.op("dve", lambda h, g=g, v4b=v4b: h.tensor_tensor(out=sc[:, g, :], in0=sc[:, g, :], in1=v4b[:, :, 0], op=ALU.add),
                          reads=[b_sc, b_imp], writes=[b_sc])
                    yield
                chk(63)
                mk.op("dve", lambda h: h.memset(sc[:, :, 0:1], 1e9), reads=[b_sc], writes=[b_sc])
                yield
                for half in range(2):
                    cur = 2 * i + half
                    rows = slice(half * 64, (half + 1) * 64)
                    if cur < 63:
                        mk.op("dve", lambda h, cur=cur, rows=rows: h.memset(sc[rows, :, cur + 1:64], -2e9), reads=[b_sc], writes=[b_sc])
                        yield
                    if cur >= 1:
                        mk.op("dve", lambda h, cur=cur, rows=rows: h.memset(sc[rows, :, cur - 1:cur], 2e9), reads=[b_sc], writes=[b_sc])
                        yield
                    mk.op("dve", lambda h, cur=cur, rows=rows: h.memset(sc[rows, :, cur:cur + 1], 3e9), reads=[b_sc], writes=[b_sc])
                    yield
                chk(64)
                for g in range(2):
                    mk.op("dve", lambda h, g=g: h.max(out=m8[:, 0:8], in_=sc[:, g, :]), reads=[b_sc], writes=[b_m8])
                    yield
                    mk.op("dve", lambda h, g=g: h.match_replace(out=sc2[:, g, :], in_to_replace=m8[:, 0:8], in_values=sc[:, g, :], imm_value=-4e9),
                          reads=[b_sc, b_m8], writes=[b_sc2])
                    yield
                    mk.op("dve", lambda h, g=g: h.max(out=m8[:, 8:16], in_=sc2[:, g, :]), reads=[b_sc2], writes=[b_m8])
                    yield
                    mk.op("dve", lambda h, g=g: h.match_replace(out=sc2[:, g, :], in_to_replace=m8[:, 8:16], in_values=sc2[:, g, :], imm_value=-4e9),
                          reads=[b_sc2, b_m8], writes=[b_sc2])
                    yield
                mk.op("dve", lambda h: h.tensor_scalar(out=nm[:], in0=sc2[:], scalar1=-3.5e9, scalar2=NEG, op0=ALU.is_gt, op1=ALU.mult),
                      reads=[b_sc2], writes=[b_nm])
                yield
                chk(65)
                pt, bpt = PTR["r"].next()
                mk.op("pe", lambda h: h.transpose(out=pt[:, 0:128], in_=nm[:].rearrange("p a b -> p (a b)"), identity=ident[:]),
                      reads=[b_nm, b_ident], writes=[bpt])
                yield
                evac(nmT[0:64, 0, :], pt[0:64, 0:128], [bpt], [b_nmT], add=True)
                yield
                evac(nmT[64:128, 1, :], pt[64:128, 0:128], [bpt], [b_nmT], add=True)
                yield


            def pump(bg, n):
                while n > 0 and bg:
                    try:
                        next(bg[0])
                        n -= 1
                    except StopIteration:
                        bg.pop(0)

            def drain(bg):
                while bg:
                    pump(bg, 1000)

            def attention_units(i, tb, qT, b_qT, sg, b_sg, catT, b_catT, nmT, b_nmT, bg=None, bg_steps=0, bg2=None):
                acc, b_acc = acc_r.next()
                for g in range(2):
                    if g == 1:
                        chk(70)
                    units = []
                    chs = [0] if i < 16 else [0, 1]
                    for n_, ch in enumerate(chs):
                        msk = None
                        if not (ch == 0 and i >= 16):
                            msk = dict(pattern=[[0, 4], [1, 128]], base=128 * i - 15 - 2048 * ch, cm=-16)
                        units.append(dict(br=0, lhsT=KCT[:, ch * 128:(ch + 1) * 128], rk=[b_KCT], V=VC[:, ch, g, :], rv=[b_VC],
                                          emask=None, mask=msk, first=(n_ == 0), last=(n_ == len(chs) - 1)))
                    for kc in range(i + 1):
                        msk = dict(pattern=[[0, 4], [1, 128]], base=0, cm=-1) if kc == i else None
                        units.append(dict(br=1, lhsT=KTs[:, kc * 128:(kc + 1) * 128], rk=[b_KTs[kc // 4]], V=Vs[:, kc, g, :], rv=[b_Vs[kc // 4]],
                                          emask=kc, mask=msk, first=(kc == 0), last=(kc == i)))
                    k0 = max(0, i - 4)
                    for kc in range(k0, i + 1):
                        msk = None
                        if kc == i:
                            msk = dict(pattern=[[0, 4], [1, 128]], base=0, cm=-1)
                        elif kc == i - 4:
                            msk = dict(pattern=[[0, 4], [-1, 128]], base=-1, cm=1)
                        slot = (kc // 4) % 2
                        units.append(dict(br=2, lhsT=KTw[:, slot, (kc % 4) * 128:(kc % 4 + 1) * 128], rk=[b_KTw[slot]],
                                          V=Vw[:, kc % 8, g, :], rv=[b_Vw[slot]], emask=None, mask=msk, first=(kc == k0), last=(kc == i)))
                    rhs_q = qT[:, g, :, tb * 128:(tb + 1) * 128]
                    state = {}

                    def emit_S(u):
                        ps_, bps_ = S_r.next()
                        u["ps"] = ps_; u["bps"] = bps_
                        has_e = u["emask"] is not None
                        mk.op("pe", lambda h: h.matmul(ps_.rearrange("p (a b) -> p a b", b=128), lhsT=u["lhsT"], rhs=rhs_q, start=True, stop=not has_e),
                              reads=[b_qT] + u["rk"], writes=[bps_], sig=not has_e)
                        if has_e:
                            kc_ = u["emask"]
                            mk.op("pe", lambda h: h.matmul(ps_.rearrange("p (a b) -> p a b", b=128), lhsT=Eall[:, kc_, :],
                                                           rhs=nmT[:, g, :].unsqueeze(1).to_broadcast([128, 4, 128]), start=False, stop=True),
                                  reads=[b_Eall, b_nmT], writes=[bps_], sig=True, add=True)

                    def emit_rest(u):
                        PT, b_PT = PT_r.next()
                        mk.op("act", lambda h: h.activation(out=PT[:].rearrange("p a b -> p (a b)"), in_=u["ps"], func=AF.Exp, scale=SCALE),
                              reads=[u["bps"]], writes=[b_PT])
                        if u["mask"] is not None:
                            m_ = u["mask"]
                            mk.op("pool", lambda h: h.affine_select(out=PT[:], in_=PT[:], pattern=m_["pattern"], compare_op=ALU.is_ge, fill=0.0,
                                                                     base=m_["base"], channel_multiplier=m_["cm"]), reads=[b_PT], writes=[b_PT])
                        if u["first"]:
                            state["po"], state["bpo"] = po_r.next()
                        po, bpo = state["po"], state["bpo"]
                        mk.op("pe", lambda h: h.matmul(po[0:65, :], lhsT=u["V"], rhs=PT[:].rearrange("p a b -> p (a b)"), start=u["first"], stop=u["last"]),
                              reads=[b_PT] + u["rv"], writes=[bpo], sig=u["last"], add=not u["first"])
                        chk(68)
                        if u["last"]:
                            finalize(u["br"], po, bpo)
                            chk(69)

                    def finalize(br, po, bpo):
                        ot, b_ot = ot_r.next()
                        evac(ot[:], po[0:65, :], [bpo], [b_ot])
                        pq, bpq = pq_r.next()
                        for hh in range(4):
                            mk.op("pe", lambda h, hh=hh: h.transpose(out=pq[:, hh * 65:(hh + 1) * 65], in_=ot[:, hh * 128:(hh + 1) * 128], identity=identf[0:65, 0:65]),
                                  reads=[b_ot, b_identf], writes=[bpq], sig=(hh == 3), add=(hh > 0))
                        chk(691)
                        pvs, bpq_s = pvs_r.next()
                        mk.op("act", lambda h: h.copy(out=pvs[:, 0:260], in_=pq[:, 0:260]), reads=[bpq], writes=[bpq_s])
                        bpq = bpq_s
                        pv = pvs[:, 0:260].rearrange("p (a c) -> p a c", c=65)
                        if br == 0:
                            mk.op("dve", lambda h: h.tensor_scalar_max(out=fin[:, 0:4], in0=pv[:, :, 64], scalar1=1e-30), reads=[bpq], writes=[b_fin])
                            mk.op("dve", lambda h: h.reciprocal(out=fin[:, 4:8], in_=fin[:, 0:4]), reads=[b_fin], writes=[b_fin])
                        else:
                            mk.op("dve", lambda h: h.reciprocal(out=fin[:, 4:8], in_=pv[:, :, 64]), reads=[bpq], writes=[b_fin])
                        chk(692)
                        gsl = sg[:, tb, g * 12:(g + 1) * 12].rearrange("p (a c) -> p a c", c=3)[:, :, br]
                        mk.op("dve", lambda h: h.tensor_tensor(out=fin[:, 4:8], in0=fin[:, 4:8], in1=gsl, op=ALU.mult), reads=[b_fin, b_sg], writes=[b_fin])
                        chk(693)
                        fb = fin[:, 4:8].unsqueeze(2).to_broadcast([128, 4, 64])
                        if br == 0:
                            mk.op("dve", lambda h: h.tensor_tensor(out=acc[:, g * 4:(g + 1) * 4, :], in0=pv[:, :, 0:64], in1=fb, op=ALU.mult),
                                  reads=[bpq, b_fin], writes=[b_acc], add=(g == 1))
                        else:
                            mk.op("dve", lambda h: h.tensor_tensor(out=tmpo[:], in0=pv[:, :, 0:64], in1=fb, op=ALU.mult), reads=[bpq, b_fin], writes=[b_tmpo])
                            mk.op("dve", lambda h: h.tensor_tensor(out=acc[:, g * 4:(g + 1) * 4, :], in0=acc[:, g * 4:(g + 1) * 4, :], in1=tmpo[:], op=ALU.add),
                                  reads=[b_acc, b_tmpo], writes=[b_acc])

                    LOOK = 1
                    for k in range(min(LOOK, len(units))):
                        emit_S(units[k])
                    chk(67)
                    if g == 1:
                        chk(71)
                    per = -(-bg_steps // max(1, 2 * len(units))) if bg else 0
                    for k, u in enumerate(units):
                        if k + LOOK < len(units):
                            emit_S(units[k + LOOK])
                        emit_rest(u)
                        if bg:
                            pump(bg, per)
                        if bg2 and k % 3 == 2:
                            pump(bg2, 1)
                chk(72)
                abf, b_abf = abf_r.next()
                mk.op("act", lambda h: h.copy(out=abf[:], in_=acc[:].rearrange("p a b -> p (a b)")), reads=[b_acc], writes=[b_abf])
                transpose_to(abf[:], b_abf, catT[:, 0:4, tb * 128:(tb + 1) * 128], b_catT, 4)

            def prep_gen(nb, hT, b_hT):
                prev = None
                for tb in range(4):
                    xt, b_xt = xt_r.next()
                    r0 = nb * TB + tb * 128
                    mk.dma("sp", lambda h: h.dma_start(out=xt[:], in_=xp[r0:r0 + 128, :]), writes=[b_xt])
                    hb, b_hb = hb_r.next()
                    st, b_st = st_r.next()
                    rmsnorm_tile(xt[:], 128, gbA[:, 0, :], [b_xt, b_gbA], (st, b_st), out_bf=hb[:], b_bf=b_hb)
                    yield
                    if prev is not None:
                        transpose_to(prev[0][:], prev[1], hT[:, :, prev[2] * 128:(prev[2] + 1) * 128], b_hT, 8)
                        yield
                    prev = (hb, b_hb, tb)
                transpose_to(prev[0][:], prev[1], hT[:, :, prev[2] * 128:(prev[2] + 1) * 128], b_hT, 8)
                yield

            for blk in range(NBLK):
                t0 = blk * TB
                hT, b_hT = hT_r.next()
                if blk == 0:
                    drain([prep_gen(0, hT, b_hT)])
                chk(2)
                qT, b_qT = qT_r.next()

                def proj_fm(col0, M, dst, dst_buf, add=True, rows=None):
                    pb, bpb = getbank()
                    for kc in range(8):
                        mk.op("pe", lambda h, kc=kc: h.matmul(pb[0:M, :], lhsT=Win[:, kc, col0:col0 + M], rhs=hT[:, kc, :], start=(kc == 0), stop=(kc == 7)),
                              reads=[b_Win, b_hT], writes=[bpb], sig=(kc == 7), add=(kc > 0))
                    evac(dst, pb[0:M, :] if rows is None else pb[rows, :], [bpb], [dst_buf], add=add)
                    return pb, bpb

                for hh in range(4):
                    pbq, bpbq = proj_fm(hh * 128, 128, qT[0:64, 0, hh, :], b_qT, rows=slice(0, 64))
                    evac(qT[64:128, 1, hh, :], pbq[64:128, :], [bpbq], [b_qT], add=True)
                proj_fm(768, 128, KTs[:, t0:t0 + TB], b_KTs[blk])
                proj_fm(1024, 128, KTw[:, blk % 2, :], b_KTw[blk % 2])
                proj_fm(512, 128, CKV[:, 0, 16:16 + TB], b_CKV)
                proj_fm(512 + 128, 128, CKV[:, 1, 16:16 + TB], b_CKV)
                sg, b_sg = sg_r.next()
                for tb in range(4):
                    pz, bz = getdbl()
                    for (c0, n, half) in ((512, 512, 0), (1024, 280, 1)):
                        for kc in range(8):
                            mk.op("pe", lambda h, kc=kc, c0=c0, n=n, half=half, tb=tb: h.matmul(
                                pz[:, half * 512:half * 512 + n], lhsT=hT[:, kc, tb * 128:(tb + 1) * 128], rhs=Win[:, kc, c0:c0 + n],
                                start=(kc == 0), stop=(kc == 7)), reads=[b_Win, b_hT], writes=[bz[half]], sig=(kc == 7), add=(kc > 0))
                    zkv, b_zkv = zkv_r.next()
                    evac(zkv[:], pz[:, 0:792], bz, [b_zkv])
                    ch = blk * 4 + tb
                    mk.op("dve", lambda h, ch=ch, zkv=zkv: h.tensor_copy(out=Vs[:, ch, :, 0:64], in_=zkv[:, 384:512].rearrange("p (g d) -> p g d", d=64)),
                          reads=[b_zkv], writes=[b_Vs[blk]], add=True)
                    mk.op("pool", lambda h, ch=ch, zkv=zkv: h.tensor_copy(out=Vw[:, ch % 8, :, 0:64], in_=zkv[:, 640:768].rearrange("p (g d) -> p g d", d=64)),
                          reads=[b_zkv], writes=[b_Vw[blk % 2]], add=True)
                    mk.op("act", lambda h, tb=tb, zkv=zkv: h.activation(out=sg[:, tb, :], in_=zkv[:, 768:792], func=AF.Sigmoid), reads=[b_zkv], writes=[b_sg], add=True)
                    r0 = t0 + tb * 128
                    mk.dma("sp", lambda h, r0=r0, zkv=zkv: h.dma_start(out=o_cmp_p[r0:r0 + 128, :], in_=zkv[:, 0:256]), reads=[b_zkv], is_out=True)
                    mk.dma("sp", lambda h, r0=r0, zkv=zkv: h.dma_start(out=o_slc_p[r0:r0 + 128, :], in_=zkv[:, 256:512]), reads=[b_zkv], is_out=True)
                    if blk == NBLK - 1:
                        mk.dma("sp", lambda h, tb=tb, zkv=zkv: h.dma_start(out=o_win_p[tb * 128:(tb + 1) * 128, :], in_=zkv[:, 512:768]), reads=[b_zkv], is_out=True)
                chk(3)
                for j in range(2):
                    pb, bpb = getbank()
                    for l in range(32):
                        mk.op("pe", lambda h, j=j, l=l: h.matmul(pb[:, 0:32], lhsT=W1[j][:, l, :], rhs=CKV[:, j, l:l + 497:16], start=(l == 0), stop=(l == 31)),
                              reads=[b_W1, b_CKV], writes=[bpb], sig=(l == 31), add=(l > 0))
                    if j == 0:
                        mk.op("act", lambda h, pb=pb: h.activation(out=gk[:], in_=pb[:, 0:32], func=AF.Gelu, bias=postt[:, 0:1]), reads=[bpb, b_post], writes=[b_gk])
                        pb2, bpb2 = getbank()
                        mk.op("pe", lambda h, pb2=pb2: h.matmul(pb2[:, 0:32], lhsT=W2[0][:], rhs=gk[:], start=True, stop=True), reads=[b_W2, b_gk], writes=[bpb2])
                        evac(KCT[:, 32 * blk:32 * blk + 32], pb2[:, 0:32], [bpb2], [b_KCT])
                    else:
                        mk.op("act", lambda h, pb=pb: h.activation(out=GV[:, 32 * blk:32 * blk + 32], in_=pb[:, 0:32], func=AF.Gelu, bias=postt[:, 1:2]),
                              reads=[bpb, b_post], writes=[b_GV])
                        ch = blk // 4
                        if blk == 0:
                            mk.op("dve", lambda h: h.memset(GV[:, 0:1], 0.0), reads=[b_GV], writes=[b_GV])
                        pb2, bpb2 = getbank()
                        mk.op("pe", lambda h, pb2=pb2, ch=ch: h.matmul(pb2[:, 0:128], lhsT=GV[:, ch * 128:(ch + 1) * 128], rhs=W2[1][:], start=True, stop=True),
                              reads=[b_W2, b_GV], writes=[bpb2])
                        evac(VC[:, ch, :, 0:64], pb2[:, 0:128].rearrange("p (g d) -> p g d", d=64), [bpb2], [b_VC])
                if blk == 0:
                    mk.op("dve", lambda h: h.memset(KCT[:, 0:1], 0.0), reads=[b_KCT], writes=[b_KCT])
                mk.op("dve", lambda h: h.tensor_copy(out=CKV[:, :, 0:16], in_=CKV[:, :, TB:TB + 16]), reads=[b_CKV], writes=[b_CKV])
                chk(4)
                catT, b_catT = catT_r.next()

                def conv_glu(c):
                    pa, bpa = getbank()
                    pg_, bpg = getbank()
                    for (pb_, bb_, col0) in ((pa, bpa, 1304 + c * 128), (pg_, bpg, 1304 + 512 + c * 128)):
                        for kc in range(8):
                            mk.op("pe", lambda h, kc=kc, pb_=pb_, col0=col0: h.matmul(pb_[:, :], lhsT=Win[:, kc, col0:col0 + 128], rhs=hT[:, kc, :],
                                                                                    start=(kc == 0), stop=(kc == 7)),
                                  reads=[b_Win, b_hT], writes=[bb_], sig=(kc == 7), add=(kc > 0))
                    tmpc, b_tmpc = tmp_r.next()
                    mk.op("act", lambda h: h.activation(out=tmpc[:], in_=pg_[:, :], func=AF.Sigmoid), reads=[bpg], writes=[b_tmpc])
                    mk.op("dve", lambda h: h.tensor_tensor(out=aT[:, c, 30:30 + TB], in0=pa[:, :], in1=tmpc[:], op=ALU.mult),
                          reads=[bpa, b_tmpc], writes=[b_aT[c]])

                def conv_taps(c):
                    mk.op("dve", lambda h: h.tensor_scalar(out=yc[:, c, :], in0=aT[:, c, 0:TB], scalar1=cw[:, c, 0:1], scalar2=cvec[:, 0, c:c + 1],
                                                           op0=ALU.mult, op1=ALU.add), reads=[b_aT[c], b_cw, b_cvec], writes=[b_yc[c]])
                    yield
                    for k in range(1, 31):
                        mk.op("dve", lambda h, k=k: h.scalar_tensor_tensor(out=yc[:, c, :], in0=aT[:, c, k:k + TB], scalar=cw[:, c, k:k + 1], in1=yc[:, c, :],
                                                                        op0=ALU.mult, op1=ALU.add), reads=[b_aT[c], b_cw, b_yc[c]], writes=[b_yc[c]])
                        yield

                def conv_finish():
                    if blk == NBLK - 1:
                        pq, bpq = getbank()
                        for c in range(4):
                            mk.op("pe", lambda h, c=c: h.transpose(out=pq[0:30, c * 128:(c + 1) * 128], in_=aT[:, c, TB:TB + 30], identity=identf[:]),
                                  reads=[b_aT[c], b_identf], writes=[bpq], sig=(c == 3), add=(c > 0))
                        evac(cst[:], pq[0:30, :], [bpq], [b_cst])
                        mk.dma("sp", lambda h: h.dma_start(out=o_conv_p, in_=cst[:]), reads=[b_cst], is_out=True)
                    for c in range(4):
                        mk.op("pool", lambda h, c=c: h.tensor_copy(out=aT[:, c, 0:30], in_=aT[:, c, TB:TB + 30]), reads=[b_aT[c]], writes=[b_aT[c]])
                    p1, bp1 = getbank()
                    for c in range(4):
                        mk.op("pe", lambda h, c=c: h.matmul(p1[:, :], lhsT=onesf[:], rhs=yc[:, c, :], start=(c == 0), stop=(c == 3)),
                              reads=[b_onesf, b_yc[c]], writes=[bp1], sig=(c == 3), add=(c > 0))
                    p2, bp2 = getbank()
                    for c in range(4):
                        tmpc, b_tmpc = tmp_r.next()
                        mk.op("act", lambda h, c=c, tmpc=tmpc: h.activation(out=tmpc[:], in_=yc[:, c, :], func=AF.Square), reads=[b_yc[c]], writes=[b_tmpc])
                        mk.op("pe", lambda h, c=c, tmpc=tmpc: h.matmul(p2[:, :], lhsT=onesf[:], rhs=tmpc[:], start=(c == 0), stop=(c == 3)),
                              reads=[b_onesf, b_tmpc], writes=[bp2], sig=True, add=(c > 0))
                    mk.op("act", lambda h: h.mul(out=mean_sb[:], in_=p1[:, :], mul=1.0 / 512), reads=[bp1], writes=[b_mean])
                    tmpc, b_tmpc = tmp_r.next()
                    mk.op("dve", lambda h: h.tensor_tensor(out=tmpc[:], in0=mean_sb[:], in1=mean_sb[:], op=ALU.mult), reads=[b_mean], writes=[b_tmpc])
                    mk.op("dve", lambda h: h.scalar_tensor_tensor(out=rstd_sb[:], in0=p2[:, :], scalar=1.0 / 512, in1=tmpc[:], op0=ALU.mult, op1=ALU.subtract),
                          reads=[bp2, b_tmpc], writes=[b_rstd])
                    mk.op("act", lambda h: h.activation(out=rstd_sb[:], in_=rstd_sb[:], func=AF.Sqrt, bias=EPS), reads=[b_rstd], writes=[b_rstd])
                    mk.op("dve", lambda h: h.reciprocal(out=rstd_sb[:], in_=rstd_sb[:]), reads=[b_rstd], writes=[b_rstd])
                    for c in range(4):
                        mk.op("dve", lambda h, c=c: h.tensor_tensor(out=yc[:, c, :], in0=yc[:, c, :], in1=mean_sb[:], op=ALU.subtract), reads=[b_yc[c], b_mean], writes=[b_yc[c]])
                        mk.op("dve", lambda h, c=c: h.tensor_tensor(out=yc[:, c, :], in0=yc[:, c, :], in1=rstd_sb[:], op=ALU.mult), reads=[b_yc[c], b_rstd], writes=[b_yc[c]])
                        mk.op("act", lambda h, c=c: h.activation(out=catT[:, 4 + c, :], in_=yc[:, c, :], func=AF.Silu, scale=cvec[:, 1, c:c + 1], bias=cvec[:, 2, c:c + 1]),
                              reads=[b_yc[c], b_cvec], writes=[b_catT], add=True)

                chk(5)
                nm_cur = nmT_r.next()
                drain([selection_tile(blk * 4, 0, qT, b_qT, nm_cur[0], nm_cur[1])])
                for c in range(4):
                    conv_glu(c)
                bg2 = [prep_gen(blk + 1, hT, b_hT)] if blk + 1 < NBLK else []
                for tb in range(4):
                    bg = []
                    nm_next = None
                    steps = 31
                    if tb < 3:
                        nm_next = nmT_r.next()
                        bg.append(selection_tile(blk * 4 + tb + 1, tb + 1, qT, b_qT, nm_next[0], nm_next[1]))
                        steps += 75
                    bg.append(conv_taps(tb))
                    attention_units(blk * 4 + tb, tb, qT, b_qT, sg, b_sg, catT, b_catT, nm_cur[0], nm_cur[1], bg, steps, bg2)
                    drain(bg)
                    nm_cur = nm_next
                    chk(6)
                drain(bg2)
                conv_finish()
                chk(73)
                for tb in range(4):
                    po, bo = getdbl()
                    for half in range(2):
                        for kc in range(8):
                            mk.op("pe", lambda h, kc=kc, half=half, tb=tb: h.matmul(po[:, half * 512:(half + 1) * 512], lhsT=catT[:, kc, tb * 128:(tb + 1) * 128],
                                                                                   rhs=Wout[:, kc, half * 512:(half + 1) * 512], start=(kc == 0), stop=(kc == 7)),
                                  reads=[b_Wout, b_catT], writes=[bo[half]], sig=(kc == 7), add=(kc > 0))
                    xt, b_xt = xt_r.next()
                    r0 = t0 + tb * 128
                    mk.dma("sp", lambda h, r0=r0, xt=xt: h.dma_start(out=xt[:], in_=xp[r0:r0 + 128, :]), writes=[b_xt])
                    mk.op("dve", lambda h, xt=xt, po=po: h.tensor_tensor(out=xt[:], in0=xt[:], in1=po[:], op=ALU.add), reads=[b_xt] + bo, writes=[b_xt])
                    mk.dma("sp", lambda h, r0=r0, xt=xt: h.dma_start(out=xs1[r0:r0 + 128, :], in_=xt[:]), reads=[b_xt], writes=[bxs1[blk]], add=True)
                chk(7)
                chk(100 + blk)
        mk.barrier()
        chk(8)

        FB = 256
        esB = ExitStack()
        with esB:
            PTR["r"] = setup_psum(esB, 6)
            Wg = mk.sb("Wg", [128, 8, DFF], BF16, esB); Wu = mk.sb("Wu", [128, 8, DFF], BF16, esB); Wd = mk.sb("Wd", [128, NF, D], BF16, esB)
            b_Wg = [mk.buf(f"Wg{i}") for i in range(8)]; b_Wu = [mk.buf(f"Wu{i}") for i in range(8)]; b_Wd = [mk.buf(f"Wd{i}") for i in range(NF)]

            def load_ffn(l):
                for kc in range(8):
                    mk.dma("pool", lambda h, kc=kc: h.dma_start(out=Wg[:, kc, :], in_=wg_d[l, kc * 128:(kc + 1) * 128, :]), writes=[b_Wg[kc]])
                    mk.dma("pool", lambda h, kc=kc: h.dma_start(out=Wu[:, kc, :], in_=wu_d[l, kc * 128:(kc + 1) * 128, :]), writes=[b_Wu[kc]])
                for f in range(NF):
                    mk.dma("pool", lambda h, f=f: h.dma_start(out=Wd[:, f, :], in_=wd_d[l, f * 128:(f + 1) * 128, :]), writes=[b_Wd[f]])

            gbB = mk.sb("gbB", [128, 2, D], F32, esB); b_gbB = mk.buf("gbB")
            PW = mk.sb("PW", [128, 4, 2, 256], BF16, esB); b_PW = mk.buf("PW")
            for g in range(4):
                mk.dma("pool", lambda h, g=g: h.dma_start(out=PW[:, g, :, :], in_=pool_w[g].rearrange("(j p) e -> p j e", p=128)), writes=[b_PW], add=(g > 0))
            band = mk.sb("band", [128, 20, 128], F32, esB); b_band = mk.buf("band")
            mk.dma("sp", lambda h: h.dma_start(out=band[:], in_=c_band.rearrange("n p t -> p n t")), writes=[b_band])
            psb = mk.sb("psb", [128, D], F32, esB); b_psb = mk.buf("psb")
            mk.dma("sp", lambda h: h.dma_start(out=psb[:], in_=pool_scale[0:1, :].partition_broadcast(128)), writes=[b_psb])

            xt_r = Rot([(mk.sb(f"xB{i}", [128, D], F32, esB), mk.buf(f"xB{i}")) for i in range(4)])
            hb_rB = Rot([(mk.sb(f"hbB{i}", [128, D], BF16, esB), mk.buf(f"hbB{i}")) for i in range(2)])
            st_r = Rot([(mk.sb(f"stB{i}", [128, 4], F32, esB), mk.buf(f"stB{i}")) for i in range(2)])
            hTB = mk.sb("hTB", [128, 8, FB], BF16, esB); b_hTB = mk.buf("hTB")
            hid = mk.sb("hid", [128, NF, FB], BF16, esB); b_hid = [mk.buf(f"hid{f}") for f in range(NF)]
            sil_r = Rot([(mk.sb(f"sil{i}", [128, FB], F32, esB), mk.buf(f"sil{i}")) for i in range(1)])
            h3_r = Rot([(mk.sb(f"h3_{i}", [128, D], F32, esB), mk.buf(f"h3_{i}")) for i in range(2)])
            zT = mk.sb("zT", [128, 8, 128], BF16, esB); b_zT = mk.buf("zT")
            tmpx = mk.sb("tmpx", [128, 512], F32, esB); b_tmpx = mk.buf("tmpx")

            print("phaseB sbuf remaining", nc.sbuf_bytes_remaining)

            def ffn_prep_norm(tiles, npart, gslot):
                hbs = []
                for ti, (x_ap, b_x) in enumerate(tiles):
                    st = st_r.next()
                    hb, b_hb = hb_rB.next()
                    rmsnorm_tile(x_ap, npart, gbB[0:npart, gslot, :], [b_x, b_gbB], st, out_bf=hb[0:npart, :], b_bf=b_hb)
                    hbs.append((hb, b_hb))
                return hbs

            def ffn_prep_T(ti, hb, b_hb, npart):
                transpose_to(hb[0:npart, :], b_hb, hTB[:, :, ti * 128:ti * 128 + npart], b_hTB, 8, npart)

            def ffn_mm(ntok):
                for f in range(NF):
                    pb_, bb_ = getbank()
                    for wi, (W_, bW_) in enumerate(((Wg, b_Wg), (Wu, b_Wu))):
                        for kc in range(8):
                            mk.op("pe", lambda h, kc=kc, W_=W_, f=f, wi=wi: h.matmul(pb_[:, wi * 256:wi * 256 + ntok], lhsT=W_[:, kc, f * 128:(f + 1) * 128],
                                                                                    rhs=hTB[:, kc, 0:ntok], start=(kc == 0), stop=(kc == 7)),
                                  reads=[bW_[kc], b_hTB], writes=[bb_], sig=(kc == 7), add=(wi + kc > 0))
                    sil, b_sil = sil_r.next()
                    mk.op("act", lambda h: h.activation(out=sil[:, 0:ntok], in_=pb_[:, 0:ntok], func=AF.Silu), reads=[bb_], writes=[b_sil])
                    mk.op("dve", lambda h: h.tensor_tensor(out=hid[:, f, 0:ntok], in0=pb_[:, 256:256 + ntok], in1=sil[:, 0:ntok], op=ALU.mult),
                          reads=[bb_, b_sil], writes=[b_hid[f]])

            def ffn_gateup(tiles, npart, gslot):
                ntok = len(tiles) * 128 if npart == 128 else npart
                hbs = ffn_prep_norm(tiles, npart, gslot)
                for ti, (hb, b_hb) in enumerate(hbs):
                    ffn_prep_T(ti, hb, b_hb, npart)
                ffn_mm(ntok)

            def ffn_down(ti, x_ap, b_x, npart):
                po, bo = getdbl()
                for half in range(2):
                    for f in range(NF):
                        mk.op("pe", lambda h, f=f, half=half: h.matmul(po[0:npart, half * 512:(half + 1) * 512], lhsT=hid[:, f, ti * 128:ti * 128 + npart],
                                                                     rhs=Wd[:, f, half * 512:(half + 1) * 512], start=(f == 0), stop=(f == NF - 1)),
                              reads=[b_Wd[f], b_hid[f]], writes=[bo[half]], sig=(f == NF - 1), add=(f > 0))
                mk.op("dve", lambda h: h.tensor_tensor(out=x_ap, in0=x_ap, in1=po[0:npart, :], op=ALU.add), reads=[b_x] + bo, writes=[b_x])

            def pool_tile(x_ap, b_x, npart, h3, b_h3, h3p_ap, b_h3p, kp, bands):
                pz, bz = getdbl()
                for c in range(8):
                    gidx = c // 2
                    mk.op("pe", lambda h, c=c, gidx=gidx: h.matmul(pz[:, c * 128:c * 128 + npart], lhsT=h3[0:npart, c * 128:(c + 1) * 128],
                                                                   rhs=band[0:npart, bands[0] + gidx, 0:npart], start=True, stop=(h3p_ap is None)),
                          reads=[b_h3, b_band], writes=[bz[c // 4]], sig=(h3p_ap is None and c % 4 == 3), add=(c % 4 > 0))
                    if h3p_ap is not None:
                        mk.op("pe", lambda h, c=c, gidx=gidx: h.matmul(pz[:, c * 128:c * 128 + npart], lhsT=h3p_ap[:, c * 128:(c + 1) * 128],
                                                                       rhs=band[128 - kp:128, bands[1] + gidx, 0:npart] if kp == 128 else band[0:kp, bands[1] + gidx, 0:npart],
                                                                       start=False, stop=True),
                              reads=[b_h3p, b_band], writes=[bz[c // 4]], sig=(c % 4 == 3), add=True)
                src = pz[:].rearrange("p (c t) -> p c t", t=128)[:, :, 0:npart]
                evac(zT[:, :, 0:npart], src, bz, [b_zT])
                py, by = getdbl()
                for g in range(4):
                    for j in range(2):
                        mk.op("pe", lambda h, g=g, j=j: h.matmul(py[0:npart, g * 256:(g + 1) * 256], lhsT=zT[:, 2 * g + j, 0:npart], rhs=PW[:, g, j, :],
                                                                 start=(j == 0), stop=(j == 1)),
                              reads=[b_zT, b_PW], writes=[by[g // 2]], sig=(j == 1 and g % 2 == 1), add=(g % 2 == 1 or j == 1))
                for hf in range(2):
                    cs = slice(hf * 512, (hf + 1) * 512)
                    mk.op("dve", lambda h, cs=cs: h.tensor_tensor(out=tmpx[0:npart, :], in0=py[0:npart, cs], in1=psb[0:npart, cs], op=ALU.mult), reads=[by[hf], b_psb], writes=[b_tmpx])
                    mk.op("dve", lambda h, cs=cs: h.tensor_tensor(out=x_ap[:, cs], in0=x_ap[:, cs], in1=tmpx[0:npart, :], op=ALU.add), reads=[b_x, b_tmpx], writes=[b_x])

            load_ffn(0)
            load_gain(gbB, b_gbB, 0, 1)
            load_gain(gbB, b_gbB, 1, 2)
            if do_samples:
                h3s, b_h3s = h3_r.next()
                SPt, b_SPt = xt_r.next()
                mk.dma("sp", lambda h: h.dma_start(out=SPt[0:60, :], in_=spool_d.rearrange("s k d -> (s k) d")), writes=[b_SPt])
                xsB, b_x_s = xt_r.next()
                x_s = xsB[0:NS, :]
                mk.dma("sp", lambda h: h.dma_start(out=x_s, in_=xs_scr), reads=[b_xs_scr], writes=[b_x_s])
                ffn_gateup([(x_s, b_x_s)], NS, 0)
                ffn_down(0, x_s, b_x_s, NS)
                rmsnorm_tile(x_s, NS, gbB[0:NS, 1, :], [b_x_s, b_gbB], st_r.next(), out_f32=h3s[0:NS, :], b_f32=b_h3s)
                for s_ in range(NS):
                    mk.dma("sp", lambda h, s_=s_: h.dma_start(out=o_pool_s[s_, 14:15, :], in_=h3s[s_:s_ + 1, :]), reads=[b_h3s], is_out=True)
                pool_tile(x_s, b_x_s, NS, h3s, b_h3s, SPt[0:60, :], b_SPt, 60, (12, 16))
                mk.dma("sp", lambda h: h.dma_start(out=xs_scr, in_=x_s), reads=[b_x_s], writes=[b_xs_scr])
            h3prev = None
            NFB = T // FB

            def load_block(fb, src, bsrc):
                tiles = []
                for ti in range(2):
                    xt, b_xt = xt_r.next()
                    r0 = fb * FB + ti * 128
                    mk.dma("sp", lambda h, xt=xt, r0=r0: h.dma_start(out=xt[:], in_=src[r0:r0 + 128, :]), reads=[bsrc[r0 // TB]], writes=[b_xt])
                    tiles.append((xt, b_xt))
                return tiles

            tiles = load_block(0, xs1, bxs1)
            hbs = ffn_prep_norm([(xt[:], b_xt) for xt, b_xt in tiles], 128, 0)
            for ti, (hb, b_hb) in enumerate(hbs):
                ffn_prep_T(ti, hb, b_hb, 128)
            for fb in range(NFB):
                t0 = fb * FB
                nxt = None
                if fb + 1 < NFB:
                    nxt = load_block(fb + 1, xs1, bxs1)
                    nhbs = ffn_prep_norm([(xt[:], b_xt) for xt, b_xt in nxt], 128, 0)
                ffn_mm(FB)
                for ti, (xt, b_xt) in enumerate(tiles):
                    ffn_down(ti, xt[:], b_xt, 128)
                    if nxt is not None:
                        ffn_prep_T(ti, nhbs[ti][0], nhbs[ti][1], 128)
                    h3, b_h3 = h3_r.next()
                    st = st_r.next()
                    rmsnorm_tile(xt[:], 128, gbB[:, 1, :], [b_xt, b_gbB], st, out_f32=h3[:], b_f32=b_h3)
                    i = fb * 2 + ti
                    if i == 31:
                        mk.dma("sp", lambda h, h3=h3: h.dma_start(out=o_pool_p, in_=h3[113:128, :]), reads=[b_h3], is_out=True)
                    if i == 0:
                        pool_tile(xt[:], b_xt, 128, h3, b_h3, None, None, 0, (8, None))
                    else:
                        pool_tile(xt[:], b_xt, 128, h3, b_h3, h3prev[0][:], h3prev[1], 128, (0, 4))
                    h3prev = (h3, b_h3)
                    r0 = t0 + ti * 128
                    mk.dma("sp", lambda h, xt=xt, r0=r0: h.dma_start(out=xs3[r0:r0 + 128, :], in_=xt[:]), reads=[b_xt], writes=[bxs3[r0 // TB]], add=True)
                tiles = nxt
            chk(9)
            load_ffn(1)
            load_gain(gbB, b_gbB, 0, 3)
            load_gain(gbB, b_gbB, 1, 4)
            if do_samples:
                xsB, b_x_s = xt_r.next()
                x_s = xsB[0:NS, :]
                mk.dma("sp", lambda h: h.dma_start(out=x_s, in_=xs_scr), reads=[b_xs_scr], writes=[b_x_s])
                ffn_gateup([(x_s, b_x_s)], NS, 0)
                ffn_down(0, x_s, b_x_s, NS)
                h3s, b_h3s = h3_r.next()
                rmsnorm_tile(x_s, NS, gbB[0:NS, 1, :], [b_x_s, b_gbB], st_r.next(), out_f32=h3s[0:NS, :], b_f32=b_h3s)
                mk.dma("sp", lambda h: h.dma_start(out=y_s, in_=h3s[0:NS, :]), reads=[b_h3s], is_out=True)
            tiles = load_block(0, xs3, bxs3)
            hbs = ffn_prep_norm([(xt[:], b_xt) for xt, b_xt in tiles], 128, 0)
            for ti, (hb, b_hb) in enumerate(hbs):
                ffn_prep_T(ti, hb, b_hb, 128)
            for fb in range(NFB):
                t0 = fb * FB
                nxt = None
                if fb + 1 < NFB:
                    nxt = load_block(fb + 1, xs3, bxs3)
                    nhbs = ffn_prep_norm([(xt[:], b_xt) for xt, b_xt in nxt], 128, 0)
                ffn_mm(FB)
                for ti, (xt, b_xt) in enumerate(tiles):
                    ffn_down(ti, xt[:], b_xt, 128)
                    if nxt is not None:
                        ffn_prep_T(ti, nhbs[ti][0], nhbs[ti][1], 128)
                    h3, b_h3 = h3_r.next()
                    st = st_r.next()
                    rmsnorm_tile(xt[:], 128, gbB[:, 1, :], [b_xt, b_gbB], st, out_f32=h3[:], b_f32=b_h3)
                    r0 = t0 + ti * 128
                    mk.dma("sp", lambda h, r0=r0, h3=h3: h.dma_start(out=y_p[r0:r0 + 128, :], in_=h3[:]), reads=[b_h3], is_out=True)
                tiles = nxt


def _consts():
    ident = np.eye(128, dtype=np.float32)
    eall = np.zeros((64, 32, 128), np.float32)
    for c in range(32):
        eall[2 * c, c, 0:64] = 1.0
        eall[2 * c + 1, c, 64:128] = 1.0
    band = np.zeros((20, 128, 128), np.float32)
    for gi, w in enumerate((2, 4, 8, 16)):
        for t in range(128):
            for d in range(w):
                s = t - d
                if s >= 0:
                    band[gi, s, t] += 1.0 / w
                    cnt = min(w, t + 1)
                    band[8 + gi, s, t] += 1.0 / cnt
                else:
                    band[4 + gi, 128 + s, t] += 1.0 / w
            band[gi, t, t] -= 1.0
            band[8 + gi, t, t] -= 1.0
    for gi, w in enumerate((2, 4, 8, 16)):
        for s_ in range(NS):
            band[12 + gi, s_, s_] = 1.0 / w - 1.0
            for k in range(16 - w, 15):
                band[16 + gi, s_ * 15 + k, s_] = 1.0 / w
    return ident, eall, band


def _consts_samp():
    c = np.zeros((128, 1024), np.float32)
    for s_ in range(NS):
        c[s_, s_ * 128:(s_ + 1) * 128] = 1.0
    for g in range(2):
        for s_ in range(NS):
            for slot in range(15):
                p = g * 64 + s_ * 15 + slot
                c[s_, 512 + p] = 1.0
                c[p, 640 + g * 4 + s_] = 1.0
    for r in range(32):
        c[r, 648 + r // 4] = 1.0
        c[r, 828 + r // 4] = 1.0
        c[r, 656 + (r % 8) * 4 + r // 8] = 1.0
    for p in range(120):
        c[p, 688 + p // 30] = 1.0
    c[0:8, 692:820] = np.arange(128, dtype=np.float32)[None, :]
    c[:, 820:828] = np.arange(8, dtype=np.float32)[None, :]
    return c


def _perm_w_in(w_in):
    q = w_in[:, :512].reshape(D, 2, 4, 64).transpose(0, 2, 1, 3).reshape(D, 512)
    return np.ascontiguousarray(np.concatenate([q, w_in[:, 512:]], axis=1))


_NC_CACHE = {}


def _make_in_maps(inp, cores=range(8)):
    f = lambda a: np.ascontiguousarray(np.asarray(a, dtype=np.float32))
    ident, eall, band = _consts()
    norms = f(np.stack([inp["norm_mix"][0], inp["norm_ffn"][0], inp["norm_mix"][1], inp["norm_ffn"][1], inp["norm_final"]]))
    shared = {
        "w_in": _perm_w_in(f(inp["w_in_a"][0])), "w_out": f(inp["w_out_a"][0]),
        "w1k": f(inp["cmp_w1_k"][0]), "w2k": f(inp["cmp_w2_k"][0]), "posk": f(inp["cmp_pos_k"][0]),
        "w1v": f(inp["cmp_w1_v"][0]), "w2v": f(inp["cmp_w2_v"][0]), "posv": f(inp["cmp_pos_v"][0]),
        "conv_w": f(inp["conv_w"][0]), "conv_b": f(inp["conv_b"][0]), "ln_g": f(inp["conv_ln_g"][0]), "ln_b": f(inp["conv_ln_b"][0]),
        "pool_w": f(inp["pool_w"][0]), "pool_scale": f(inp["pool_scale"][0]).reshape(1, D), "norms": norms,
        "wg": f(inp["w_ffn_gate"]), "wu": f(inp["w_ffn_up"]), "wd": f(inp["w_ffn_down"]),
        "c_ident": ident, "c_eall": eall, "c_band": band, "c_samp": _consts_samp(),
        "cache_cmp": f(inp["cache_cmp_kv"][0]).reshape(5120 * 8, 16 * 256),
        "cache_slc": f(inp["cache_slc_kv"][0]).reshape(5120 * 2, 64 * 256),
    }
    xp = f(inp["x_prompt"])
    xs = f(inp["x_sample"]).reshape(32, D)
    pt = np.ascontiguousarray(np.asarray(inp["page_table"], dtype=np.int32))
    swin = f(inp["state_win_kv"][0]).reshape(32, 512, 256)
    sconv = f(inp["state_conv"][0])
    spool = f(inp["state_pool"][0])
    maps = []
    for c in cores:
        m = dict(shared)
        m["xp"] = xp[c]
        sl = slice(c * NS, (c + 1) * NS)
        m["xs"] = np.ascontiguousarray(xs[sl]); m["pt"] = np.ascontiguousarray(pt[sl])
        m["state_win"] = np.ascontiguousarray(swin[sl]); m["state_conv"] = np.ascontiguousarray(sconv[sl]); m["state_pool"] = np.ascontiguousarray(spool[sl])
        maps.append(m)
    return maps


def kernel(x_prompt, x_sample, cache_cmp_kv, cache_slc_kv, state_win_kv, state_conv, state_pool, page_table,
           norm_mix, norm_ffn, norm_final, w_in_a, w_out_a, cmp_pos_k, cmp_w1_k, cmp_w2_k, cmp_pos_v, cmp_w1_v,
           cmp_w2_v, conv_w, conv_b, conv_ln_g, conv_ln_b, pool_w, pool_scale, w_ffn_gate, w_ffn_up, w_ffn_down):
    inp = dict(x_prompt=x_prompt, x_sample=x_sample, cache_cmp_kv=cache_cmp_kv, cache_slc_kv=cache_slc_kv, state_win_kv=state_win_kv,
               state_conv=state_conv, state_pool=state_pool, page_table=page_table, norm_mix=norm_mix, norm_ffn=norm_ffn, norm_final=norm_final,
               w_in_a=w_in_a, w_out_a=w_out_a, cmp_pos_k=cmp_pos_k, cmp_w1_k=cmp_w1_k, cmp_w2_k=cmp_w2_k, cmp_pos_v=cmp_pos_v, cmp_w1_v=cmp_w1_v,
               cmp_w2_v=cmp_w2_v, conv_w=conv_w, conv_b=conv_b, conv_ln_g=conv_ln_g, conv_ln_b=conv_ln_b, pool_w=pool_w, pool_scale=pool_scale,
               w_ffn_gate=w_ffn_gate, w_ffn_up=w_ffn_up, w_ffn_down=w_ffn_down)
    if "nc" not in _NC_CACHE:
        _NC_CACHE["nc"] = build_nc(do_samples=True)
    nc = _NC_CACHE["nc"]
    in_maps = _make_in_maps(inp)
    res = run_bass_kernel_spmd(nc, in_maps, core_ids=list(range(8)))
    R = res.results
    B = 8
    DB = 32
    cat = lambda nm: np.concatenate([R[c][nm] for c in range(B)], axis=0)
    y_prompt = np.stack([R[c]["y_p"] for c in range(B)])
    cmp_p = np.stack([R[c]["o_cmp_p"] for c in range(B)]).reshape(1, B, T, 2, 2, 64)
    slc_p = np.stack([R[c]["o_slc_p"] for c in range(B)]).reshape(1, B, T, 2, 2, 64)
    win_p = np.stack([R[c]["o_win_p"] for c in range(B)]).reshape(1, B, 512, 2, 2, 64)
    conv_p = np.stack([R[c]["o_conv_p"] for c in range(B)]).reshape(1, B, 30, 512)
    pool_p = np.stack([R[c]["o_pool_p"] for c in range(B)]).reshape(1, B, 15, D)
    y_sample = cat("y_s").reshape(DB, 1, D)
    cmp_s = cat("o_cmp_s").reshape(1, DB, 1, 2, 2, 64)
    slc_s = cat("o_slc_s").reshape(1, DB, 1, 2, 2, 64)
    win_s = cat("o_win_s").reshape(1, DB, 512, 2, 2, 64)
    conv_s = cat("o_conv_s").reshape(1, DB, 30, 512)
    pool_s = cat("o_pool_s").reshape(1, DB, 15, D)
    return (y_prompt, y_sample, cmp_p, cmp_s, slc_p, slc_s, win_p, win_s, conv_p, conv_s, pool_p, pool_s)
```

```python
import numpy as np
import concourse.bass as bass
import concourse.mybir as mybir
from concourse.bass_utils import run_bass_kernel_spmd
from contextlib import ExitStack

F32 = mybir.dt.float32
BF16 = mybir.dt.bfloat16
I32 = mybir.dt.int32
U32 = mybir.dt.uint32
AF = mybir.ActivationFunctionType
ALU = mybir.AluOpType
AX = mybir.AxisListType

SCALE = 0.125
NEG = -30000.0
EPS = 1e-6
T = 4096
D = 1024
NBLK = 8
TB = 512
DFF = 2816
NF = 22
INA = 2328
NS = 4


class Buf:
    __slots__ = ("name", "w", "r", "base")

    def __init__(self, name):
        self.name = name
        self.w = {}
        self.r = {}
        self.base = {}


class Eng:
    def __init__(self, name):
        self.name = name
        self.cnt = 0
        self.known = {}


class MK:
    NLANES = 20

    def __init__(self, nc, es):
        self.nc = nc
        self.es = es
        self.eng = {n: Eng(n) for n in ("pe", "act", "dve", "pool", "sp")}
        self.h = {"pe": nc.tensor, "act": nc.scalar, "dve": nc.vector, "pool": nc.gpsimd, "sp": nc.sync}
        self.sems = {}
        self.nbuf = 0
        self.out_events = []
        self.lanes = {q: {"next": 0, "cnt": [0] * self.NLANES} for q in ("sp", "pool")}
        self.nops = 0
        self.uid = 0
        self.stopped = False

    def sb(self, name, shape, dt, es=None):
        self.uid += 1
        return (es or self.es).enter_context(self.nc.sbuf_tensor(f"{name}_{self.uid}", list(shape), dt))

    def ps(self, name, shape, dt, es=None):
        self.uid += 1
        return (es or self.es).enter_context(self.nc.psum_tensor(f"{name}_{self.uid}", list(shape), dt))

    def buf(self, name=None):
        self.nbuf += 1
        return Buf(name or f"b{self.nbuf}")

    def _sem(self, key):
        if key not in self.sems:
            self.sems[key] = self.es.enter_context(self.nc.semaphore(key))
        return self.sems[key]

    def _collect(self, e, reads, writes, add=False):
        waits = {}

        def need(ev):
            k, v = ev
            if e.name == "pe" and k == "e_pe":
                return
            if e.known.get(k, 0) >= v:
                return
            if waits.get(k, 0) < v:
                waits[k] = v

        for b in reads:
            for ev in b.w.items():
                need(ev)
        for b in writes:
            for ev in (b.base if add else b.w).items():
                need(ev)
            for ev in b.r.items():
                need(ev)
        return waits

    def _emit_waits(self, e, waits):
        h = self.h[e.name]
        for k, v in waits.items():
            e.known[k] = v
            h.wait_ge(self._sem(k), v)

    def _record(self, ev, reads, writes, add):
        k, v = ev
        for b in reads:
            if b.r.get(k, 0) < v:
                b.r[k] = v
        for b in writes:
            if add:
                if b.w.get(k, 0) < v:
                    b.w[k] = v
            else:
                b.w = {k: v}
                b.base = {k: v}
                b.r = {}

    def op(self, engname, fn, reads=(), writes=(), sig=True, add=False):
        if self.stopped:
            return None
        e = self.eng[engname]
        waits = self._collect(e, reads, writes, add)
        self._emit_waits(e, waits)
        key = "e_" + engname
        ins = fn(self.h[engname])
        if sig:
            e.cnt += 1
            ev = (key, e.cnt)
            ins.then_inc(self._sem(key), 1)
        else:
            ev = (key, e.cnt + 1)
        self._record(ev, reads, writes, add)
        self.nops += 1
        return ev

    def dma(self, engname, fn, reads=(), writes=(), is_out=False, add=False):
        if self.stopped:
            return None
        e = self.eng[engname]
        ln = self.lanes[engname]
        li = ln["next"]
        ln["next"] = (li + 1) % self.NLANES
        key = f"l_{engname}{li}"
        waits = self._collect(e, reads, writes, add)
        if ln["cnt"][li] > 0 and e.known.get(key, 0) < ln["cnt"][li]:
            waits[key] = max(waits.get(key, 0), ln["cnt"][li])
        self._emit_waits(e, waits)
        ins = fn(self.h[engname])
        ln["cnt"][li] += 16
        ev = (key, ln["cnt"][li])
        ins.then_inc(self._sem(key), 16)
        self._record(ev, reads, writes, add)
        if is_out:
            self.out_events.append(ev)
        self.nops += 1
        return ev

    def barrier(self):
        if self.stopped:
            return
        targets = {}
        for n, e in self.eng.items():
            if e.cnt > 0:
                targets["e_" + n] = e.cnt
        for q, ln in self.lanes.items():
            for i, c in enumerate(ln["cnt"]):
                if c > 0:
                    targets[f"l_{q}{i}"] = c
        for n, e in self.eng.items():
            waits = {}
            for k, v in targets.items():
                if k == "e_pe" and n == "pe":
                    continue
                if e.known.get(k, 0) < v:
                    waits[k] = v
            self._emit_waits(e, waits)

    def finish(self):
        e = self.eng["sp"]
        final = {}
        for q, ln in self.lanes.items():
            for i, c in enumerate(ln["cnt"]):
                k = f"l_{q}{i}"
                if c > 0 and e.known.get(k, 0) < c:
                    final[k] = c
        for n, en in self.eng.items():
            if n != "sp" and en.cnt > 0:
                final["e_" + n] = en.cnt
        self._emit_waits(e, final)


class Rot:
    def __init__(self, items):
        self.items = items
        self.i = 0

    def next(self):
        it = self.items[self.i]
        self.i = (self.i + 1) % len(self.items)
        return it


class _Stop(Exception):
    pass


def build_nc(do_samples=True, stop=None):
    nc = bass.Bass("TRN2", target_bir_lowering=False)
    es = ExitStack()
    with es:
        mk = MK(nc, es)

        hits = {}

        def chk(n):
            if stop is None:
                return
            code, nth = stop if isinstance(stop, tuple) else (stop, 1)
            if n == code:
                hits[n] = hits.get(n, 0) + 1
                if hits[n] == nth:
                    mk.stopped = True
        _build_body(nc, mk, do_samples, chk)
        mk.finish()
        print("nops", mk.nops, {n: e.cnt for n, e in mk.eng.items()})
    return nc


def _build_body(nc, mk, do_samples, chk):
    if True:

        def din(name, shape, dt=F32):
            return nc.dram_tensor(name, list(shape), dt, kind="ExternalInput").ap()

        def dout(name, shape, dt=F32):
            return nc.dram_tensor(name, list(shape), dt, kind="ExternalOutput").ap()

        def dscr(name, shape, dt=F32):
            return nc.dram_tensor(name, list(shape), dt, kind="Internal").ap()

        xp = din("xp", [T, D])
        w_in = din("w_in", [D, INA])
        w_out = din("w_out", [D, D])
        w1k = din("w1k", [32, 64, 64]); w2k = din("w2k", [64, 64]); posk = din("posk", [32, 64])
        w1v = din("w1v", [32, 64, 64]); w2v = din("w2v", [64, 64]); posv = din("posv", [32, 64])
        conv_w = din("conv_w", [31, 512]); conv_b = din("conv_b", [512])
        ln_g = din("ln_g", [512]); ln_b = din("ln_b", [512])
        pool_w = din("pool_w", [4, 256, 256]); pool_scale = din("pool_scale", [1, D])
        norms = din("norms", [5, D])
        wg_d = din("wg", [2, D, DFF]); wu_d = din("wu", [2, D, DFF]); wd_d = din("wd", [2, DFF, D])
        c_ident = din("c_ident", [128, 128])
        c_eall = din("c_eall", [64, 32, 128])
        c_band = din("c_band", [20, 128, 128])
        c_cmask = din("c_cmask", [128, 2, 128])

        y_p = dout("y_p", [T, D])
        o_cmp_p = dout("o_cmp_p", [T, 256]); o_slc_p = dout("o_slc_p", [T, 256]); o_win_p = dout("o_win_p", [512, 256])
        o_conv_p = dout("o_conv_p", [30, 512]); o_pool_p = dout("o_pool_p", [15, D])
        xs1 = dscr("xs1", [T, D]); xs3 = dscr("xs3", [T, D])
        bxs1 = [mk.buf(f"xs1_{i}") for i in range(NBLK)]
        bxs3 = [mk.buf(f"xs3_{i}") for i in range(NBLK)]

        if do_samples:
            xs_d = din("xs", [NS, D])
            pt_d = din("pt", [NS, 128], I32)
            cmp_rows = din("cache_cmp", [5120 * 8, 16 * 256])
            slc_hp = din("cache_slc", [5120 * 2, 64 * 256])
            swin_d = din("state_win", [NS, 512, 256])
            sconv_d = din("state_conv", [NS, 30, 512])
            spool_d = din("state_pool", [NS, 15, D])
            c_samp = din("c_samp", [128, 1024])
            y_s = dout("y_s", [NS, D])
            o_cmp_s = dout("o_cmp_s", [NS, 256]); o_slc_s = dout("o_slc_s", [NS, 256]); o_win_s = dout("o_win_s", [NS, 512, 256])
            o_conv_s = dout("o_conv_s", [NS, 30, 512]); o_pool_s = dout("o_pool_s", [NS, 15, D])
            hp_scr = dscr("hp_scr", [8, 16], I32); b_hp_scr = mk.buf("hp_scr")
            xs_scr = dscr("xs_scr", [NS, D]); b_xs_scr = mk.buf("xs_scr")
            C_OH = 0
            C_SEL2 = 512
            C_GS = 640
            C_GSEL = 648
            C_SELGH = 656
            C_IND30 = 688
            C_IOTA = 692
            C_CADD = 820
            C_DM = 828

        ident = mk.sb("ident", [128, 128], BF16); b_ident = mk.buf("ident")
        identf = mk.sb("identf", [128, 128], F32); b_identf = mk.buf("identf")
        onesf = mk.sb("onesf", [128, 128], F32); b_onesf = mk.buf("onesf")
        mk.dma("sp", lambda h: h.dma_start(out=identf[:], in_=c_ident), writes=[b_identf])
        mk.dma("pool", lambda h: h.dma_start(out=ident[:], in_=c_ident), writes=[b_ident])
        mk.op("dve", lambda h: h.memset(onesf[:], 1.0), writes=[b_onesf])

        def load_gain(gt, b_gt, slot, i):
            mk.dma("sp", lambda h: h.dma_start(out=gt[:, slot, :], in_=norms[i:i + 1, :].partition_broadcast(128)), writes=[b_gt], add=True)

        P = {"pd": [], "banks": [], "i": 0, "ng": 0}

        def setup_psum(scope, ng):
            P["pd"] = [mk.ps(f"pd{i}", [128, 1024], F32, scope) for i in range(ng // 2)]
            P["banks"] = []
            for i in range(ng // 2):
                P["banks"].append((P["pd"][i][:, 0:512], mk.buf(f"bank{2 * i}")))
                P["banks"].append((P["pd"][i][:, 512:1024], mk.buf(f"bank{2 * i + 1}")))
            P["i"] = 0
            P["ng"] = ng
            return Rot([(mk.ps(f"ptb{i}", [128, 1024], BF16, scope), mk.buf(f"ptb{i}")) for i in range(1)])

        def getbank():
            b = P["banks"][P["i"]]
            P["i"] = (P["i"] + 1) % P["ng"]
            return b

        def getdbl():
            if P["i"] % 2:
                P["i"] = (P["i"] + 1) % P["ng"]
            j = P["i"] // 2
            P["i"] = (P["i"] + 2) % P["ng"]
            return P["pd"][j], [P["banks"][2 * j][1], P["banks"][2 * j + 1][1]]

        PTR = {}

        evac_tog = {"i": 0}

        def evac(out, in_, reads, writes, eng=None, add=False):
            if eng is None:
                eng = "act" if evac_tog["i"] % 2 == 0 else "dve"
                evac_tog["i"] += 1
            if eng == "act":
                return mk.op("act", lambda h: h.copy(out=out, in_=in_), reads=reads, writes=writes, add=add)
            return mk.op("dve", lambda h: h.tensor_copy(out=out, in_=in_), reads=reads, writes=writes, add=add)

        def rmsnorm_tile(x_ap, npart, g_ap, reads, st, out_bf=None, b_bf=None, out_f32=None, b_f32=None):
            st_t, st_b = st
            jt, jb = (out_f32, b_f32) if out_f32 is not None else (out_bf, b_bf)
            mk.op("act", lambda h: h.activation(out=jt, in_=x_ap, func=AF.Square, accum_out=st_t[0:npart, 0:1]), reads=reads, writes=[jb, st_b])
            mk.op("act", lambda h: h.activation(out=st_t[0:npart, 1:2], in_=st_t[0:npart, 0:1], func=AF.Sqrt, scale=1.0 / D, bias=EPS),
                  reads=[st_b], writes=[st_b])
            mk.op("dve", lambda h: h.reciprocal(out=st_t[0:npart, 2:3], in_=st_t[0:npart, 1:2]), reads=[st_b], writes=[st_b])
            mk.op("dve", lambda h: h.scalar_tensor_tensor(out=jt, in0=x_ap, scalar=st_t[0:npart, 2:3], in1=g_ap, op0=ALU.mult, op1=ALU.mult),
                  reads=list(reads) + [st_b], writes=[jb])
            if out_f32 is not None and out_bf is not None:
                mk.op("act", lambda h: h.copy(out=out_bf, in_=out_f32), reads=[b_f32], writes=[b_bf])

        def transpose_to(hb_ap, hb_buf, dst_ap, dst_buf, nchunk, npart=128):
            pt, bpt = PTR["r"].next()
            for kc in range(nchunk):
                mk.op("pe", lambda h, kc=kc: h.transpose(out=pt[:, kc * 128:kc * 128 + npart], in_=hb_ap[:, kc * 128:(kc + 1) * 128],
                                                         identity=ident[0:npart, 0:npart]),
                      reads=[hb_buf, b_ident], writes=[bpt], sig=(kc == nchunk - 1), add=(kc > 0))
            src = pt[:, 0:nchunk * 128].rearrange("p (c t) -> p c t", t=128)[:, :, 0:npart]
            evac(dst_ap, src, [bpt], [dst_buf], add=True)

        esA = ExitStack()
        with esA:
            PTR["r"] = setup_psum(esA, 2)
            po_r = Rot([(mk.ps(f"pacc{i}", [128, 512], F32, esA), mk.buf(f"pacc{i}")) for i in range(2)])
            S_r = Rot([(mk.ps(f"psS{i}", [128, 512], F32, esA)[:, :], mk.buf(f"psS{i}")) for i in range(3)])
            gbA = mk.sb("gbA", [128, 1, D], F32, esA); b_gbA = mk.buf("gbA")
            load_gain(gbA, b_gbA, 0, 0)
            Win = mk.sb("Win", [128, 8, INA], BF16, esA); b_Win = mk.buf("Win")
            Wout = mk.sb("Wout", [128, 8, D], BF16, esA); b_Wout = mk.buf("Wout")
            for kc in range(8):
                mk.dma("pool", lambda h, kc=kc: h.dma_start(out=Win[:, kc, :], in_=w_in[kc * 128:(kc + 1) * 128, :]), writes=[b_Win], add=True)
            for kc in range(8):
                mk.dma("pool", lambda h, kc=kc: h.dma_start(out=Wout[:, kc, :], in_=w_out[kc * 128:(kc + 1) * 128, :]), writes=[b_Wout], add=True)
            W1 = {}; W2 = {}; post = {}
            b_W1 = mk.buf("W1"); b_W2 = mk.buf("W2"); b_post = mk.buf("post")
            posT = mk.sb("posT", [128, 2, 32], BF16, esA); b_posT = mk.buf("posT")
            postt = mk.sb("postt", [128, 2], F32, esA)
            for j, (w1d, w2d, posd) in enumerate(((w1k, w2k, posk), (w1v, w2v, posv))):
                W1[j] = mk.sb(f"W1_{j}", [128, 32, 128], BF16, esA)
                W2[j] = mk.sb(f"W2_{j}", [128, 128], BF16, esA)
                mk.op("pool", lambda h, j=j: h.memset(W1[j][:], 0.0), writes=[b_W1], add=(j > 0))
                mk.op("pool", lambda h, j=j: h.memset(W2[j][:], 0.0), writes=[b_W2], add=(j > 0))
            with nc.allow_non_contiguous_dma(reason="small weight relayout"):
                for j, (w1d, w2d, posd) in enumerate(((w1k, w2k, posk), (w1v, w2v, posv))):
                    for g in range(2):
                        mk.dma("pool", lambda h, j=j, g=g, w1d=w1d: h.dma_start(out=W1[j][g * 64:(g + 1) * 64, :, g * 64:(g + 1) * 64],
                                                                                  in_=w1d.rearrange("l d e -> d l e")), writes=[b_W1], add=(j + g > 0))
                        mk.dma("pool", lambda h, j=j, g=g, w2d=w2d: h.dma_start(out=W2[j][g * 64:(g + 1) * 64, g * 64:(g + 1) * 64], in_=w2d),
                               writes=[b_W2], add=(j + g > 0))
                        mk.dma("pool", lambda h, j=j, g=g, posd=posd: h.dma_start(out=posT[g * 64:(g + 1) * 64, j, :], in_=posd.rearrange("l d -> d l")),
                               writes=[b_posT], add=(j + g > 0))
                cw = mk.sb("cw", [128, 4, 31], F32, esA); b_cw = mk.buf("cw")
                cvec = mk.sb("cvec", [128, 3, 4], F32, esA); b_cvec = mk.buf("cvec")
                for c in range(4):
                    mk.dma("sp", lambda h, c=c: h.dma_start(out=cw[:, c, :], in_=conv_w[:, c * 128:(c + 1) * 128].rearrange("k p -> p k")), writes=[b_cw], add=(c > 0))
                for i, v in enumerate((conv_b, ln_g, ln_b)):
                    mk.dma("sp", lambda h, i=i, v=v: h.dma_start(out=cvec[:, i, :], in_=v.rearrange("(c p) -> p c", p=128)), writes=[b_cvec], add=(i > 0))
            for j in range(2):
                pb, bpb = getbank()
                for l in range(32):
                    mk.op("pe", lambda h, j=j, l=l: h.matmul(pb[:, 0:1], lhsT=W1[j][:, l, :], rhs=posT[:, j, l:l + 1], start=(l == 0), stop=(l == 31)),
                          reads=[b_W1, b_posT], writes=[bpb], sig=(l == 31), add=(l > 0))
                evac(postt[:, j:j + 1], pb[:, 0:1], [bpb], [b_post], add=(j > 0))

            chk(1)
            if do_samples:
                esS = ExitStack()
                with esS:
                    csamp = mk.sb("csamp", [128, 1024], F32, esS); b_csamp = mk.buf("csamp")
                    mk.dma("sp", lambda h: h.dma_start(out=csamp[:], in_=c_samp), writes=[b_csamp])
                    sst_ = mk.sb("s_st", [128, 4], F32, esS); b_sst_ = mk.buf("s_st")
                    xs_sb = mk.sb("xs_sb", [NS, D], F32, esS); b_xs_sb = mk.buf("xs_sb")
                    hbs = mk.sb("hbs", [NS, D], BF16, esS); b_hbs = mk.buf("hbs")
                    hTs = mk.sb("hTs", [128, 8, NS], BF16, esS); b_hTs = mk.buf("hTs")
                    zs = mk.sb("zs", [NS, INA], F32, esS); b_zs = mk.buf("zs")
                    mk.dma("sp", lambda h: h.dma_start(out=xs_sb[:], in_=xs_d), writes=[b_xs_sb])
                    rmsnorm_tile(xs_sb[:], NS, gbA[0:NS, 0, :], [b_xs_sb, b_gbA], (sst_, b_sst_), out_bf=hbs[:], b_bf=b_hbs)
                    transpose_to(hbs[:], b_hbs, hTs[:, :, :], b_hTs, 8, NS)
                    for c0 in range(0, INA, 512):
                        n = min(512, INA - c0)
                        pb, bpb = getbank()
                        for kc in range(8):
                            mk.op("pe", lambda h, kc=kc: h.matmul(pb[0:NS, 0:n], lhsT=hTs[:, kc, :], rhs=Win[:, kc, c0:c0 + n], start=(kc == 0), stop=(kc == 7)),
                                  reads=[b_hTs, b_Win], writes=[bpb], sig=(kc == 7), add=(kc > 0))
                        evac(zs[:, c0:c0 + n], pb[0:NS, 0:n], [bpb], [b_zs], add=True)
                    qTs = mk.sb("qTs", [128, 4, NS], BF16, esS); b_qTs = mk.buf("qTs")
                    for hh in range(4):
                        pb, bpb = getbank()
                        for kc in range(8):
                            mk.op("pe", lambda h, kc=kc: h.matmul(pb[:, 0:NS], lhsT=Win[:, kc, hh * 128:(hh + 1) * 128], rhs=hTs[:, kc, :], start=(kc == 0), stop=(kc == 7)),
                                  reads=[b_hTs, b_Win], writes=[bpb], sig=(kc == 7), add=(kc > 0))
                        evac(qTs[:, hh, :], pb[:, 0:NS], [bpb], [b_qTs], add=True)
                    Lq = mk.sb("Lq", [128, NS, 32], BF16, esS); b_Lq = mk.buf("Lq")
                    mk.op("dve", lambda h: h.memset(Lq[:], 0.0), writes=[b_Lq])
                    for s_ in range(NS):
                        for g in range(2):
                            c0 = s_ * 8 + g * 4
                            mk.op("dve", lambda h, s_=s_, g=g, c0=c0: h.tensor_copy(out=Lq[g * 64:(g + 1) * 64, s_, c0:c0 + 4], in_=qTs[g * 64:(g + 1) * 64, :, s_]),
                                  reads=[b_qTs], writes=[b_Lq], add=True)
                    mk.dma("sp", lambda h: h.dma_start(out=o_cmp_s, in_=zs[:, 512:768]), reads=[b_zs], is_out=True)
                    mk.dma("sp", lambda h: h.dma_start(out=o_slc_s, in_=zs[:, 768:1024]), reads=[b_zs], is_out=True)
                    for s_ in range(NS):
                        mk.dma("sp", lambda h, s_=s_: h.dma_start(out=o_win_s[s_, 0:511, :], in_=swin_d[s_, 1:512, :]), is_out=True)
                        mk.dma("sp", lambda h, s_=s_: h.dma_start(out=o_win_s[s_, 511:512, :], in_=zs[s_:s_ + 1, 1024:1280]), reads=[b_zs], is_out=True)
                        mk.dma("sp", lambda h, s_=s_: h.dma_start(out=o_conv_s[s_, 0:29, :], in_=sconv_d[s_, 1:30, :]), is_out=True)
                        mk.dma("sp", lambda h, s_=s_: h.dma_start(out=o_pool_s[s_, 0:14, :], in_=spool_d[s_, 1:15, :]), is_out=True)

                    ptT = mk.sb("ptT", [128, NS], I32, esS); b_ptT = mk.buf("ptT")
                    with nc.allow_non_contiguous_dma(reason="page table transpose"):
                        mk.dma("sp", lambda h: h.dma_start(out=ptT[:], in_=pt_d.rearrange("s p -> p s")), writes=[b_ptT])
                    ptTf = mk.sb("ptTf", [128, NS], F32, esS); b_ptTf = mk.buf("ptTf")
                    mk.op("dve", lambda h: h.tensor_copy(out=ptTf[:], in_=ptT[:]), reads=[b_ptT], writes=[b_ptTf])
                    mk.op("dve", lambda h: h.tensor_scalar(out=ptTf[:], in0=ptTf[:], scalar1=8.0, scalar2=None, op0=ALU.mult), reads=[b_ptTf], writes=[b_ptTf])
                    idxcf = mk.sb("idxcf", [128, NS, 8], F32, esS); b_idxcf = mk.buf("idxcf")
                    mk.op("dve", lambda h: h.tensor_tensor(out=idxcf[:], in0=ptTf[:].unsqueeze(2).to_broadcast([128, NS, 8]),
                                                           in1=csamp[:, C_CADD:C_CADD + 8].unsqueeze(1).to_broadcast([128, NS, 8]), op=ALU.add),
                          reads=[b_ptTf, b_csamp], writes=[b_idxcf])
                    idxc = mk.sb("idxc", [128, NS * 8], I32, esS); b_idxc = mk.buf("idxc")
                    mk.op("dve", lambda h: h.tensor_copy(out=idxc[:], in_=idxcf[:].rearrange("p a b -> p (a b)")), reads=[b_idxcf], writes=[b_idxc])

                    KCs = mk.sb("KCs", [128, NS, 8, 128], BF16, esS); b_KCs = [mk.buf(f"KCs{i}") for i in range(NS)]
                    VCs = mk.sb("VCs", [128, NS, 8, 128], BF16, esS); b_VCs = [mk.buf(f"VCs{i}") for i in range(NS)]
                    esC = ExitStack()
                    with esC:
                        X_r = Rot([(mk.sb(f"Xg{i}", [128, 16, 256], F32, esC), mk.buf(f"Xg{i}")) for i in range(2)])
                        XTP = (mk.sb("XTP", [128, 2, 16, 128], BF16, esC), mk.buf("XTP"))
                        XT_r = Rot([(mk.sb(f"XT{i}", [128, 2, 16, 128], BF16, esC), mk.buf(f"XT{i}")) for i in range(2)])
                        gh_r = Rot([(mk.sb(f"gh{i}", [128, 128], BF16, esC), mk.buf(f"gh{i}")) for i in range(2)])

                        def compress_chunk(s_, c, XTa, XTb, shift):
                            for kv in range(2):
                                pb, bpb = getbank()
                                for l in range(32):
                                    if l < 16:
                                        rhs = XTa[0][:, kv, l, :]; rb = XTa[1]; ncol = 128
                                    elif not shift:
                                        rhs = XTb[0][:, kv, l - 16, :]; rb = XTb[1]; ncol = 128
                                    else:
                                        rhs = XTb[0][:, kv, l - 16, 1:128]; rb = XTb[1]; ncol = 127
                                    mk.op("pe", lambda h, l=l, rhs=rhs, ncol=ncol: h.matmul(pb[:, 0:ncol], lhsT=W1[kv][:, l, :], rhs=rhs, start=(l == 0), stop=(l == 31)),
                                          reads=[b_W1, rb], writes=[bpb], sig=(l == 31), add=(l > 0))
                                gh, b_gh = gh_r.next()
                                mk.op("act", lambda h, gh=gh, pb=pb, kv=kv: h.activation(out=gh[:], in_=pb[:, 0:128], func=AF.Gelu, bias=postt[:, kv:kv + 1]),
                                      reads=[bpb, b_post], writes=[b_gh])
                                pb2, bpb2 = getbank()
                                if kv == 0:
                                    mk.op("pe", lambda h, gh=gh, pb2=pb2: h.matmul(pb2[:, 0:128], lhsT=W2[0][:], rhs=gh[:], start=True, stop=True), reads=[b_W2, b_gh], writes=[bpb2])
                                    evac(KCs[:, s_, c, :], pb2[:, 0:128], [bpb2], [b_KCs[s_]], add=True)
                                else:
                                    mk.op("pe", lambda h, gh=gh, pb2=pb2: h.matmul(pb2[:, 0:128], lhsT=gh[:], rhs=W2[1][:], start=True, stop=True), reads=[b_W2, b_gh], writes=[bpb2])
                                    evac(VCs[:, s_, c, :], pb2[:, 0:128], [bpb2], [b_VCs[s_]], add=True)

                        for s_ in range(NS):
                            prev = None
                            for c in range(8):
                                X, b_X = X_r.next()
                                col = s_ * 8 + c
                                mk.dma("pool", lambda h, X=X, col=col: h.indirect_dma_start(out=X[:].rearrange("p a b -> p (a b)"), out_offset=None, in_=cmp_rows,
                                                                                          in_offset=bass.IndirectOffsetOnAxis(ap=idxc[:, col:col + 1], axis=0)),
                                       reads=[b_idxc], writes=[b_X])
                                XTc = XTP if c == 0 else XT_r.next()
                                for kv in range(2):
                                    for q4 in range(4):
                                        pb, bpb = getbank()
                                        for r in range(4):
                                            l = q4 * 4 + r
                                            mk.op("pe", lambda h, X=X, l=l, r=r, kv=kv: h.transpose(out=pb[:, r * 128:(r + 1) * 128], in_=X[:, l, kv * 128:(kv + 1) * 128], identity=identf[:]),
                                                  reads=[b_X, b_identf], writes=[bpb], sig=(r == 3), add=(r > 0))
                                        evac(XTc[0][:, kv, q4 * 4:(q4 + 1) * 4, :], pb[:, :].rearrange("p (a b) -> p a b", b=128), [bpb], [XTc[1]], add=True)
                                if prev is not None:
                                    compress_chunk(s_, c - 1, prev, XTc, False)
                                prev = XTc
                            compress_chunk(s_, 7, prev, XTP, True)
                    mk.barrier()
                    pS, bS = getdbl()
                    for half in range(2):
                        for s_ in range(NS):
                            mk.op("pe", lambda h, s_=s_, half=half: h.matmul(pS[0:32, half * 512:(half + 1) * 512], lhsT=Lq[:, s_, :],
                                                                            rhs=KCs[:, s_, half * 4:(half + 1) * 4, :].rearrange("p a b -> p (a b)"), start=(s_ == 0), stop=(s_ == NS - 1)),
                                  reads=[b_Lq, b_KCs[s_]], writes=[bS[half]], sig=(s_ == NS - 1), add=(s_ > 0))
                    pall = mk.sb("pall", [32, 1024], F32, esS); b_pall = mk.buf("pall")
                    sm = mk.sb("sm", [32, 4], F32, esS); b_sm = mk.buf("sm")
                    mk.op("act", lambda h: h.activation(out=pall[:], in_=pS[0:32, :], func=AF.Exp, scale=SCALE), reads=bS, writes=[b_pall])
                    mk.op("dve", lambda h: h.memset(pall[:, 1023:1024], 0.0), reads=[b_pall], writes=[b_pall])
                    mk.op("dve", lambda h: h.tensor_reduce(out=sm[:, 0:1], in_=pall[:], axis=AX.X, op=ALU.add), reads=[b_pall], writes=[b_sm])
                    mk.op("dve", lambda h: h.reciprocal(out=sm[:, 1:2], in_=sm[:, 0:1]), reads=[b_sm], writes=[b_sm])
                    mk.op("dve", lambda h: h.tensor_scalar(out=pall[:], in0=pall[:], scalar1=sm[:, 1:2], scalar2=None, op0=ALU.mult), reads=[b_pall, b_sm], writes=[b_pall])
                    hpp = mk.sb("hpp", [128, 1], I32, esS); b_hpp = mk.buf("hpp")
                    esI = ExitStack()
                    esI.__enter__()
                    Bbuf = mk.sb("Bbuf", [8, 1032], F32, esI); b_Bbuf = mk.buf("Bbuf")
                    mk.op("dve", lambda h: h.memset(Bbuf[:], 0.0), writes=[b_Bbuf])
                    pI, bI = getdbl()
                    for half in range(2):
                        mk.op("pe", lambda h, half=half: h.matmul(pI[0:8, half * 512:(half + 1) * 512], lhsT=csamp[0:32, C_GSEL:C_GSEL + 8], rhs=pall[:, half * 512:(half + 1) * 512],
                                                                 start=True, stop=True), reads=[b_csamp, b_pall], writes=[bI[half]])
                    mk.op("dve", lambda h: h.tensor_copy(out=Bbuf[:, 1:1025].rearrange("p (g c) -> p c g", c=8), in_=pI[0:8, :].rearrange("p (c g) -> p c g", g=128)),
                          reads=bI, writes=[b_Bbuf])
                    scs = mk.sb("scs", [8, 260], F32, esI); b_scs = mk.buf("scs")
                    scs2 = mk.sb("scs2", [8, 260], F32, esI); b_scs2 = mk.buf("scs2")
                    sAs = mk.sb("sAs", [8, 257], F32, esI); b_sAs = mk.buf("sAs")
                    Bv = Bbuf[:, 0:1028].rearrange("p (j r) -> p j r", r=4)
                    Bv2 = Bbuf[:, 4:1032].rearrange("p (j r) -> p j r", r=4)
                    mk.op("dve", lambda h: h.tensor_reduce(out=sAs[:], in_=Bv[:, :, 1:4], axis=AX.X, op=ALU.add), reads=[b_Bbuf], writes=[b_sAs])
                    mk.op("dve", lambda h: h.scalar_tensor_tensor(out=scs[:, 0:257], in0=sAs[:], scalar=2.0, in1=Bv[:, :, 0], op0=ALU.mult, op1=ALU.add),
                          reads=[b_sAs, b_Bbuf], writes=[b_scs])
                    mk.op("dve", lambda h: h.tensor_tensor(out=scs[:, 0:257], in0=scs[:, 0:257], in1=Bv2[:, :, 0], op=ALU.add), reads=[b_scs, b_Bbuf], writes=[b_scs])
                    mk.op("dve", lambda h: h.memset(scs[:, 0:1], 1e9), reads=[b_scs], writes=[b_scs])
                    mk.op("dve", lambda h: h.memset(scs[:, 255:256], 2e9), reads=[b_scs], writes=[b_scs])
                    mk.op("dve", lambda h: h.memset(scs[:, 256:257], 3e9), reads=[b_scs], writes=[b_scs])
                    mk.op("dve", lambda h: h.memset(scs[:, 257:260], -4e9), reads=[b_scs], writes=[b_scs])
                    m8s = mk.sb("m8s", [8, 16], F32, esI); b_m8s = mk.buf("m8s")
                    i8s = mk.sb("i8s", [8, 16], U32, esI); b_i8s = mk.buf("i8s")
                    mk.op("dve", lambda h: h.max(out=m8s[:, 0:8], in_=scs[:]), reads=[b_scs], writes=[b_m8s])
                    mk.op("dve", lambda h: h.max_index(out=i8s[:, 0:8], in_max=m8s[:, 0:8], in_values=scs[:]), reads=[b_scs, b_m8s], writes=[b_i8s])
                    mk.op("dve", lambda h: h.match_replace(out=scs2[:], in_to_replace=m8s[:, 0:8], in_values=scs[:], imm_value=-4e9), reads=[b_scs, b_m8s], writes=[b_scs2])
                    mk.op("dve", lambda h: h.max(out=m8s[:, 8:16], in_=scs2[:]), reads=[b_scs2], writes=[b_m8s])
                    mk.op("dve", lambda h: h.max_index(out=i8s[:, 8:16], in_max=m8s[:, 8:16], in_values=scs2[:]), reads=[b_scs2, b_m8s], writes=[b_i8s])
                    ixf = mk.sb("ixf", [8, 16], F32, esI); b_ixf = mk.buf("ixf")
                    hlf = mk.sb("hlf", [8, 16], F32, esI); b_hlf = mk.buf("hlf")
                    pgf = mk.sb("pgf", [8, 16], F32, esI); b_pgf = mk.buf("pgf")
                    mk.op("dve", lambda h: h.tensor_copy(out=ixf[:], in_=i8s[:]), reads=[b_i8s], writes=[b_ixf])
                    io2 = mk.sb("io2", [8, 128], F32, esI); b_io2 = mk.buf("io2")
                    oh2 = mk.sb("oh2", [8, 16, 128], F32, esI); b_oh2 = mk.buf("oh2")
                    mk.op("dve", lambda h: h.tensor_scalar(out=io2[:], in0=csamp[0:8, C_IOTA:C_IOTA + 128], scalar1=1.0, scalar2=2.0, op0=ALU.add, op1=ALU.mult), reads=[b_csamp], writes=[b_io2])
                    mk.op("dve", lambda h: h.tensor_tensor(out=oh2[:], in0=ixf[:].unsqueeze(2).to_broadcast([8, 16, 128]), in1=io2[:].unsqueeze(1).to_broadcast([8, 16, 128]), op=ALU.is_ge),
                          reads=[b_ixf, b_io2], writes=[b_oh2])
                    mk.op("dve", lambda h: h.tensor_reduce(out=pgf[:], in_=oh2[:], axis=AX.X, op=ALU.add), reads=[b_oh2], writes=[b_pgf])
                    mk.op("dve", lambda h: h.scalar_tensor_tensor(out=hlf[:], in0=pgf[:], scalar=-2.0, in1=ixf[:], op0=ALU.mult, op1=ALU.add), reads=[b_pgf, b_ixf], writes=[b_hlf])
                    pt8 = mk.sb("pt8", [8, 128], I32, esI); b_pt8 = mk.buf("pt8")
                    for s_ in range(NS):
                        for g in range(2):
                            mk.dma("sp", lambda h, s_=s_, g=g: h.dma_start(out=pt8[s_ * 2 + g:s_ * 2 + g + 1, :], in_=pt_d[s_:s_ + 1, :]), writes=[b_pt8], add=True)
                    pt8f = mk.sb("pt8f", [8, 128], F32, esI); b_pt8f = mk.buf("pt8f")
                    mk.op("dve", lambda h: h.tensor_copy(out=pt8f[:], in_=pt8[:]), reads=[b_pt8], writes=[b_pt8f])
                    oh = mk.sb("oh", [8, 16, 128], F32, esI); b_oh = mk.buf("oh")
                    mk.op("dve", lambda h: h.tensor_tensor(out=oh[:], in0=csamp[0:8, C_IOTA:C_IOTA + 128].unsqueeze(1).to_broadcast([8, 16, 128]),
                                                           in1=pgf[:].unsqueeze(2).to_broadcast([8, 16, 128]), op=ALU.is_equal), reads=[b_csamp, b_pgf], writes=[b_oh])
                    mk.op("dve", lambda h: h.tensor_tensor(out=oh[:], in0=oh[:], in1=pt8f[:].unsqueeze(1).to_broadcast([8, 16, 128]), op=ALU.mult), reads=[b_oh, b_pt8f], writes=[b_oh])
                    phys = mk.sb("phys", [8, 16], F32, esI); b_phys = mk.buf("phys")
                    mk.op("dve", lambda h: h.tensor_reduce(out=phys[:], in_=oh[:], axis=AX.X, op=ALU.add), reads=[b_oh], writes=[b_phys])
                    mk.op("dve", lambda h: h.scalar_tensor_tensor(out=phys[:], in0=phys[:], scalar=2.0, in1=hlf[:], op0=ALU.mult, op1=ALU.add), reads=[b_phys, b_hlf], writes=[b_phys])
                    hpi = mk.sb("hpi", [8, 16], I32, esI); b_hpi = mk.buf("hpi")
                    mk.op("dve", lambda h: h.tensor_copy(out=hpi[:], in_=phys[:]), reads=[b_phys], writes=[b_hpi])
                    mk.dma("sp", lambda h: h.dma_start(out=hp_scr, in_=hpi[:]), reads=[b_hpi], writes=[b_hp_scr])
                    mk.op("dve", lambda h: h.memset(hpp[:], 0), writes=[b_hpp])
                    with nc.allow_non_contiguous_dma(reason="index relayout"):
                        for g in range(2):
                            for s_ in range(NS):
                                p0 = g * 64 + s_ * 15
                                mk.dma("sp", lambda h, g=g, s_=s_, p0=p0: h.dma_start(out=hpp[p0:p0 + 15, :], in_=hp_scr[s_ * 2 + g:s_ * 2 + g + 1, 1:16].rearrange("a b -> b a")),
                                       reads=[b_hp_scr], writes=[b_hpp], add=True)
                    esI.close()
                    mk.barrier()
                    oS = mk.sb("oS", [NS, 2, 4, 65], F32, esS); b_oS = mk.buf("oS")
                    esG = ExitStack()
                    with esG:
                        SG = mk.sb("SG", [128, 64, 256], F32, esG); b_SG = mk.buf("SG")
                        mk.dma("pool", lambda h: h.indirect_dma_start(out=SG[:].rearrange("p a b -> p (a b)"), out_offset=None, in_=slc_hp,
                                                                      in_offset=bass.IndirectOffsetOnAxis(ap=hpp[:, 0:1], axis=0)), reads=[b_hpp], writes=[b_SG])
                        qb = mk.sb("qb", [128, 512], F32, esG); b_qb = mk.buf("qb")
                        pb, bpb = getbank()
                        mk.op("pe", lambda h: h.matmul(pb[:, :], lhsT=csamp[0:NS, C_SEL2:C_SEL2 + 128], rhs=zs[:, 0:512], start=True, stop=True), reads=[b_csamp, b_zs], writes=[bpb])
                        evac(qb[:], pb[:, :], [bpb], [b_qb])
                        tmpS = mk.sb("tmpS", [128, 32, 64], F32, esG); b_tmpS = mk.buf("tmpS")
                        scS = mk.sb("scS", [128, 4, 64], F32, esG); b_scS = mk.buf("scS")
                        oaug = mk.sb("oaug", [128, 4, 65], F32, esG); b_oaug = mk.buf("oaug")
                        mk.op("dve", lambda h: h.memset(oaug[:], 0.0), writes=[b_oaug])
                        for g in range(2):
                            pr = slice(g * 64, g * 64 + 60)
                            for hh in range(4):
                                qv = qb[pr, hh * 128 + g * 64:hh * 128 + g * 64 + 64]
                                for rh in range(2):
                                    rr = slice(rh * 32, (rh + 1) * 32)
                                    mk.op("dve", lambda h, pr=pr, qv=qv, g=g, rr=rr: h.tensor_tensor(out=tmpS[pr], in0=SG[pr, rr, g * 64:(g + 1) * 64],
                                                                                                  in1=qv.unsqueeze(1).to_broadcast([60, 32, 64]), op=ALU.mult),
                                          reads=[b_SG, b_qb], writes=[b_tmpS])
                                    mk.op("dve", lambda h, pr=pr, hh=hh, rr=rr: h.tensor_reduce(out=scS[pr, hh, rr], in_=tmpS[pr], axis=AX.X, op=ALU.add), reads=[b_tmpS], writes=[b_scS], add=True)
                            mk.op("act", lambda h, pr=pr: h.activation(out=scS[pr], in_=scS[pr], func=AF.Exp, scale=SCALE), reads=[b_scS], writes=[b_scS])
                            mk.op("dve", lambda h, pr=pr: h.tensor_reduce(out=oaug[pr, :, 64], in_=scS[pr], axis=AX.X, op=ALU.add), reads=[b_scS], writes=[b_oaug], add=True)
                            for hh in range(4):
                                for dh in range(2):
                                    dd = slice(dh * 32, (dh + 1) * 32)
                                    vv = SG[pr, :, 128 + g * 64 + dh * 32:128 + g * 64 + (dh + 1) * 32].rearrange("p r d -> p d r")
                                    mk.op("dve", lambda h, pr=pr, vv=vv, hh=hh: h.tensor_tensor(out=tmpS[pr], in0=vv,
                                                                                              in1=scS[pr, hh, :].unsqueeze(1).to_broadcast([60, 32, 64]), op=ALU.mult),
                                          reads=[b_SG, b_scS], writes=[b_tmpS])
                                    mk.op("dve", lambda h, pr=pr, hh=hh, dd=dd: h.tensor_reduce(out=oaug[pr, hh, dd], in_=tmpS[pr], axis=AX.X, op=ALU.add), reads=[b_tmpS], writes=[b_oaug], add=True)
                        pO, bpO = getdbl()
                        for g in range(2):
                            mk.op("pe", lambda h, g=g: h.matmul(pO[0:NS, g * 512:g * 512 + 260], lhsT=csamp[:, C_GS + g * 4:C_GS + g * 4 + 4], rhs=oaug[:].rearrange("p a b -> p (a b)"),
                                                               start=True, stop=True), reads=[b_csamp, b_oaug], writes=[bpO[g]])
                        for g in range(2):
                            evac(oS[:, g, :, :], pO[0:NS, g * 512:g * 512 + 260].rearrange("p (a b) -> p a b", b=65), [bpO[g]], [b_oS], add=True)
                    mk.barrier()
                    qv4 = zs[:, 0:512].rearrange("p (a g d) -> p a g d", a=4, g=2)
                    tq = mk.sb("tq", [NS, 4, 2, 64], F32, esS); b_tq = mk.buf("tq")
                    enew = mk.sb("enew", [NS, 2, 4, 2], F32, esS); b_enew = mk.buf("enew")
                    for bi, c0 in enumerate((768, 1024)):
                        kn = zs[:, c0:c0 + 128].rearrange("p (g d) -> p g d", g=2)
                        mk.op("dve", lambda h, kn=kn: h.tensor_tensor(out=tq[:], in0=qv4, in1=kn.unsqueeze(1).to_broadcast([NS, 4, 2, 64]), op=ALU.mult), reads=[b_zs], writes=[b_tq])
                        mk.op("dve", lambda h, bi=bi: h.tensor_reduce(out=enew[:, bi, :, :], in_=tq[:], axis=AX.X, op=ALU.add), reads=[b_tq], writes=[b_enew], add=True)
                    mk.op("act", lambda h: h.activation(out=enew[:], in_=enew[:], func=AF.Exp, scale=SCALE), reads=[b_enew], writes=[b_enew])
                    Wn = mk.sb("Wn", [128, NS, 4, 256], F32, esS); b_Wn = mk.buf("Wn")
                    for s_ in range(NS):
                        mk.dma("sp", lambda h, s_=s_: h.dma_start(out=Wn[:, s_, :, :], in_=swin_d[s_].rearrange("(c p) d -> p c d", p=128)), writes=[b_Wn], add=True)
                    Vaug = mk.sb("Vaug", [128, NS, 4, 2, 65], F32, esS); b_Vaug = mk.buf("Vaug")
                    mk.op("pool", lambda h: h.memset(Vaug[:, :, :, :, 64:65], 1.0), writes=[b_Vaug])
                    for s_ in range(NS):
                        mk.op("pool", lambda h, s_=s_: h.tensor_copy(out=Vaug[:, s_, :, :, 0:64], in_=Wn[:, s_, :, 128:256].rearrange("p c (g d) -> p c g d", g=2)),
                              reads=[b_Wn], writes=[b_Vaug], add=True)
                    Pz = mk.sb("Pz", [128, NS, 4, 8, NS], F32, esS); b_Pz = mk.buf("Pz")
                    mk.op("dve", lambda h: h.memset(Pz[:], 0.0), writes=[b_Pz])
                    swt = mk.sb("swt", [128, 4, 8], F32, esS); b_swt = mk.buf("swt")
                    tw = mk.sb("tw", [128, 4, 64], F32, esS); b_tw = mk.buf("tw")
                    qbw = mk.sb("qbw", [128, 512], F32, esS); b_qbw = mk.buf("qbw")
                    for s_ in range(NS):
                        pb, bpb = getbank()
                        mk.op("pe", lambda h, s_=s_: h.matmul(pb[:, :], lhsT=csamp[0:NS, C_OH + s_ * 128:C_OH + (s_ + 1) * 128], rhs=zs[:, 0:512], start=True, stop=True),
                              reads=[b_csamp, b_zs], writes=[bpb])
                        evac(qbw[:], pb[:, :], [bpb], [b_qbw])
                        for g in range(2):
                            for hh in range(4):
                                qv = qbw[:, hh * 128 + g * 64:hh * 128 + g * 64 + 64]
                                mk.op("dve", lambda h, s_=s_, g=g, qv=qv: h.tensor_tensor(out=tw[:], in0=Wn[:, s_, :, g * 64:(g + 1) * 64], in1=qv.unsqueeze(1).to_broadcast([128, 4, 64]), op=ALU.mult),
                                      reads=[b_Wn, b_qbw], writes=[b_tw])
                                mk.op("dve", lambda h, g=g, hh=hh: h.tensor_reduce(out=swt[:, :, g * 4 + hh], in_=tw[:], axis=AX.X, op=ALU.add), reads=[b_tw], writes=[b_swt], add=True)
                        mk.op("act", lambda h, s_=s_: h.activation(out=Pz[:, s_, :, :, s_], in_=swt[:], func=AF.Exp, scale=SCALE), reads=[b_swt], writes=[b_Pz], add=True)
                        mk.op("dve", lambda h, s_=s_: h.memset(Pz[0:1, s_, 0, :, s_], 0.0), reads=[b_Pz], writes=[b_Pz], add=True)
                    pW, bpW = getdbl()
                    for gh in range(8):
                        g = gh // 4
                        first = True
                        for s_ in range(NS):
                            for c in range(4):
                                last = (s_ == NS - 1 and c == 3)
                                mk.op("pe", lambda h, gh=gh, g=g, s_=s_, c=c, first=first, last=last: h.matmul(
                                    pW[0:NS, g * 512 + (gh % 4) * 65:g * 512 + (gh % 4) * 65 + 65], lhsT=Pz[:, s_, c, gh, :], rhs=Vaug[:, s_, c, g, :], start=first, stop=last),
                                      reads=[b_Pz, b_Vaug], writes=[bpW[g]], sig=last, add=not (gh % 4 == 0 and first))
                                first = False
                    oW = mk.sb("oW", [NS, 2, 4, 65], F32, esS); b_oW = mk.buf("oW")
                    for g in range(2):
                        evac(oW[:, g, :, :], pW[0:NS, g * 512:g * 512 + 260].rearrange("p (a b) -> p a b", b=65), [bpW[g]], [b_oW], add=True)
                    pTs = mk.sb("pTs", [128, 8, 32], BF16, esS); b_pTs = mk.buf("pTs")
                    for c in range(8):
                        pb, bpb = getbank()
                        mk.op("pe", lambda h, c=c: h.transpose(out=pb[:, 0:32], in_=pall[:, c * 128:(c + 1) * 128], identity=identf[0:32, 0:32]), reads=[b_pall, b_identf], writes=[bpb])
                        evac(pTs[:, c, :], pb[:, 0:32], [bpb], [b_pTs], add=True)
                    pC, bpC = getdbl()
                    for sg_ in range(8):
                        s_, g = sg_ // 2, sg_ % 2
                        for c in range(8):
                            mk.op("pe", lambda h, sg_=sg_, s_=s_, g=g, c=c: h.matmul(pC[0:32, (sg_ // 4) * 512 + (sg_ % 4) * 64:(sg_ // 4) * 512 + (sg_ % 4) * 64 + 64], lhsT=pTs[:, c, :],
                                                                                   rhs=VCs[:, s_, c, g * 64:(g + 1) * 64], start=(c == 0), stop=(c == 7)),
                                  reads=[b_pTs, b_VCs[s_]], writes=[bpC[sg_ // 4]], sig=(c == 7), add=not (sg_ % 4 == 0 and c == 0))
                    ocx = mk.sb("ocx", [32, 8, 64], F32, esS); b_ocx = mk.buf("ocx")
                    for half in range(2):
                        evac(ocx[:, half * 4:(half + 1) * 4, :], pC[0:32, half * 512:half * 512 + 256].rearrange("p (a b) -> p a b", b=64), [bpC[half]], [b_ocx], add=True)
                    mk.op("dve", lambda h: h.tensor_tensor(out=ocx[:], in0=ocx[:], in1=csamp[0:32, C_DM:C_DM + 8].unsqueeze(2).to_broadcast([32, 8, 64]), op=ALU.mult),
                          reads=[b_ocx, b_csamp], writes=[b_ocx])
                    oc32 = mk.sb("oc32", [32, 64], F32, esS); b_oc32 = mk.buf("oc32")
                    mk.op("dve", lambda h: h.tensor_reduce(out=oc32[:], in_=ocx[:].rearrange("p a d -> p d a"), axis=AX.X, op=ALU.add), reads=[b_ocx], writes=[b_oc32])
                    pR, bpR = getbank()
                    for gh in range(8):
                        mk.op("pe", lambda h, gh=gh: h.matmul(pR[0:NS, gh * 64:(gh + 1) * 64], lhsT=csamp[0:32, C_SELGH + gh * 4:C_SELGH + gh * 4 + 4], rhs=oc32[:], start=True, stop=True),
                              reads=[b_csamp, b_oc32], writes=[bpR], sig=(gh == 7), add=(gh > 0))
                    sgs = mk.sb("sgs", [NS, 24], F32, esS); b_sgs = mk.buf("sgs")
                    mk.op("act", lambda h: h.activation(out=sgs[:], in_=zs[:, 1280:1304], func=AF.Sigmoid), reads=[b_zs], writes=[b_sgs])
                    cat_s = mk.sb("cat_s", [NS, D], F32, esS); b_cat_s = mk.buf("cat_s")
                    gv = sgs[:].rearrange("p (gh b) -> p gh b", b=3)
                    av = cat_s[:, 0:512].rearrange("p (gh d) -> p gh d", d=64)
                    mk.op("dve", lambda h: h.tensor_tensor(out=av, in0=pR[0:NS, :].rearrange("p (gh d) -> p gh d", d=64), in1=gv[:, :, 0:1].to_broadcast([NS, 8, 64]), op=ALU.mult),
                          reads=[bpR, b_sgs], writes=[b_cat_s])
                    fin_s = mk.sb("fin_s", [NS, 2, 4, 4], F32, esS); b_fin_s = mk.buf("fin_s")
                    tmo = mk.sb("tmo", [NS, 2, 4, 65], F32, esS); b_tmo = mk.buf("tmo")
                    for bi, (oX, b_oX) in enumerate(((oS, b_oS), (oW, b_oW))):
                        c0 = (768, 1024)[bi] + 128
                        vn = zs[:, c0:c0 + 128].rearrange("p (g d) -> p g d", g=2)
                        en = enew[:, bi, :, :].rearrange("p a g -> p g a")
                        mk.op("dve", lambda h, vn=vn, en=en: h.tensor_tensor(out=tmo[:, :, :, 0:64], in0=vn.unsqueeze(2).to_broadcast([NS, 2, 4, 64]),
                                                                         in1=en.unsqueeze(3).to_broadcast([NS, 2, 4, 64]), op=ALU.mult), reads=[b_zs, b_enew], writes=[b_tmo])
                        mk.op("dve", lambda h, en=en: h.tensor_copy(out=tmo[:, :, :, 64], in_=en), reads=[b_enew], writes=[b_tmo], add=True)
                        mk.op("dve", lambda h, oX=oX: h.tensor_tensor(out=oX[:], in0=oX[:], in1=tmo[:], op=ALU.add), reads=[b_oX, b_tmo], writes=[b_oX])
                        mk.op("dve", lambda h, oX=oX: h.reciprocal(out=fin_s[:, :, :, 0], in_=oX[:, :, :, 64]), reads=[b_oX], writes=[b_fin_s])
                        gsl = gv[:, :, bi + 1].rearrange("p (g a) -> p g a", g=2)
                        mk.op("dve", lambda h, gsl=gsl: h.tensor_tensor(out=fin_s[:, :, :, 1], in0=fin_s[:, :, :, 0], in1=gsl, op=ALU.mult), reads=[b_fin_s, b_sgs], writes=[b_fin_s])
                        mk.op("dve", lambda h, oX=oX: h.tensor_tensor(out=tmo[:, :, :, 0:64], in0=oX[:, :, :, 0:64], in1=fin_s[:, :, :, 1:2].to_broadcast([NS, 2, 4, 64]), op=ALU.mult),
                              reads=[b_oX, b_fin_s], writes=[b_tmo])
                        mk.op("dve", lambda h: h.tensor_tensor(out=av.rearrange("p (g a) d -> p g a d", g=2), in0=av.rearrange("p (g a) d -> p g a d", g=2), in1=tmo[:, :, :, 0:64], op=ALU.add),
                              reads=[b_cat_s, b_tmo], writes=[b_cat_s])
                    a_s = mk.sb("a_s", [NS, 512], F32, esS); b_a_s = mk.buf("a_s")
                    mk.op("act", lambda h: h.activation(out=a_s[:], in_=zs[:, 1816:2328], func=AF.Sigmoid), reads=[b_zs], writes=[b_a_s])
                    mk.op("dve", lambda h: h.tensor_tensor(out=a_s[:], in0=a_s[:], in1=zs[:, 1304:1816], op=ALU.mult), reads=[b_a_s, b_zs], writes=[b_a_s])
                    for s_ in range(NS):
                        mk.dma("sp", lambda h, s_=s_: h.dma_start(out=o_conv_s[s_, 29:30, :], in_=a_s[s_:s_ + 1, :]), reads=[b_a_s], is_out=True)
                    stc = mk.sb("stc", [120, 512], F32, esS); b_stc = mk.buf("stc")
                    w30 = mk.sb("w30", [120, 512], F32, esS); b_w30 = mk.buf("w30")
                    mk.dma("sp", lambda h: h.dma_start(out=stc[:], in_=sconv_d.rearrange("s k c -> (s k) c")), writes=[b_stc])
                    for s_ in range(NS):
                        mk.dma("sp", lambda h, s_=s_: h.dma_start(out=w30[s_ * 30:(s_ + 1) * 30, :], in_=conv_w[0:30, :]), writes=[b_w30], add=True)
                    cb4 = mk.sb("cb4", [NS, 4, 512], F32, esS); b_cb4 = mk.buf("cb4")
                    mk.dma("sp", lambda h: h.dma_start(out=cb4[:, 0, :], in_=conv_w[30:31, :].partition_broadcast(NS)), writes=[b_cb4], add=True)
                    for i_, v in enumerate((conv_b, ln_g, ln_b)):
                        mk.dma("sp", lambda h, i_=i_, v=v: h.dma_start(out=cb4[:, 1 + i_, :], in_=v.rearrange("(o c) -> o c", o=1).partition_broadcast(NS)), writes=[b_cb4], add=True)
                    mk.op("dve", lambda h: h.tensor_tensor(out=stc[:], in0=stc[:], in1=w30[:], op=ALU.mult), reads=[b_stc, b_w30], writes=[b_stc])
                    pY, bpY = getbank()
                    mk.op("pe", lambda h: h.matmul(pY[0:NS, :], lhsT=csamp[0:120, C_IND30:C_IND30 + 4], rhs=stc[:], start=True, stop=True), reads=[b_csamp, b_stc], writes=[bpY])
                    ys = mk.sb("ys", [NS, 512], F32, esS); b_ys = mk.buf("ys")
                    ysq = mk.sb("ysq", [NS, 512], F32, esS); b_ysq = mk.buf("ysq")
                    mk.op("dve", lambda h: h.tensor_tensor(out=ys[:], in0=a_s[:], in1=cb4[:, 0, :], op=ALU.mult), reads=[b_a_s, b_cb4], writes=[b_ys])
                    mk.op("dve", lambda h: h.tensor_tensor(out=ys[:], in0=ys[:], in1=pY[0:NS, :], op=ALU.add), reads=[b_ys, bpY], writes=[b_ys])
                    mk.op("dve", lambda h: h.tensor_tensor(out=ys[:], in0=ys[:], in1=cb4[:, 1, :], op=ALU.add), reads=[b_ys, b_cb4], writes=[b_ys])
                    lst = mk.sb("lst", [NS, 8], F32, esS); b_lst = mk.buf("lst")
                    mk.op("dve", lambda h: h.tensor_reduce(out=lst[:, 0:1], in_=ys[:], axis=AX.X, op=ALU.add), reads=[b_ys], writes=[b_lst])
                    mk.op("dve", lambda h: h.tensor_scalar(out=lst[:, 1:2], in0=lst[:, 0:1], scalar1=1.0 / 512, scalar2=None, op0=ALU.mult), reads=[b_lst], writes=[b_lst])
                    mk.op("dve", lambda h: h.tensor_scalar(out=ys[:], in0=ys[:], scalar1=lst[:, 1:2], scalar2=None, op0=ALU.subtract), reads=[b_ys, b_lst], writes=[b_ys])
                    mk.op("act", lambda h: h.activation(out=ysq[:], in_=ys[:], func=AF.Square, accum_out=lst[:, 2:3]), reads=[b_ys], writes=[b_ysq, b_lst])
                    mk.op("act", lambda h: h.activation(out=lst[:, 3:4], in_=lst[:, 2:3], func=AF.Sqrt, scale=1.0 / 512, bias=EPS), reads=[b_lst], writes=[b_lst])
                    mk.op("dve", lambda h: h.reciprocal(out=lst[:, 4:5], in_=lst[:, 3:4]), reads=[b_lst], writes=[b_lst])
                    mk.op("dve", lambda h: h.scalar_tensor_tensor(out=ys[:], in0=ys[:], scalar=lst[:, 4:5], in1=cb4[:, 2, :], op0=ALU.mult, op1=ALU.mult), reads=[b_ys, b_lst, b_cb4], writes=[b_ys])
                    mk.op("dve", lambda h: h.tensor_tensor(out=ys[:], in0=ys[:], in1=cb4[:, 3, :], op=ALU.add), reads=[b_ys, b_cb4], writes=[b_ys])
                    mk.op("act", lambda h: h.activation(out=cat_s[:, 512:1024], in_=ys[:], func=AF.Silu), reads=[b_ys], writes=[b_cat_s], add=True)
                    catb = mk.sb("catb", [NS, D], BF16, esS); b_catb = mk.buf("catb")
                    mk.op("act", lambda h: h.copy(out=catb[:], in_=cat_s[:]), reads=[b_cat_s], writes=[b_catb])
                    catTs = mk.sb("catTs", [128, 8, NS], BF16, esS); b_catTs = mk.buf("catTs")
                    transpose_to(catb[:], b_catb, catTs[:, :, :], b_catTs, 8, NS)
                    pM, bpM = getdbl()
                    for half in range(2):
                        for kc in range(8):
                            mk.op("pe", lambda h, kc=kc, half=half: h.matmul(pM[0:NS, half * 512:(half + 1) * 512], lhsT=catTs[:, kc, :], rhs=Wout[:, kc, half * 512:(half + 1) * 512],
                                                                           start=(kc == 0), stop=(kc == 7)), reads=[b_catTs, b_Wout], writes=[bpM[half]], sig=(kc == 7), add=(kc > 0))
                    mk.op("dve", lambda h: h.tensor_tensor(out=xs_sb[:], in0=xs_sb[:], in1=pM[0:NS, :], op=ALU.add), reads=[b_xs_sb] + bpM, writes=[b_xs_sb])
                    mk.dma("sp", lambda h: h.dma_start(out=xs_scr, in_=xs_sb[:]), reads=[b_xs_sb], writes=[b_xs_scr])
                mk.barrier()
            cmask = mk.sb("cmask", [128, 2, 128], BF16, esA); b_cmask = mk.buf("cmask")
            mk.dma("pool", lambda h: h.dma_start(out=cmask[:], in_=c_cmask), writes=[b_cmask])
            Eall = mk.sb("Eall", [128, 32, 128], BF16, esA); b_Eall = mk.buf("Eall")
            mk.dma("pool", lambda h: h.dma_start(out=Eall[0:64], in_=c_eall), writes=[b_Eall])
            mk.dma("pool", lambda h: h.dma_start(out=Eall[64:128], in_=c_eall), writes=[b_Eall], add=True)

            KTs = mk.sb("KTs", [128, T], BF16, esA); b_KTs = [mk.buf(f"KTs{i}") for i in range(NBLK)]
            KTw = mk.sb("KTw", [128, 2, TB], BF16, esA); b_KTw = [mk.buf(f"KTw{i}") for i in range(2)]
            Vs = mk.sb("Vs", [128, 32, 2, 65], BF16, esA); b_Vs = [mk.buf(f"Vs{i}") for i in range(NBLK)]
            Vw = mk.sb("Vw", [128, 8, 2, 65], BF16, esA); b_Vw = [mk.buf(f"Vw{i}") for i in range(2)]
            mk.op("pool", lambda h: h.memset(Vs[:, :, :, 64:65], 1.0), writes=b_Vs)
            mk.op("pool", lambda h: h.memset(Vw[:, :, :, 64:65], 1.0), writes=b_Vw)
            CKV = mk.sb("CKV", [128, 2, 16 + TB], BF16, esA); b_CKV = mk.buf("CKV")
            mk.op("pool", lambda h: h.memset(CKV[:, :, 0:16], 0.0), writes=[b_CKV])
            KCT = mk.sb("KCT", [128, 256], BF16, esA); b_KCT = mk.buf("KCT")
            GV = mk.sb("GV", [128, 256], BF16, esA); b_GV = mk.buf("GV")
            VC = mk.sb("VC", [128, 2, 2, 65], BF16, esA); b_VC = mk.buf("VC")
            mk.op("pool", lambda h: h.memset(KCT[:], 0.0), writes=[b_KCT])
            mk.op("pool", lambda h: h.memset(GV[:], 0.0), writes=[b_GV])
            mk.op("pool", lambda h: h.memset(VC[:], 0.0), writes=[b_VC])
            mk.op("pool", lambda h: h.memset(VC[:, :, :, 64:65], 1.0), writes=[b_VC])
            mk.op("pool", lambda h: h.memset(VC[0:1, 0, :, 64:65], 0.0), writes=[b_VC])
            gk = mk.sb("gk", [128, 32], BF16, esA); b_gk = mk.buf("gk")

            xt_r = Rot([(mk.sb(f"xt{i}", [128, D], F32, esA), mk.buf(f"xt{i}")) for i in range(2)])
            hb_r = Rot([(mk.sb(f"hb{i}", [128, D], BF16, esA), mk.buf(f"hb{i}")) for i in range(2)])
            st_r = Rot([(mk.sb(f"st{i}", [128, 4], F32, esA), mk.buf(f"st{i}")) for i in range(2)])
            hT_r = Rot([(mk.sb(f"hT{i}", [128, 8, TB], BF16, esA), mk.buf(f"hT{i}")) for i in range(1)])
            qT_r = Rot([(mk.sb(f"qT{i}", [128, 2, 4, TB], BF16, esA), mk.buf(f"qT{i}")) for i in range(1)])
            for qt_, bq_ in qT_r.items:
                mk.op("pool", lambda h, qt_=qt_: h.memset(qt_[:], 0.0), writes=[bq_])
            zkv_r = Rot([(mk.sb(f"zkv{i}", [128, 792], F32, esA), mk.buf(f"zkv{i}")) for i in range(2)])
            sg_r = Rot([(mk.sb(f"sg{i}", [128, 4, 24], F32, esA), mk.buf(f"sg{i}")) for i in range(2)])
            aT = mk.sb("aT", [128, 4, 30 + TB], F32, esA); b_aT = [mk.buf(f"aT{c}") for c in range(4)]
            mk.op("pool", lambda h: h.memset(aT[:, :, 0:30], 0.0), writes=b_aT)
            yc = mk.sb("yc", [128, 4, TB], F32, esA); b_yc = [mk.buf(f"yc{c}") for c in range(4)]
            tmp_r = Rot([(mk.sb(f"tmpc{i}", [128, TB], F32, esA), mk.buf(f"tmpc{i}")) for i in range(2)])
            mean_sb = mk.sb("mean_sb", [128, TB], F32, esA); b_mean = mk.buf("mean")
            rstd_sb = mk.sb("rstd_sb", [128, TB], F32, esA); b_rstd = mk.buf("rstd")
            catT_r = Rot([(mk.sb(f"catT{i}", [128, 8, TB], BF16, esA), mk.buf(f"catT{i}")) for i in range(1)])
            PT_r = Rot([(mk.sb(f"PT{i}", [128, 4, 128], BF16, esA), mk.buf(f"PT{i}")) for i in range(4)])
            eS = mk.sb("eS", [128, 4, 256], F32, esA); b_eS = mk.buf("eS")
            imp = mk.sb("imp", [128, 2, 260], F32, esA); b_imp = mk.buf("imp")
            mk.op("pool", lambda h: h.memset(imp[:], 0.0), writes=[b_imp])
            sst = mk.sb("sst", [128, 16], F32, esA); b_sst = mk.buf("sst")
            sA = mk.sb("sA", [128, 64], F32, esA); b_sA = mk.buf("sA")
            sc = mk.sb("sc", [128, 2, 64], F32, esA); b_sc = mk.buf("sc")
            sc2 = mk.sb("sc2", [128, 2, 64], F32, esA); b_sc2 = mk.buf("sc2")
            m8 = mk.sb("m8", [128, 16], F32, esA); b_m8 = mk.buf("m8")
            nm = mk.sb("nm", [128, 2, 64], BF16, esA); b_nm = mk.buf("nm")
            nmT_r = Rot([(mk.sb(f"nmT{i}", [128, 2, 128], BF16, esA), mk.buf(f"nmT{i}")) for i in range(2)])
            for nt_, bn_ in nmT_r.items:
                mk.op("pool", lambda h, nt_=nt_: h.memset(nt_[:], 0.0), writes=[bn_])
            ot_r = Rot([(mk.sb(f"ot{i}", [65, 512], F32, esA), mk.buf(f"ot{i}")) for i in range(2)])
            fin = mk.sb("fin", [128, 8], F32, esA); b_fin = mk.buf("fin")
            pvs_r = Rot([(mk.sb(f"pvs{i}", [128, 260], F32, esA), mk.buf(f"pvs{i}")) for i in range(2)])
            tmpo = mk.sb("tmpo", [128, 4, 64], F32, esA); b_tmpo = mk.buf("tmpo")
            acc_r = Rot([(mk.sb(f"acc{i}", [128, 8, 64], F32, esA), mk.buf(f"acc{i}")) for i in range(2)])
            abf_r = Rot([(mk.sb(f"abf{i}", [128, 512], BF16, esA), mk.buf(f"abf{i}")) for i in range(2)])
            cst = mk.sb("cst", [30, 512], F32, esA); b_cst = mk.buf("cst")

            print("phaseA sbuf remaining", nc.sbuf_bytes_remaining)

            def selection_tile(i, tb, qT, b_qT, nmT, b_nmT):
                for g in range(2):
                    psS, bS = getdbl()
                    for hh in range(4):
                        mk.op("pe", lambda h, hh=hh, g=g: h.matmul(psS[:, hh * 256:(hh + 1) * 256], lhsT=qT[:, g, hh, tb * 128:(tb + 1) * 128],
                                                                rhs=KCT[:, :], start=True, stop=True),
                              reads=[b_qT, b_KCT], writes=[bS[hh // 2]], sig=(hh % 2 == 1), add=(hh % 2 == 1))
                    mk.op("act", lambda h: h.activation(out=eS[:].rearrange("p a b -> p (a b)"), in_=psS[:], func=AF.Exp, scale=SCALE),
                          reads=bS, writes=[b_eS])
                    yield
                    chk(61)
                    mk.op("pool", lambda h: h.affine_select(out=eS[:], in_=eS[:], pattern=[[0, 4], [-16, 256]], compare_op=ALU.is_ge, fill=0.0,
                                                             base=128 * i - 15, channel_multiplier=1), reads=[b_eS], writes=[b_eS])
                    yield
                    mk.op("pool", lambda h: h.memset(eS[:, :, 0:1], 0.0), reads=[b_eS], writes=[b_eS])
                    yield
                    chk(62)
                    mk.op("dve", lambda h: h.tensor_reduce(out=sst[:, 0:4], in_=eS[:], axis=AX.X, op=ALU.add), reads=[b_eS], writes=[b_sst])
                    yield
                    mk.op("dve", lambda h: h.tensor_scalar_max(out=sst[:, 4:8], in0=sst[:, 0:4], scalar1=1e-30), reads=[b_sst], writes=[b_sst])
                    yield
                    mk.op("dve", lambda h: h.reciprocal(out=sst[:, 8:12], in_=sst[:, 4:8]), reads=[b_sst], writes=[b_sst])
                    yield
                    mk.op("dve", lambda h, g=g: h.tensor_scalar(out=imp[:, g, 0:256], in0=eS[:, 0, :], scalar1=sst[:, 8:9], scalar2=None, op0=ALU.mult),
                          reads=[b_eS, b_sst], writes=[b_imp])
                    yield
                    for hh in range(1, 4):
                        mk.op("dve", lambda h, g=g, hh=hh: h.scalar_tensor_tensor(out=imp[:, g, 0:256], in0=eS[:, hh, :], scalar=sst[:, 8 + hh:9 + hh],
                                                                               in1=imp[:, g, 0:256], op0=ALU.mult, op1=ALU.add),
                              reads=[b_eS, b_sst, b_imp], writes=[b_imp])
                        yield
                    v4 = imp[:, g, 0:256].rearrange("p (j r) -> p j r", r=4)
                    v4b = imp[:, g, 4:260].rearrange("p (j r) -> p j r", r=4)
                    mk.op("dve", lambda h, v4=v4: h.tensor_reduce(out=sA[:], in_=v4[:, :, 1:4], axis=AX.X, op=ALU.add), reads=[b_imp], writes=[b_sA])
                    yield
                    mk.op("dve", lambda h, g=g, v4=v4: h.scalar_tensor_tensor(out=sc[:, g, :], in0=sA[:], scalar=2.0, in1=v4[:, :, 0], op0=ALU.mult, op1=ALU.add),
                          reads=[b_sA, b_imp], writes=[b_sc])
                    yield
                    mk.op("dve", lambda h, g=g, v4b=v4b: h.tensor_tensor(out=sc[:, g, :], in0=sc[:, g, :], in1=v4b[:, :, 0], op=ALU.add),
                          reads=[b_sc, b_imp], writes=[b_sc])
                    yield
                chk(63)
                mk.op("dve", lambda h: h.memset(sc[:, :, 0:1], 1e9), reads=[b_sc], writes=[b_sc])
                yield
                for half in range(2):
                    cur = 2 * i + half
                    rows = slice(half * 64, (half + 1) * 64)
                    if cur < 63:
                        mk.op("dve", lambda h, cur=cur, rows=rows: h.memset(sc[rows, :, cur + 1:64], -2e9), reads=[b_sc], writes=[b_sc])
                        yield
                    if cur >= 1:
                        mk.op("dve", lambda h, cur=cur, rows=rows: h.memset(sc[rows, :, cur - 1:cur], 2e9), reads=[b_sc], writes=[b_sc])
                        yield
                    mk.op("dve", lambda h, cur=cur, rows=rows: h.memset(sc[rows, :, cur:cur + 1], 3e9), reads=[b_sc], writes=[b_sc])
                    yield
                chk(64)
                for g in range(2):
                    mk.op("dve", lambda h, g=g: h.max(out=m8[:, 0:8], in_=sc[:, g, :]), reads=[b_sc], writes=[b_m8])
                    yield
                    mk.op("dve", lambda h, g=g: h.match_replace(out=sc2[:, g, :], in_to_replace=m8[:, 0:8], in_values=sc[:, g, :], imm_value=-4e9),
                          reads=[b_sc, b_m8], writes=[b_sc2])
                    yield
                    mk.op("dve", lambda h, g=g: h.max(out=m8[:, 8:16], in_=sc2[:, g, :]), reads=[b_sc2], writes=[b_m8])
                    yield
                    mk.op("dve", lambda h, g=g: h.match_replace(out=sc2[:, g, :], in_to_replace=m8[:, 8:16], in_values=sc2[:, g, :], imm_value=-4e9),
                          reads=[b_sc2, b_m8], writes=[b_sc2])
                    yield
                mk.op("dve", lambda h: h.tensor_scalar(out=nm[:], in0=sc2[:], scalar1=-3.5e9, scalar2=NEG, op0=ALU.is_gt, op1=ALU.mult),
                      reads=[b_sc2], writes=[b_nm])
                yield
                chk(65)
                pt, bpt = PTR["r"].next()
                mk.op("pe", lambda h: h.transpose(out=pt[:, 0:128], in_=nm[:].rearrange("p a b -> p (a b)"), identity=ident[:]),
                      reads=[b_nm, b_ident], writes=[bpt])
                yield
                evac(nmT[0:64, 0, :], pt[0:64, 0:128], [bpt], [b_nmT], add=True)
                yield
                evac(nmT[64:128, 1, :], pt[64:128, 0:128], [bpt], [b_nmT], add=True)
                yield


            def pump(bg, n):
                while n > 0 and bg:
                    try:
                        next(bg[0])
                        n -= 1
                    except StopIteration:
                        bg.pop(0)

            def drain(bg):
                while bg:
                    pump(bg, 1000)

            def attention_units(i, tb, qT, b_qT, sg, b_sg, catT, b_catT, nmT, b_nmT, bg=None, bg_steps=0, bg2=None):
                acc, b_acc = acc_r.next()
                for g in range(2):
                    if g == 1:
                        chk(70)
                    units = []
                    chs = [0] if i < 16 else [0, 1]
                    for n_, ch in enumerate(chs):
                        msk = None
                        if not (ch == 0 and i >= 16):
                            msk = dict(pattern=[[0, 4], [1, 128]], base=128 * i - 15 - 2048 * ch, cm=-16)
                        units.append(dict(br=0, lhsT=KCT[:, ch * 128:(ch + 1) * 128], rk=[b_KCT], V=VC[:, ch, g, :], rv=[b_VC],
                                          emask=None, mask=msk, first=(n_ == 0), last=(n_ == len(chs) - 1)))
                    for kc in range(i + 1):
                        msk = None
                        cmk = 0 if kc == i else None
                        units.append(dict(cmk=cmk, br=1, lhsT=KTs[:, kc * 128:(kc + 1) * 128], rk=[b_KTs[kc // 4]], V=Vs[:, kc, g, :], rv=[b_Vs[kc // 4]],
                                          emask=kc, mask=msk, first=(kc == 0), last=(kc == i)))
                    k0 = max(0, i - 4)
                    for kc in range(k0, i + 1):
                        msk = None
                        cmk = 0 if kc == i else (1 if kc == i - 4 else None)
                        slot = (kc // 4) % 2
                        units.append(dict(cmk=cmk, br=2, lhsT=KTw[:, slot, (kc % 4) * 128:(kc % 4 + 1) * 128], rk=[b_KTw[slot]],
                                          V=Vw[:, kc % 8, g, :], rv=[b_Vw[slot]], emask=None, mask=msk, first=(kc == k0), last=(kc == i)))
                    rhs_q = qT[:, g, :, tb * 128:(tb + 1) * 128]
                    state = {}

                    def emit_S(u):
                        ps_, bps_ = S_r.next()
                        u["ps"] = ps_; u["bps"] = bps_
                        has_e = u["emask"] is not None
                        has_c = u.get("cmk") is not None
                        pv3 = ps_.rearrange("p (a b) -> p a b", b=128)
                        mk.op("pe", lambda h: h.matmul(pv3, lhsT=u["lhsT"], rhs=rhs_q, start=True, stop=not (has_e or has_c)),
                              reads=[b_qT] + u["rk"], writes=[bps_], sig=not (has_e or has_c))
                        if has_e:
                            kc_ = u["emask"]
                            mk.op("pe", lambda h: h.matmul(pv3, lhsT=Eall[:, kc_, :],
                                                           rhs=nmT[:, g, :].unsqueeze(1).to_broadcast([128, 4, 128]), start=False, stop=not has_c),
                                  reads=[b_Eall, b_nmT], writes=[bps_], sig=not has_c, add=True)
                        if has_c:
                            mk.op("pe", lambda h: h.matmul(pv3, lhsT=ident[:], rhs=cmask[:, u["cmk"], :].unsqueeze(1).to_broadcast([128, 4, 128]), start=False, stop=True),
                                  reads=[b_ident, b_cmask], writes=[bps_], sig=True, add=True)

                    def emit_rest(u):
                        PT, b_PT = PT_r.next()
                        mk.op("act", lambda h: h.activation(out=PT[:].rearrange("p a b -> p (a b)"), in_=u["ps"], func=AF.Exp, scale=SCALE),
                              reads=[u["bps"]], writes=[b_PT])
                        if u["mask"] is not None:
                            m_ = u["mask"]
                            mk.op("pool", lambda h: h.affine_select(out=PT[:], in_=PT[:], pattern=m_["pattern"], compare_op=ALU.is_ge, fill=0.0,
                                                                     base=m_["base"], channel_multiplier=m_["cm"]), reads=[b_PT], writes=[b_PT])
                        if u["first"]:
                            state["po"], state["bpo"] = po_r.next()
                        po, bpo = state["po"], state["bpo"]
                        mk.op("pe", lambda h: h.matmul(po[0:65, :], lhsT=u["V"], rhs=PT[:].rearrange("p a b -> p (a b)"), start=u["first"], stop=u["last"]),
                              reads=[b_PT] + u["rv"], writes=[bpo], sig=u["last"], add=not u["first"])
                        chk(68)
                        if u["last"]:
                            finalize(u["br"], po, bpo)
                            chk(69)

                    def finalize(br, po, bpo):
                        ot, b_ot = ot_r.next()
                        evac(ot[:], po[0:65, :], [bpo], [b_ot])
                        pq, bpq = getbank()
                        for hh in range(4):
                            mk.op("pe", lambda h, hh=hh: h.transpose(out=pq[:, hh * 65:(hh + 1) * 65], in_=ot[:, hh * 128:(hh + 1) * 128], identity=identf[0:65, 0:65]),
                                  reads=[b_ot, b_identf], writes=[bpq], sig=(hh == 3), add=(hh > 0))
                        chk(691)
                        pvs, bpq_s = pvs_r.next()
                        mk.op("act", lambda h: h.copy(out=pvs[:, 0:260], in_=pq[:, 0:260]), reads=[bpq], writes=[bpq_s])
                        bpq = bpq_s
                        pv = pvs[:, 0:260].rearrange("p (a c) -> p a c", c=65)
                        if br == 0:
                            mk.op("dve", lambda h: h.tensor_scalar_max(out=fin[:, 0:4], in0=pv[:, :, 64], scalar1=1e-30), reads=[bpq], writes=[b_fin])
                            mk.op("dve", lambda h: h.reciprocal(out=fin[:, 4:8], in_=fin[:, 0:4]), reads=[b_fin], writes=[b_fin])
                        else:
                            mk.op("dve", lambda h: h.reciprocal(out=fin[:, 4:8], in_=pv[:, :, 64]), reads=[bpq], writes=[b_fin])
                        chk(692)
                        gsl = sg[:, tb, g * 12:(g + 1) * 12].rearrange("p (a c) -> p a c", c=3)[:, :, br]
                        mk.op("dve", lambda h: h.tensor_tensor(out=fin[:, 4:8], in0=fin[:, 4:8], in1=gsl, op=ALU.mult), reads=[b_fin, b_sg], writes=[b_fin])
                        chk(693)
                        fb = fin[:, 4:8].unsqueeze(2).to_broadcast([128, 4, 64])
                        if br == 0:
                            mk.op("dve", lambda h: h.tensor_tensor(out=acc[:, g * 4:(g + 1) * 4, :], in0=pv[:, :, 0:64], in1=fb, op=ALU.mult),
                                  reads=[bpq, b_fin], writes=[b_acc], add=(g == 1))
                        else:
                            mk.op("dve", lambda h: h.tensor_tensor(out=tmpo[:], in0=pv[:, :, 0:64], in1=fb, op=ALU.mult), reads=[bpq, b_fin], writes=[b_tmpo])
                            mk.op("dve", lambda h: h.tensor_tensor(out=acc[:, g * 4:(g + 1) * 4, :], in0=acc[:, g * 4:(g + 1) * 4, :], in1=tmpo[:], op=ALU.add),
                                  reads=[b_acc, b_tmpo], writes=[b_acc])

                    LOOK = 2
                    for k in range(min(LOOK, len(units))):
                        emit_S(units[k])
                    chk(67)
                    if g == 1:
                        chk(71)
                    per = -(-bg_steps // max(1, 2 * len(units))) if bg else 0
                    for k, u in enumerate(units):
                        if k + LOOK < len(units):
                            emit_S(units[k + LOOK])
                        emit_rest(u)
                        if bg:
                            pump(bg, per)
                        if bg2 and k % 3 == 2:
                            pump(bg2, 1)
                chk(72)
                abf, b_abf = abf_r.next()
                mk.op("act", lambda h: h.copy(out=abf[:], in_=acc[:].rearrange("p a b -> p (a b)")), reads=[b_acc], writes=[b_abf])
                transpose_to(abf[:], b_abf, catT[:, 0:4, tb * 128:(tb + 1) * 128], b_catT, 4)

            def prep_gen(nb, hT, b_hT):
                prev = None
                for tb in range(4):
                    xt, b_xt = xt_r.next()
                    r0 = nb * TB + tb * 128
                    mk.dma("sp", lambda h: h.dma_start(out=xt[:], in_=xp[r0:r0 + 128, :]), writes=[b_xt])
                    hb, b_hb = hb_r.next()
                    st, b_st = st_r.next()
                    rmsnorm_tile(xt[:], 128, gbA[:, 0, :], [b_xt, b_gbA], (st, b_st), out_bf=hb[:], b_bf=b_hb)
                    yield
                    if prev is not None:
                        transpose_to(prev[0][:], prev[1], hT[:, :, prev[2] * 128:(prev[2] + 1) * 128], b_hT, 8)
                        yield
                    prev = (hb, b_hb, tb)
                transpose_to(prev[0][:], prev[1], hT[:, :, prev[2] * 128:(prev[2] + 1) * 128], b_hT, 8)
                yield

            for blk in range(NBLK):
                t0 = blk * TB
                hT, b_hT = hT_r.next()
                if blk == 0:
                    drain([prep_gen(0, hT, b_hT)])
                chk(2)
                qT, b_qT = qT_r.next()

                def proj_fm(col0, M, dst, dst_buf, add=True, rows=None):
                    pb, bpb = getbank()
                    for kc in range(8):
                        mk.op("pe", lambda h, kc=kc: h.matmul(pb[0:M, :], lhsT=Win[:, kc, col0:col0 + M], rhs=hT[:, kc, :], start=(kc == 0), stop=(kc == 7)),
                              reads=[b_Win, b_hT], writes=[bpb], sig=(kc == 7), add=(kc > 0))
                    evac(dst, pb[0:M, :] if rows is None else pb[rows, :], [bpb], [dst_buf], add=add)
                    return pb, bpb

                for hh in range(4):
                    pbq, bpbq = proj_fm(hh * 128, 128, qT[0:64, 0, hh, :], b_qT, rows=slice(0, 64))
                    evac(qT[64:128, 1, hh, :], pbq[64:128, :], [bpbq], [b_qT], add=True)
                proj_fm(768, 128, KTs[:, t0:t0 + TB], b_KTs[blk])
                proj_fm(1024, 128, KTw[:, blk % 2, :], b_KTw[blk % 2])
                proj_fm(512, 128, CKV[:, 0, 16:16 + TB], b_CKV)
                proj_fm(512 + 128, 128, CKV[:, 1, 16:16 + TB], b_CKV)
                sg, b_sg = sg_r.next()
                for tb in range(4):
                    pz, bz = getdbl()
                    for (c0, n, half) in ((512, 512, 0), (1024, 280, 1)):
                        for kc in range(8):
                            mk.op("pe", lambda h, kc=kc, c0=c0, n=n, half=half, tb=tb: h.matmul(
                                pz[:, half * 512:half * 512 + n], lhsT=hT[:, kc, tb * 128:(tb + 1) * 128], rhs=Win[:, kc, c0:c0 + n],
                                start=(kc == 0), stop=(kc == 7)), reads=[b_Win, b_hT], writes=[bz[half]], sig=(kc == 7), add=(kc > 0))
                    zkv, b_zkv = zkv_r.next()
                    evac(zkv[:], pz[:, 0:792], bz, [b_zkv])
                    ch = blk * 4 + tb
                    mk.op("dve", lambda h, ch=ch, zkv=zkv: h.tensor_copy(out=Vs[:, ch, :, 0:64], in_=zkv[:, 384:512].rearrange("p (g d) -> p g d", d=64)),
                          reads=[b_zkv], writes=[b_Vs[blk]], add=True)
                    mk.op("pool", lambda h, ch=ch, zkv=zkv: h.tensor_copy(out=Vw[:, ch % 8, :, 0:64], in_=zkv[:, 640:768].rearrange("p (g d) -> p g d", d=64)),
                          reads=[b_zkv], writes=[b_Vw[blk % 2]], add=True)
                    mk.op("act", lambda h, tb=tb, zkv=zkv: h.activation(out=sg[:, tb, :], in_=zkv[:, 768:792], func=AF.Sigmoid), reads=[b_zkv], writes=[b_sg], add=True)
                    r0 = t0 + tb * 128
                    mk.dma("sp", lambda h, r0=r0, zkv=zkv: h.dma_start(out=o_cmp_p[r0:r0 + 128, :], in_=zkv[:, 0:256]), reads=[b_zkv], is_out=True)
                    mk.dma("sp", lambda h, r0=r0, zkv=zkv: h.dma_start(out=o_slc_p[r0:r0 + 128, :], in_=zkv[:, 256:512]), reads=[b_zkv], is_out=True)
                    if blk == NBLK - 1:
                        mk.dma("sp", lambda h, tb=tb, zkv=zkv: h.dma_start(out=o_win_p[tb * 128:(tb + 1) * 128, :], in_=zkv[:, 512:768]), reads=[b_zkv], is_out=True)
                chk(3)
                for j in range(2):
                    pb, bpb = getbank()
                    for l in range(32):
                        mk.op("pe", lambda h, j=j, l=l: h.matmul(pb[:, 0:32], lhsT=W1[j][:, l, :], rhs=CKV[:, j, l:l + 497:16], start=(l == 0), stop=(l == 31)),
                              reads=[b_W1, b_CKV], writes=[bpb], sig=(l == 31), add=(l > 0))
                    if j == 0:
                        mk.op("act", lambda h, pb=pb: h.activation(out=gk[:], in_=pb[:, 0:32], func=AF.Gelu, bias=postt[:, 0:1]), reads=[bpb, b_post], writes=[b_gk])
                        pb2, bpb2 = getbank()
                        mk.op("pe", lambda h, pb2=pb2: h.matmul(pb2[:, 0:32], lhsT=W2[0][:], rhs=gk[:], start=True, stop=True), reads=[b_W2, b_gk], writes=[bpb2])
                        evac(KCT[:, 32 * blk:32 * blk + 32], pb2[:, 0:32], [bpb2], [b_KCT])
                    else:
                        mk.op("act", lambda h, pb=pb: h.activation(out=GV[:, 32 * blk:32 * blk + 32], in_=pb[:, 0:32], func=AF.Gelu, bias=postt[:, 1:2]),
                              reads=[bpb, b_post], writes=[b_GV])
                        ch = blk // 4
                        if blk == 0:
                            mk.op("dve", lambda h: h.memset(GV[:, 0:1], 0.0), reads=[b_GV], writes=[b_GV])
                        pb2, bpb2 = getbank()
                        mk.op("pe", lambda h, pb2=pb2, ch=ch: h.matmul(pb2[:, 0:128], lhsT=GV[:, ch * 128:(ch + 1) * 128], rhs=W2[1][:], start=True, stop=True),
                              reads=[b_W2, b_GV], writes=[bpb2])
                        evac(VC[:, ch, :, 0:64], pb2[:, 0:128].rearrange("p (g d) -> p g d", d=64), [bpb2], [b_VC])
                if blk == 0:
                    mk.op("dve", lambda h: h.memset(KCT[:, 0:1], 0.0), reads=[b_KCT], writes=[b_KCT])
                mk.op("dve", lambda h: h.tensor_copy(out=CKV[:, :, 0:16], in_=CKV[:, :, TB:TB + 16]), reads=[b_CKV], writes=[b_CKV])
                chk(4)
                catT, b_catT = catT_r.next()

                def conv_glu(c):
                    pa, bpa = getbank()
                    pg_, bpg = getbank()
                    for (pb_, bb_, col0) in ((pa, bpa, 1304 + c * 128), (pg_, bpg, 1304 + 512 + c * 128)):
                        for kc in range(8):
                            mk.op("pe", lambda h, kc=kc, pb_=pb_, col0=col0: h.matmul(pb_[:, :], lhsT=Win[:, kc, col0:col0 + 128], rhs=hT[:, kc, :],
                                                                                    start=(kc == 0), stop=(kc == 7)),
                                  reads=[b_Win, b_hT], writes=[bb_], sig=(kc == 7), add=(kc > 0))
                    tmpc, b_tmpc = tmp_r.next()
                    mk.op("act", lambda h: h.activation(out=tmpc[:], in_=pg_[:, :], func=AF.Sigmoid), reads=[bpg], writes=[b_tmpc])
                    mk.op("dve", lambda h: h.tensor_tensor(out=aT[:, c, 30:30 + TB], in0=pa[:, :], in1=tmpc[:], op=ALU.mult),
                          reads=[bpa, b_tmpc], writes=[b_aT[c]])

                def conv_taps(c):
                    mk.op("dve", lambda h: h.tensor_scalar(out=yc[:, c, :], in0=aT[:, c, 0:TB], scalar1=cw[:, c, 0:1], scalar2=cvec[:, 0, c:c + 1],
                                                           op0=ALU.mult, op1=ALU.add), reads=[b_aT[c], b_cw, b_cvec], writes=[b_yc[c]])
                    yield
                    for k in range(1, 31):
                        mk.op("dve", lambda h, k=k: h.scalar_tensor_tensor(out=yc[:, c, :], in0=aT[:, c, k:k + TB], scalar=cw[:, c, k:k + 1], in1=yc[:, c, :],
                                                                        op0=ALU.mult, op1=ALU.add), reads=[b_aT[c], b_cw, b_yc[c]], writes=[b_yc[c]])
                        yield

                def conv_finish():
                    if blk == NBLK - 1:
                        pq, bpq = getbank()
                        for c in range(4):
                            mk.op("pe", lambda h, c=c: h.transpose(out=pq[0:30, c * 128:(c + 1) * 128], in_=aT[:, c, TB:TB + 30], identity=identf[:]),
                                  reads=[b_aT[c], b_identf], writes=[bpq], sig=(c == 3), add=(c > 0))
                        evac(cst[:], pq[0:30, :], [bpq], [b_cst])
                        mk.dma("sp", lambda h: h.dma_start(out=o_conv_p, in_=cst[:]), reads=[b_cst], is_out=True)
                    for c in range(4):
                        mk.op("pool", lambda h, c=c: h.tensor_copy(out=aT[:, c, 0:30], in_=aT[:, c, TB:TB + 30]), reads=[b_aT[c]], writes=[b_aT[c]])
                    p1, bp1 = getbank()
                    for c in range(4):
                        mk.op("pe", lambda h, c=c: h.matmul(p1[:, :], lhsT=onesf[:], rhs=yc[:, c, :], start=(c == 0), stop=(c == 3)),
                              reads=[b_onesf, b_yc[c]], writes=[bp1], sig=(c == 3), add=(c > 0))
                    p2, bp2 = getbank()
                    for c in range(4):
                        tmpc, b_tmpc = tmp_r.next()
                        mk.op("act", lambda h, c=c, tmpc=tmpc: h.activation(out=tmpc[:], in_=yc[:, c, :], func=AF.Square), reads=[b_yc[c]], writes=[b_tmpc])
                        mk.op("pe", lambda h, c=c, tmpc=tmpc: h.matmul(p2[:, :], lhsT=onesf[:], rhs=tmpc[:], start=(c == 0), stop=(c == 3)),
                              reads=[b_onesf, b_tmpc], writes=[bp2], sig=True, add=(c > 0))
                    mk.op("act", lambda h: h.mul(out=mean_sb[:], in_=p1[:, :], mul=1.0 / 512), reads=[bp1], writes=[b_mean])
                    tmpc, b_tmpc = tmp_r.next()
                    mk.op("dve", lambda h: h.tensor_tensor(out=tmpc[:], in0=mean_sb[:], in1=mean_sb[:], op=ALU.mult), reads=[b_mean], writes=[b_tmpc])
                    mk.op("dve", lambda h: h.scalar_tensor_tensor(out=rstd_sb[:], in0=p2[:, :], scalar=1.0 / 512, in1=tmpc[:], op0=ALU.mult, op1=ALU.subtract),
                          reads=[bp2, b_tmpc], writes=[b_rstd])
                    mk.op("act", lambda h: h.activation(out=rstd_sb[:], in_=rstd_sb[:], func=AF.Sqrt, bias=EPS), reads=[b_rstd], writes=[b_rstd])
                    mk.op("dve", lambda h: h.reciprocal(out=rstd_sb[:], in_=rstd_sb[:]), reads=[b_rstd], writes=[b_rstd])
                    for c in range(4):
                        mk.op("dve", lambda h, c=c: h.tensor_tensor(out=yc[:, c, :], in0=yc[:, c, :], in1=mean_sb[:], op=ALU.subtract), reads=[b_yc[c], b_mean], writes=[b_yc[c]])
                        mk.op("dve", lambda h, c=c: h.tensor_tensor(out=yc[:, c, :], in0=yc[:, c, :], in1=rstd_sb[:], op=ALU.mult), reads=[b_yc[c], b_rstd], writes=[b_yc[c]])
                        mk.op("act", lambda h, c=c: h.activation(out=catT[:, 4 + c, :], in_=yc[:, c, :], func=AF.Silu, scale=cvec[:, 1, c:c + 1], bias=cvec[:, 2, c:c + 1]),
                              reads=[b_yc[c], b_cvec], writes=[b_catT], add=True)

                chk(5)
                nm_cur = nmT_r.next()
                drain([selection_tile(blk * 4, 0, qT, b_qT, nm_cur[0], nm_cur[1])])
                for c in range(4):
                    conv_glu(c)
                bg2 = [prep_gen(blk + 1, hT, b_hT)] if blk + 1 < NBLK else []
                for tb in range(4):
                    bg = []
                    nm_next = None
                    steps = 31
                    if tb < 3:
                        nm_next = nmT_r.next()
                        bg.append(selection_tile(blk * 4 + tb + 1, tb + 1, qT, b_qT, nm_next[0], nm_next[1]))
                        steps += 75
                    bg.append(conv_taps(tb))
                    attention_units(blk * 4 + tb, tb, qT, b_qT, sg, b_sg, catT, b_catT, nm_cur[0], nm_cur[1], bg, steps, bg2)
                    drain(bg)
                    nm_cur = nm_next
                    chk(6)
                drain(bg2)
                conv_finish()
                chk(73)
                for tb in range(4):
                    po, bo = getdbl()
                    for half in range(2):
                        for kc in range(8):
                            mk.op("pe", lambda h, kc=kc, half=half, tb=tb: h.matmul(po[:, half * 512:(half + 1) * 512], lhsT=catT[:, kc, tb * 128:(tb + 1) * 128],
                                                                                   rhs=Wout[:, kc, half * 512:(half + 1) * 512], start=(kc == 0), stop=(kc == 7)),
                                  reads=[b_Wout, b_catT], writes=[bo[half]], sig=(kc == 7), add=(kc > 0))
                    xt, b_xt = xt_r.next()
                    r0 = t0 + tb * 128
                    mk.dma("sp", lambda h, r0=r0, xt=xt: h.dma_start(out=xt[:], in_=xp[r0:r0 + 128, :]), writes=[b_xt])
                    mk.op("dve", lambda h, xt=xt, po=po: h.tensor_tensor(out=xt[:], in0=xt[:], in1=po[:], op=ALU.add), reads=[b_xt] + bo, writes=[b_xt])
                    mk.dma("sp", lambda h, r0=r0, xt=xt: h.dma_start(out=xs1[r0:r0 + 128, :], in_=xt[:]), reads=[b_xt], writes=[bxs1[blk]], add=True)
                chk(7)
                chk(100 + blk)
        mk.barrier()
        chk(8)

        FB = 256
        esB = ExitStack()
        with esB:
            PTR["r"] = setup_psum(esB, 6)
            Wg = mk.sb("Wg", [128, 8, DFF], BF16, esB); Wu = mk.sb("Wu", [128, 8, DFF], BF16, esB); Wd = mk.sb("Wd", [128, NF, D], BF16, esB)
            b_Wg = [mk.buf(f"Wg{i}") for i in range(8)]; b_Wu = [mk.buf(f"Wu{i}") for i in range(8)]; b_Wd = [mk.buf(f"Wd{i}") for i in range(NF)]

            def load_ffn(l):
                for kc in range(8):
                    mk.dma("pool", lambda h, kc=kc: h.dma_start(out=Wg[:, kc, :], in_=wg_d[l, kc * 128:(kc + 1) * 128, :]), writes=[b_Wg[kc]])
                    mk.dma("pool", lambda h, kc=kc: h.dma_start(out=Wu[:, kc, :], in_=wu_d[l, kc * 128:(kc + 1) * 128, :]), writes=[b_Wu[kc]])
                for f in range(NF):
                    mk.dma("pool", lambda h, f=f: h.dma_start(out=Wd[:, f, :], in_=wd_d[l, f * 128:(f + 1) * 128, :]), writes=[b_Wd[f]])

            gbB = mk.sb("gbB", [128, 2, D], F32, esB); b_gbB = mk.buf("gbB")
            PW = mk.sb("PW", [128, 4, 2, 256], BF16, esB); b_PW = mk.buf("PW")
            for g in range(4):
                mk.dma("pool", lambda h, g=g: h.dma_start(out=PW[:, g, :, :], in_=pool_w[g].rearrange("(j p) e -> p j e", p=128)), writes=[b_PW], add=(g > 0))
            band = mk.sb("band", [128, 20, 128], F32, esB); b_band = mk.buf("band")
            mk.dma("sp", lambda h: h.dma_start(out=band[:], in_=c_band.rearrange("n p t -> p n t")), writes=[b_band])
            psb = mk.sb("psb", [128, D], F32, esB); b_psb = mk.buf("psb")
            mk.dma("sp", lambda h: h.dma_start(out=psb[:], in_=pool_scale[0:1, :].partition_broadcast(128)), writes=[b_psb])

            xt_r = Rot([(mk.sb(f"xB{i}", [128, D], F32, esB), mk.buf(f"xB{i}")) for i in range(4)])
            hb_rB = Rot([(mk.sb(f"hbB{i}", [128, D], BF16, esB), mk.buf(f"hbB{i}")) for i in range(2)])
            st_r = Rot([(mk.sb(f"stB{i}", [128, 4], F32, esB), mk.buf(f"stB{i}")) for i in range(2)])
            hTB = mk.sb("hTB", [128, 8, FB], BF16, esB); b_hTB = mk.buf("hTB")
            hid = mk.sb("hid", [128, NF, FB], BF16, esB); b_hid = [mk.buf(f"hid{f}") for f in range(NF)]
            sil_r = Rot([(mk.sb(f"sil{i}", [128, FB], F32, esB), mk.buf(f"sil{i}")) for i in range(1)])
            h3_r = Rot([(mk.sb(f"h3_{i}", [128, D], F32, esB), mk.buf(f"h3_{i}")) for i in range(2)])
            zT = mk.sb("zT", [128, 8, 128], BF16, esB); b_zT = mk.buf("zT")
            tmpx = mk.sb("tmpx", [128, 512], F32, esB); b_tmpx = mk.buf("tmpx")

            print("phaseB sbuf remaining", nc.sbuf_bytes_remaining)

            def ffn_prep_norm(tiles, npart, gslot):
                hbs = []
                for ti, (x_ap, b_x) in enumerate(tiles):
                    st = st_r.next()
                    hb, b_hb = hb_rB.next()
                    rmsnorm_tile(x_ap, npart, gbB[0:npart, gslot, :], [b_x, b_gbB], st, out_bf=hb[0:npart, :], b_bf=b_hb)
                    hbs.append((hb, b_hb))
                return hbs

            def ffn_prep_T(ti, hb, b_hb, npart):
                transpose_to(hb[0:npart, :], b_hb, hTB[:, :, ti * 128:ti * 128 + npart], b_hTB, 8, npart)

            def ffn_mm(ntok):
                for f in range(NF):
                    pb_, bb_ = getbank()
                    for wi, (W_, bW_) in enumerate(((Wg, b_Wg), (Wu, b_Wu))):
                        for kc in range(8):
                            mk.op("pe", lambda h, kc=kc, W_=W_, f=f, wi=wi: h.matmul(pb_[:, wi * 256:wi * 256 + ntok], lhsT=W_[:, kc, f * 128:(f + 1) * 128],
                                                                                    rhs=hTB[:, kc, 0:ntok], start=(kc == 0), stop=(kc == 7)),
                                  reads=[bW_[kc], b_hTB], writes=[bb_], sig=(kc == 7), add=(wi + kc > 0))
                    sil, b_sil = sil_r.next()
                    mk.op("act", lambda h: h.activation(out=sil[:, 0:ntok], in_=pb_[:, 0:ntok], func=AF.Silu), reads=[bb_], writes=[b_sil])
                    mk.op("dve", lambda h: h.tensor_tensor(out=hid[:, f, 0:ntok], in0=pb_[:, 256:256 + ntok], in1=sil[:, 0:ntok], op=ALU.mult),
                          reads=[bb_, b_sil], writes=[b_hid[f]])

            def ffn_gateup(tiles, npart, gslot):
                ntok = len(tiles) * 128 if npart == 128 else npart
                hbs = ffn_prep_norm(tiles, npart, gslot)
                for ti, (hb, b_hb) in enumerate(hbs):
                    ffn_prep_T(ti, hb, b_hb, npart)
                ffn_mm(ntok)

            def ffn_down(ti, x_ap, b_x, npart):
                po, bo = getdbl()
                for half in range(2):
                    for f in range(NF):
                        mk.op("pe", lambda h, f=f, half=half: h.matmul(po[0:npart, half * 512:(half + 1) * 512], lhsT=hid[:, f, ti * 128:ti * 128 + npart],
                                                                     rhs=Wd[:, f, half * 512:(half + 1) * 512], start=(f == 0), stop=(f == NF - 1)),
                              reads=[b_Wd[f], b_hid[f]], writes=[bo[half]], sig=(f == NF - 1), add=(f > 0))
                mk.op("dve", lambda h: h.tensor_tensor(out=x_ap, in0=x_ap, in1=po[0:npart, :], op=ALU.add), reads=[b_x] + bo, writes=[b_x])

            def pool_tile(x_ap, b_x, npart, h3, b_h3, h3p_ap, b_h3p, kp, bands):
                pz, bz = getdbl()
                for c in range(8):
                    gidx = c // 2
                    mk.op("pe", lambda h, c=c, gidx=gidx: h.matmul(pz[:, c * 128:c * 128 + npart], lhsT=h3[0:npart, c * 128:(c + 1) * 128],
                                                                   rhs=band[0:npart, bands[0] + gidx, 0:npart], start=True, stop=(h3p_ap is None)),
                          reads=[b_h3, b_band], writes=[bz[c // 4]], sig=(h3p_ap is None and c % 4 == 3), add=(c % 4 > 0))
                    if h3p_ap is not None:
                        mk.op("pe", lambda h, c=c, gidx=gidx: h.matmul(pz[:, c * 128:c * 128 + npart], lhsT=h3p_ap[:, c * 128:(c + 1) * 128],
                                                                       rhs=band[128 - kp:128, bands[1] + gidx, 0:npart] if kp == 128 else band[0:kp, bands[1] + gidx, 0:npart],
                                                                       start=False, stop=True),
                              reads=[b_h3p, b_band], writes=[bz[c // 4]], sig=(c % 4 == 3), add=True)
                src = pz[:].rearrange("p (c t) -> p c t", t=128)[:, :, 0:npart]
                evac(zT[:, :, 0:npart], src, bz, [b_zT])
                py, by = getdbl()
                for g in range(4):
                    for j in range(2):
                        mk.op("pe", lambda h, g=g, j=j: h.matmul(py[0:npart, g * 256:(g + 1) * 256], lhsT=zT[:, 2 * g + j, 0:npart], rhs=PW[:, g, j, :],
                                                                 start=(j == 0), stop=(j == 1)),
                              reads=[b_zT, b_PW], writes=[by[g // 2]], sig=(j == 1 and g % 2 == 1), add=(g % 2 == 1 or j == 1))
                for hf in range(2):
                    cs = slice(hf * 512, (hf + 1) * 512)
                    mk.op("dve", lambda h, cs=cs: h.tensor_tensor(out=tmpx[0:npart, :], in0=py[0:npart, cs], in1=psb[0:npart, cs], op=ALU.mult), reads=[by[hf], b_psb], writes=[b_tmpx])
                    mk.op("dve", lambda h, cs=cs: h.tensor_tensor(out=x_ap[:, cs], in0=x_ap[:, cs], in1=tmpx[0:npart, :], op=ALU.add), reads=[b_x, b_tmpx], writes=[b_x])

            load_ffn(0)
            load_gain(gbB, b_gbB, 0, 1)
            load_gain(gbB, b_gbB, 1, 2)
            if do_samples:
                h3s, b_h3s = h3_r.next()
                SPt, b_SPt = xt_r.next()
                mk.dma("sp", lambda h: h.dma_start(out=SPt[0:60, :], in_=spool_d.rearrange("s k d -> (s k) d")), writes=[b_SPt])
                xsB, b_x_s = xt_r.next()
                x_s = xsB[0:NS, :]
                mk.dma("sp", lambda h: h.dma_start(out=x_s, in_=xs_scr), reads=[b_xs_scr], writes=[b_x_s])
                ffn_gateup([(x_s, b_x_s)], NS, 0)
                ffn_down(0, x_s, b_x_s, NS)
                rmsnorm_tile(x_s, NS, gbB[0:NS, 1, :], [b_x_s, b_gbB], st_r.next(), out_f32=h3s[0:NS, :], b_f32=b_h3s)
                for s_ in range(NS):
                    mk.dma("sp", lambda h, s_=s_: h.dma_start(out=o_pool_s[s_, 14:15, :], in_=h3s[s_:s_ + 1, :]), reads=[b_h3s], is_out=True)
                pool_tile(x_s, b_x_s, NS, h3s, b_h3s, SPt[0:60, :], b_SPt, 60, (12, 16))
                mk.dma("sp", lambda h: h.dma_start(out=xs_scr, in_=x_s), reads=[b_x_s], writes=[b_xs_scr])
            h3prev = None
            NFB = T // FB

            def load_block(fb, src, bsrc):
                tiles = []
                for ti in range(2):
                    xt, b_xt = xt_r.next()
                    r0 = fb * FB + ti * 128
                    mk.dma("sp", lambda h, xt=xt, r0=r0: h.dma_start(out=xt[:], in_=src[r0:r0 + 128, :]), reads=[bsrc[r0 // TB]], writes=[b_xt])
                    tiles.append((xt, b_xt))
                return tiles

            tiles = load_block(0, xs1, bxs1)
            hbs = ffn_prep_norm([(xt[:], b_xt) for xt, b_xt in tiles], 128, 0)
            for ti, (hb, b_hb) in enumerate(hbs):
                ffn_prep_T(ti, hb, b_hb, 128)
            for fb in range(NFB):
                t0 = fb * FB
                nxt = None
                if fb + 1 < NFB:
                    nxt = load_block(fb + 1, xs1, bxs1)
                    nhbs = ffn_prep_norm([(xt[:], b_xt) for xt, b_xt in nxt], 128, 0)
                ffn_mm(FB)
                for ti, (xt, b_xt) in enumerate(tiles):
                    ffn_down(ti, xt[:], b_xt, 128)
                    if nxt is not None:
                        ffn_prep_T(ti, nhbs[ti][0], nhbs[ti][1], 128)
                    h3, b_h3 = h3_r.next()
                    st = st_r.next()
                    rmsnorm_tile(xt[:], 128, gbB[:, 1, :], [b_xt, b_gbB], st, out_f32=h3[:], b_f32=b_h3)
                    i = fb * 2 + ti
                    if i == 31:
                        mk.dma("sp", lambda h, h3=h3: h.dma_start(out=o_pool_p, in_=h3[113:128, :]), reads=[b_h3], is_out=True)
                    if i == 0:
                        pool_tile(xt[:], b_xt, 128, h3, b_h3, None, None, 0, (8, None))
                    else:
                        pool_tile(xt[:], b_xt, 128, h3, b_h3, h3prev[0][:], h3prev[1], 128, (0, 4))
                    h3prev = (h3, b_h3)
                    r0 = t0 + ti * 128
                    mk.dma("sp", lambda h, xt=xt, r0=r0: h.dma_start(out=xs3[r0:r0 + 128, :], in_=xt[:]), reads=[b_xt], writes=[bxs3[r0 // TB]], add=True)
                tiles = nxt
            chk(9)
            load_ffn(1)
            load_gain(gbB, b_gbB, 0, 3)
            load_gain(gbB, b_gbB, 1, 4)
            if do_samples:
                xsB, b_x_s = xt_r.next()
                x_s = xsB[0:NS, :]
                mk.dma("sp", lambda h: h.dma_start(out=x_s, in_=xs_scr), reads=[b_xs_scr], writes=[b_x_s])
                ffn_gateup([(x_s, b_x_s)], NS, 0)
                ffn_down(0, x_s, b_x_s, NS)
                h3s, b_h3s = h3_r.next()
                rmsnorm_tile(x_s, NS, gbB[0:NS, 1, :], [b_x_s, b_gbB], st_r.next(), out_f32=h3s[0:NS, :], b_f32=b_h3s)
                mk.dma("sp", lambda h: h.dma_start(out=y_s, in_=h3s[0:NS, :]), reads=[b_h3s], is_out=True)
            tiles = load_block(0, xs3, bxs3)
            hbs = ffn_prep_norm([(xt[:], b_xt) for xt, b_xt in tiles], 128, 0)
            for ti, (hb, b_hb) in enumerate(hbs):
                ffn_prep_T(ti, hb, b_hb, 128)
            for fb in range(NFB):
                t0 = fb * FB
                nxt = None
                if fb + 1 < NFB:
                    nxt = load_block(fb + 1, xs3, bxs3)
                    nhbs = ffn_prep_norm([(xt[:], b_xt) for xt, b_xt in nxt], 128, 0)
                ffn_mm(FB)
                for ti, (xt, b_xt) in enumerate(tiles):
                    ffn_down(ti, xt[:], b_xt, 128)
                    if nxt is not None:
                        ffn_prep_T(ti, nhbs[ti][0], nhbs[ti][1], 128)
                    h3, b_h3 = h3_r.next()
                    st = st_r.next()
                    rmsnorm_tile(xt[:], 128, gbB[:, 1, :], [b_xt, b_gbB], st, out_f32=h3[:], b_f32=b_h3)
                    r0 = t0 + ti * 128
                    mk.dma("sp", lambda h, r0=r0, h3=h3: h.dma_start(out=y_p[r0:r0 + 128, :], in_=h3[:]), reads=[b_h3], is_out=True)
                tiles = nxt


def _consts():
    ident = np.eye(128, dtype=np.float32)
    eall = np.zeros((64, 32, 128), np.float32)
    for c in range(32):
        eall[2 * c, c, 0:64] = 1.0
        eall[2 * c + 1, c, 64:128] = 1.0
    band = np.zeros((20, 128, 128), np.float32)
    for gi, w in enumerate((2, 4, 8, 16)):
        for t in range(128):
            for d in range(w):
                s = t - d
                if s >= 0:
                    band[gi, s, t] += 1.0 / w
                    cnt = min(w, t + 1)
                    band[8 + gi, s, t] += 1.0 / cnt
                else:
                    band[4 + gi, 128 + s, t] += 1.0 / w
            band[gi, t, t] -= 1.0
            band[8 + gi, t, t] -= 1.0
    for gi, w in enumerate((2, 4, 8, 16)):
        for s_ in range(NS):
            band[12 + gi, s_, s_] = 1.0 / w - 1.0
            for k in range(16 - w, 15):
                band[16 + gi, s_ * 15 + k, s_] = 1.0 / w
    return ident, eall, band


def _cmask():
    p = np.arange(128)[:, None]
    q = np.arange(128)[None, :]
    m = np.zeros((128, 2, 128), np.float32)
    m[:, 0, :] = np.where(p <= q, 0.0, NEG)
    m[:, 1, :] = np.where(p > q, 0.0, NEG)
    return m


def _consts_samp():
    c = np.zeros((128, 1024), np.float32)
    for s_ in range(NS):
        c[s_, s_ * 128:(s_ + 1) * 128] = 1.0
    for g in range(2):
        for s_ in range(NS):
            for slot in range(15):
                p = g * 64 + s_ * 15 + slot
                c[s_, 512 + p] = 1.0
                c[p, 640 + g * 4 + s_] = 1.0
    for r in range(32):
        c[r, 648 + r // 4] = 1.0
        c[r, 828 + r // 4] = 1.0
        c[r, 656 + (r % 8) * 4 + r // 8] = 1.0
    for p in range(120):
        c[p, 688 + p // 30] = 1.0
    c[0:8, 692:820] = np.arange(128, dtype=np.float32)[None, :]
    c[:, 820:828] = np.arange(8, dtype=np.float32)[None, :]
    return c


def _perm_w_in(w_in):
    q = w_in[:, :512].reshape(D, 2, 4, 64).transpose(0, 2, 1, 3).reshape(D, 512)
    return np.ascontiguousarray(np.concatenate([q, w_in[:, 512:]], axis=1))


_NC_CACHE = {}


def _make_in_maps(inp, cores=range(8)):
    f = lambda a: np.ascontiguousarray(np.asarray(a, dtype=np.float32))
    ident, eall, band = _consts()
    norms = f(np.stack([inp["norm_mix"][0], inp["norm_ffn"][0], inp["norm_mix"][1], inp["norm_ffn"][1], inp["norm_final"]]))
    shared = {
        "w_in": _perm_w_in(f(inp["w_in_a"][0])), "w_out": f(inp["w_out_a"][0]),
        "w1k": f(inp["cmp_w1_k"][0]), "w2k": f(inp["cmp_w2_k"][0]), "posk": f(inp["cmp_pos_k"][0]),
        "w1v": f(inp["cmp_w1_v"][0]), "w2v": f(inp["cmp_w2_v"][0]), "posv": f(inp["cmp_pos_v"][0]),
        "conv_w": f(inp["conv_w"][0]), "conv_b": f(inp["conv_b"][0]), "ln_g": f(inp["conv_ln_g"][0]), "ln_b": f(inp["conv_ln_b"][0]),
        "pool_w": f(inp["pool_w"][0]), "pool_scale": f(inp["pool_scale"][0]).reshape(1, D), "norms": norms,
        "wg": f(inp["w_ffn_gate"]), "wu": f(inp["w_ffn_up"]), "wd": f(inp["w_ffn_down"]),
        "c_ident": ident, "c_eall": eall, "c_band": band, "c_samp": _consts_samp(), "c_cmask": _cmask(),
        "cache_cmp": f(inp["cache_cmp_kv"][0]).reshape(5120 * 8, 16 * 256),
        "cache_slc": f(inp["cache_slc_kv"][0]).reshape(5120 * 2, 64 * 256),
    }
    xp = f(inp["x_prompt"])
    xs = f(inp["x_sample"]).reshape(32, D)
    pt = np.ascontiguousarray(np.asarray(inp["page_table"], dtype=np.int32))
    swin = f(inp["state_win_kv"][0]).reshape(32, 512, 256)
    sconv = f(inp["state_conv"][0])
    spool = f(inp["state_pool"][0])
    maps = []
    for c in cores:
        m = dict(shared)
        m["xp"] = xp[c]
        sl = slice(c * NS, (c + 1) * NS)
        m["xs"] = np.ascontiguousarray(xs[sl]); m["pt"] = np.ascontiguousarray(pt[sl])
        m["state_win"] = np.ascontiguousarray(swin[sl]); m["state_conv"] = np.ascontiguousarray(sconv[sl]); m["state_pool"] = np.ascontiguousarray(spool[sl])
        maps.append(m)
    return maps


def kernel(x_prompt, x_sample, cache_cmp_kv, cache_slc_kv, state_win_kv, state_conv, state_pool, page_table,
           norm_mix, norm_ffn, norm_final, w_in_a, w_out_a, cmp_pos_k, cmp_w1_k, cmp_w2_k, cmp_pos_v, cmp_w1_v,
           cmp_w2_v, conv_w, conv_b, conv_ln_g, conv_ln_b, pool_w, pool_scale, w_ffn_gate, w_ffn_up, w_ffn_down):
    inp = dict(x_prompt=x_prompt, x_sample=x_sample, cache_cmp_kv=cache_cmp_kv, cache_slc_kv=cache_slc_kv, state_win_kv=state_win_kv,
               state_conv=state_conv, state_pool=state_pool, page_table=page_table, norm_mix=norm_mix, norm_ffn=norm_ffn, norm_final=norm_final,
               w_in_a=w_in_a, w_out_a=w_out_a, cmp_pos_k=cmp_pos_k, cmp_w1_k=cmp_w1_k, cmp_w2_k=cmp_w2_k, cmp_pos_v=cmp_pos_v, cmp_w1_v=cmp_w1_v,
               cmp_w2_v=cmp_w2_v, conv_w=conv_w, conv_b=conv_b, conv_ln_g=conv_ln_g, conv_ln_b=conv_ln_b, pool_w=pool_w, pool_scale=pool_scale,
               w_ffn_gate=w_ffn_gate, w_ffn_up=w_ffn_up, w_ffn_down=w_ffn_down)
    if "nc" not in _NC_CACHE:
        _NC_CACHE["nc"] = build_nc(do_samples=True)
    nc = _NC_CACHE["nc"]
    in_maps = _make_in_maps(inp)
    res = run_bass_kernel_spmd(nc, in_maps, core_ids=list(range(8)))
    R = res.results
    B = 8
    DB = 32
    cat = lambda nm: np.concatenate([R[c][nm] for c in range(B)], axis=0)
    y_prompt = np.stack([R[c]["y_p"] for c in range(B)])
    cmp_p = np.stack([R[c]["o_cmp_p"] for c in range(B)]).reshape(1, B, T, 2, 2, 64)
    slc_p = np.stack([R[c]["o_slc_p"] for c in range(B)]).reshape(1, B, T, 2, 2, 64)
    win_p = np.stack([R[c]["o_win_p"] for c in range(B)]).reshape(1, B, 512, 2, 2, 64)
    conv_p = np.stack([R[c]["o_conv_p"] for c in range(B)]).reshape(1, B, 30, 512)
    pool_p = np.stack([R[c]["o_pool_p"] for c in range(B)]).reshape(1, B, 15, D)
    y_sample = cat("y_s").reshape(DB, 1, D)
    cmp_s = cat("o_cmp_s").reshape(1, DB, 1, 2, 2, 64)
    slc_s = cat("o_slc_s").reshape(1, DB, 1, 2, 2, 64)
    win_s = cat("o_win_s").reshape(1, DB, 512, 2, 2, 64)
    conv_s = cat("o_conv_s").reshape(1, DB, 30, 512)
    pool_s = cat("o_pool_s").reshape(1, DB, 15, D)
    return (y_prompt, y_sample, cmp_p, cmp_s, slc_p, slc_s, win_p, win_s, conv_p, conv_s, pool_p, pool_s)
```
